# Optimizing a Trainium2 kernel written in Bass

```python
import math
import jax
import jax.numpy as jnp
from jax import lax
import numpy as np

D_MODEL = 2048
BATCH = 4
SEQ = 2048
DEPTH = 4

MIX_WIDTH = D_MODEL
ATTN_HEAD_DIM = 128
ATTN_WIDTH = MIX_WIDTH // 2
ATTN_HEADS = ATTN_WIDTH // ATTN_HEAD_DIM
MOBA_BLOCK = 256
MOBA_TOPK = 3
MOBA_QCHUNK = 32
REL_BUCKETS = 32
REL_MAX_DIST = 128
SSM_WIDTH = MIX_WIDTH - ATTN_WIDTH
SSM_HEAD_DIM = 64
SSM_HEADS = SSM_WIDTH // SSM_HEAD_DIM
SSM_GROUPS = 2
SSM_HEADS_PER_GROUP = SSM_HEADS // SSM_GROUPS
SSM_STATE = 128
SSM_CONV = 4
SSM_CHUNK = 128
SSM_CONV_CH = SSM_WIDTH + 2 * SSM_GROUPS * SSM_STATE
IN_COLS = 3 * ATTN_WIDTH + SSM_WIDTH + SSM_CONV_CH + SSM_HEADS
FFN_HIDDEN = 256 * (-(-8 * D_MODEL // (3 * 256)))
FFN_CONV = 3
NORM_EPS = 1e-6
NEG = -1e30

kernel_name = 'hybrid_moba_ssd_convffn'


def rms_norm(x, w):
    xf = x.astype(jnp.float32)
    y = xf * lax.rsqrt(jnp.mean(xf * xf, axis=-1, keepdims=True) + NORM_EPS)
    return (y * w.astype(jnp.float32)).astype(x.dtype)


def causal_depthwise_conv(x, w, b):
    k = w.shape[0]
    y = lax.conv_general_dilated(x, w[:, None, :].astype(x.dtype), window_strides=(1,), padding=[(k - 1, 0)],
                                 dimension_numbers=('NWC', 'WIO', 'NWC'), feature_group_count=x.shape[-1])
    return y + b.astype(x.dtype)


def t5_bucket(rel):
    n = jnp.maximum(rel, 0)
    max_exact = REL_BUCKETS // 2
    nf = jnp.maximum(n, 1).astype(jnp.float32)
    large = max_exact + (jnp.log(nf / max_exact) / math.log(REL_MAX_DIST / max_exact)
                         * (REL_BUCKETS - max_exact)).astype(jnp.int32)
    large = jnp.minimum(large, REL_BUCKETS - 1)
    return jnp.where(n < max_exact, n, large)


def moba_attention(q, k, v, rel_bias):
    bsz, s, h, dh = q.shape
    nb = -(-s // MOBA_BLOCK)
    pad = nb * MOBA_BLOCK - s
    n_sel = min(MOBA_TOPK, nb - 1)
    scale = dh ** -0.5
    qh = q.transpose(0, 2, 1, 3)
    kh = jnp.pad(k.transpose(0, 2, 1, 3), ((0, 0), (0, 0), (0, pad), (0, 0)))
    vh = jnp.pad(v.transpose(0, 2, 1, 3), ((0, 0), (0, 0), (0, pad), (0, 0)))
    kb = kh.reshape(bsz, h, nb, MOBA_BLOCK, dh)
    vb = vh.reshape(bsz, h, nb, MOBA_BLOCK, dh)
    bias_h = rel_bias.T
    nqc = s // MOBA_QCHUNK
    q_chunks = qh.reshape(bsz, h, nqc, MOBA_QCHUNK, dh).transpose(2, 0, 1, 3, 4)
    t_blk = jnp.arange(MOBA_BLOCK)
    b_idx = jnp.arange(bsz)[:, None, None, None]
    h_idx = jnp.arange(h)[None, :, None, None]
    kmean = kb.mean(axis=3) if n_sel > 0 else None

    def one_chunk(args):
        c, qc = args
        q0 = c * MOBA_QCHUNK
        qpos = q0 + jnp.arange(MOBA_QCHUNK)
        own = q0 // MOBA_BLOCK
        k_own = lax.dynamic_slice_in_dim(kh, own * MOBA_BLOCK, MOBA_BLOCK, axis=2)
        v_own = lax.dynamic_slice_in_dim(vh, own * MOBA_BLOCK, MOBA_BLOCK, axis=2)
        rel_own = qpos[:, None] - (own * MOBA_BLOCK + t_blk)[None, :]
        lg_own = (jnp.einsum('bhqd,bhtd->bhqt', qc, k_own) * scale).astype(jnp.float32) \
            + bias_h[:, t5_bucket(rel_own)].astype(jnp.float32)
        lg_own = jnp.where(rel_own >= 0, lg_own, NEG)
        if n_sel == 0:
            p_own = jax.nn.softmax(lg_own, axis=-1).astype(v.dtype)
            return jnp.einsum('bhqt,bhtd->bhqd', p_own, v_own)
        gate = jnp.einsum('bhqd,bhnd->bhqn', qc, kmean).astype(jnp.float32)
        gate = jnp.where(jnp.arange(nb) < own, gate, NEG)
        _, idx = lax.top_k(gate, n_sel)
        valid = idx < own
        k_sel = kb[b_idx, h_idx, idx]
        v_sel = vb[b_idx, h_idx, idx]
        rel_sel = qpos[None, None, :, None, None] - (idx[..., None] * MOBA_BLOCK + t_blk)
        lg_sel = (jnp.einsum('bhqd,bhqjtd->bhqjt', qc, k_sel) * scale).astype(jnp.float32) \
            + bias_h[h_idx[..., None], t5_bucket(rel_sel)].astype(jnp.float32)
        lg_sel = jnp.where(valid[..., None], lg_sel, NEG)
        n_g = n_sel * MOBA_BLOCK
        lg = jnp.concatenate([lg_sel.reshape(bsz, h, MOBA_QCHUNK, n_g), lg_own], axis=-1)
        p = jax.nn.softmax(lg, axis=-1).astype(v.dtype)
        p_sel = p[..., :n_g].reshape(bsz, h, MOBA_QCHUNK, n_sel, MOBA_BLOCK)
        return jnp.einsum('bhqjt,bhqjtd->bhqd', p_sel, v_sel) + jnp.einsum('bhqt,bhtd->bhqd', p[..., n_g:], v_own)

    out = lax.map(one_chunk, (jnp.arange(nqc), q_chunks))
    return out.transpose(1, 0, 3, 2, 4).reshape(bsz, s, h * dh)


def ssd_chunked(x, dt, a, bm, cm):
    bsz, s, g, hg, p = x.shape
    n = bm.shape[-1]
    L = SSM_CHUNK
    nc = s // L
    xc = (x * dt[..., None]).reshape(bsz, nc, L, g, hg, p)
    a_cs = jnp.cumsum((dt * a).reshape(bsz, nc, L, g, hg), axis=2)
    bc = bm.reshape(bsz, nc, L, g, n)
    cc = cm.reshape(bsz, nc, L, g, n)
    seg = a_cs[:, :, :, None] - a_cs[:, :, None, :]
    causal = jnp.tril(jnp.ones((L, L), dtype=bool))[:, :, None, None]
    decay = jnp.exp(jnp.where(causal, seg, -jnp.inf))
    cb = jnp.einsum('bclgn,bcsgn->bclsg', cc, bc)
    y_diag = jnp.einsum('bclsg,bclsgh,bcsghp->bclghp', cb, decay, xc)
    decay_states = jnp.exp(a_cs[:, :, -1:] - a_cs)
    states = jnp.einsum('bclgn,bclgh,bclghp->bcghpn', bc, decay_states, xc)
    chunk_decay = jnp.exp(a_cs[:, :, -1])

    def step(hstate, inp):
        st, dec = inp
        return hstate * dec[..., None, None] + st, hstate

    _, prev = lax.scan(step, jnp.zeros_like(states[:, 0]),
                       (jnp.moveaxis(states, 1, 0), jnp.moveaxis(chunk_decay, 1, 0)))
    prev = jnp.moveaxis(prev, 0, 1)
    y_off = jnp.einsum('bclgn,bcghpn,bclgh->bclghp', cc, prev, jnp.exp(a_cs))
    return (y_diag + y_off).reshape(bsz, s, g, hg, p)


def mamba2_mixer(z, xbc, dt_raw, conv_w, conv_b, dt_bias, a_log, d_skip, norm_w):
    bsz, s, _ = z.shape
    f32 = jnp.float32
    xbc = jax.nn.silu(causal_depthwise_conv(xbc, conv_w, conv_b))
    xs, bm, cm = jnp.split(xbc, [SSM_WIDTH, SSM_WIDTH + SSM_GROUPS * SSM_STATE], axis=-1)
    xs = xs.reshape(bsz, s, SSM_GROUPS, SSM_HEADS_PER_GROUP, SSM_HEAD_DIM).astype(f32)
    bm = bm.reshape(bsz, s, SSM_GROUPS, SSM_STATE).astype(f32)
    cm = cm.reshape(bsz, s, SSM_GROUPS, SSM_STATE).astype(f32)
    dt = jax.nn.softplus(dt_raw.astype(f32) + dt_bias.astype(f32)).reshape(bsz, s, SSM_GROUPS, SSM_HEADS_PER_GROUP)
    a = -jnp.exp(a_log.astype(f32)).reshape(SSM_GROUPS, SSM_HEADS_PER_GROUP)
    y = ssd_chunked(xs, dt, a, bm, cm) + d_skip.astype(f32).reshape(SSM_GROUPS, SSM_HEADS_PER_GROUP, 1) * xs
    gw = SSM_WIDTH // SSM_GROUPS
    y = y.reshape(bsz, s, SSM_GROUPS, gw) * jax.nn.silu(z.astype(f32)).reshape(bsz, s, SSM_GROUPS, gw)
    y = y * lax.rsqrt(jnp.mean(y * y, axis=-1, keepdims=True) + NORM_EPS)
    return (y.reshape(bsz, s, SSM_WIDTH) * norm_w.astype(f32)).astype(z.dtype)


def setup_inputs(seed: int = 0) -> dict:
    key = jax.random.key(seed)
    ks = jax.random.split(key, 18)
    f32 = jnp.float32

    def normal(k, shape, scale):
        return jax.random.normal(k, shape, f32) * scale

    def gain(k, width):
        return 1.0 + normal(k, (DEPTH, width), 0.02)

    x = normal(ks[0], (BATCH, SEQ, D_MODEL), 1.0)
    rel_bias = normal(ks[1], (REL_BUCKETS, ATTN_HEADS), 0.5)
    ln_mix_pre = gain(ks[2], D_MODEL)
    w_in = normal(ks[3], (DEPTH, D_MODEL, IN_COLS), D_MODEL ** -0.5)
    ssm_conv_w = normal(ks[4], (DEPTH, SSM_CONV, SSM_CONV_CH), SSM_CONV ** -0.5)
    ssm_conv_b = normal(ks[5], (DEPTH, SSM_CONV_CH), 0.02)
    dt0 = jnp.exp(jax.random.uniform(ks[6], (DEPTH, SSM_HEADS), f32, math.log(1e-3), math.log(1e-1)))
    dt_bias = dt0 + jnp.log(-jnp.expm1(-dt0))
    a_log = jnp.log(jax.random.uniform(ks[7], (DEPTH, SSM_HEADS), f32, 1.0, 16.0))
    d_skip = 1.0 + normal(ks[8], (DEPTH, SSM_HEADS), 0.1)
    ssm_norm_w = gain(ks[9], SSM_WIDTH)
    w_out = normal(ks[10], (DEPTH, MIX_WIDTH, D_MODEL), MIX_WIDTH ** -0.5)
    ln_mix_post = gain(ks[11], D_MODEL)
    ln_ffn_pre = gain(ks[12], D_MODEL)
    w_ffn_up = normal(ks[13], (DEPTH, D_MODEL, 2 * FFN_HIDDEN), D_MODEL ** -0.5)
    ffn_conv_w = normal(ks[14], (DEPTH, FFN_CONV, 2 * FFN_HIDDEN), FFN_CONV ** -0.5)
    ffn_conv_b = normal(ks[15], (DEPTH, 2 * FFN_HIDDEN), 0.02)
    w_ffn_down = normal(ks[16], (DEPTH, FFN_HIDDEN, D_MODEL), FFN_HIDDEN ** -0.5)
    ln_ffn_post = gain(ks[17], D_MODEL)
    return {'x': x, 'rel_bias': rel_bias, 'ln_mix_pre': ln_mix_pre, 'w_in': w_in,
            'ssm_conv_w': ssm_conv_w, 'ssm_conv_b': ssm_conv_b, 'dt_bias': dt_bias, 'a_log': a_log,
            'd_skip': d_skip, 'ssm_norm_w': ssm_norm_w, 'w_out': w_out, 'ln_mix_post': ln_mix_post,
            'ln_ffn_pre': ln_ffn_pre, 'w_ffn_up': w_ffn_up, 'ffn_conv_w': ffn_conv_w,
            'ffn_conv_b': ffn_conv_b, 'w_ffn_down': w_ffn_down, 'ln_ffn_post': ln_ffn_post}


def reference(x, rel_bias, ln_mix_pre, w_in, ssm_conv_w, ssm_conv_b, dt_bias, a_log, d_skip,
              ssm_norm_w, w_out, ln_mix_post, ln_ffn_pre, w_ffn_up, ffn_conv_w, ffn_conv_b,
              w_ffn_down, ln_ffn_post):
    bsz, s, _ = x.shape
    aw, sw = ATTN_WIDTH, SSM_WIDTH
    splits = [aw, 2 * aw, 3 * aw, 3 * aw + sw, 3 * aw + sw + SSM_CONV_CH]
    for l in range(DEPTH):
        h = rms_norm(x, ln_mix_pre[l])
        proj = h @ w_in[l]
        q, k, v, z, xbc, dt_raw = jnp.split(proj, splits, axis=-1)
        shp = (bsz, s, ATTN_HEADS, ATTN_HEAD_DIM)
        attn = moba_attention(q.reshape(shp), k.reshape(shp), v.reshape(shp), rel_bias)
        ssm = mamba2_mixer(z, xbc, dt_raw, ssm_conv_w[l], ssm_conv_b[l], dt_bias[l], a_log[l],
                           d_skip[l], ssm_norm_w[l])
        mixed = jnp.concatenate([attn, ssm], axis=-1) @ w_out[l]
        x = x + rms_norm(mixed, ln_mix_post[l])
        h = rms_norm(x, ln_ffn_pre[l])
        u = causal_depthwise_conv(h @ w_ffn_up[l], ffn_conv_w[l], ffn_conv_b[l])
        gate, up = jnp.split(u, [FFN_HIDDEN], axis=-1)
        f = (jax.nn.gelu(gate, approximate=True) * up) @ w_ffn_down[l]
        x = x + rms_norm(f, ln_ffn_post[l])
    return x
```

```python
import numpy as np
from contextlib import ExitStack
import concourse.bass as bass
import concourse.mybir as mybir
from concourse.bass_utils import run_bass_kernel_spmd

F32 = mybir.dt.float32
BF = mybir.dt.bfloat16
AF = mybir.ActivationFunctionType
OP = mybir.AluOpType
AX = mybir.AxisListType

P = 128
T = 2048
D = 2048
L = 4
NFT = 16
TT = 512
NTT = 4
HID = 5632
NHT = 44
IN_COLS = 5648
IN_TILES = 45
EPS = 1e-6
NEG = -1e30


class Buf:
    __slots__ = ("name", "writers", "readers", "sem", "semcnt", "excl")

    def __init__(self, name="", excl=False):
        self.name = name
        self.excl = excl
        self.writers = []
        self.readers = []
        self.sem = None
        self.semcnt = 0


class Trk:
    ENG = ("pe", "act", "dve", "pool", "sp")

    def __init__(self, nc):
        self.nc = nc
        self.ops = {e: [] for e in self.ENG}
        self.cnt = {e: 0 for e in self.ENG}
        self.known = {e: {} for e in self.ENG}
        self.nsem = 0
        self.semcount = {}
        self.free_sems = []
        self.owners = []

    def _need(self, eng, tok, waits):
        key, val, teng = tok
        if teng == eng and eng == "pe":
            return
        if self.known[eng].get(key, 0) >= val:
            return
        if waits.get(key, 0) < val:
            waits[key] = val

    def _flush(self, eng, waits):
        for k, v in waits.items():
            self.known[eng][k] = v
            self.ops[eng].append(("wait", k, v))

    def _deps(self, eng, r, w):
        waits = {}
        for b in r:
            for t in b.writers:
                self._need(eng, t, waits)
            if b.excl:
                for t in b.readers:
                    if t[2] != eng:
                        self._need(eng, t, waits)
        for b in w:
            for t in b.writers:
                self._need(eng, t, waits)
            for t in b.readers:
                self._need(eng, t, waits)
        self._flush(eng, waits)

    def _commit(self, tok, r, w):
        for b in r:
            b.readers.append(tok)
            if len(b.readers) > 64:
                last = {}
                for t in b.readers:
                    if last.get(t[0], (0, 0, 0))[1] < t[1]:
                        last[t[0]] = t
                b.readers = list(last.values())
        for b in w:
            b.writers = [tok]
            b.readers = []

    def op(self, eng, fn, r=(), w=()):
        self._deps(eng, r, w)
        self.cnt[eng] += 1
        tok = (eng, self.cnt[eng], eng)
        self.ops[eng].append(("op", fn))
        self._commit(tok, r, w)

    def dma(self, eng, out, in_, r=(), w=(), owner=None):
        self._deps(eng, r, w)
        owner = owner or (w[0] if w else r[0])
        if owner.sem is None:
            if self.free_sems:
                owner.sem = self.free_sems.pop()
                owner.semcnt = self.semcount.get(owner.sem, 0)
            else:
                owner.sem = ("d", self.nsem)
                self.nsem += 1
            self.owners.append(owner)
        owner.semcnt += 16
        self.semcount[owner.sem] = owner.semcnt
        tok = (owner.sem, owner.semcnt, "dma")
        self.ops[eng].append(("dma", out, in_, owner.sem))
        self._commit(tok, r, w)

    def barrier(self):
        for e in self.ENG:
            waits = {}
            for f in self.ENG:
                if self.cnt[f] > 0 and not (e == f == "pe"):
                    self._need(e, (f, self.cnt[f], f), waits)
            for k, v in self.semcount.items():
                self._need(e, (k, v, "dma"), waits)
            self._flush(e, waits)
        keep = []
        for o in self.owners:
            if o.name:
                keep.append(o)
            else:
                self.free_sems.append(o.sem)
                o.sem = None
        self.owners = keep

    def emit(self):
        nc = self.nc
        with ExitStack() as es:
            sems = {}
            for e in self.ENG:
                sems[e] = es.enter_context(nc.semaphore("s_" + e))
            for i in range(self.nsem):
                sems[("d", i)] = es.enter_context(nc.semaphore("sd%d" % i))
            block = es.enter_context(nc.Block())
            engobj = {"pe": "tensor", "act": "scalar", "dve": "vector", "pool": "gpsimd", "sp": "sync"}

            def make(ename):
                def body(eng):
                    for o in self.ops[ename]:
                        if o[0] == "wait":
                            eng.wait_ge(sems[o[1]], o[2])
                        elif o[0] == "op":
                            o[1](eng).then_inc(sems[ename], 1)
                        else:
                            eng.dma_start(out=o[1], in_=o[2]).then_inc(sems[o[3]], 16)
                return body
            for ename in self.ENG:
                if self.ops[ename]:
                    getattr(block, engobj[ename])(make(ename))


def _ptab_layout():
    off = {}
    c = 0
    for name, n in (("g_mix_pre", L * 16), ("g_mix_post", L * 16), ("g_ffn_pre", L * 16), ("g_ffn_post", L * 16),
                    ("sconv_w", L * 12 * 4), ("sconv_b", L * 12), ("dexp", L * 8), ("snw", L * 8),
                    ("fconv_w", L * 88 * 3), ("fconv_b", L * 88), ("dtb", L * 16), ("alog", L * 16)):
        off[name] = c
        c += n
    return off, c


PT_OFF, PT_COLS = _ptab_layout()


class Arena:
    def __init__(self, ap, words):
        self.ap = ap
        self.words = words
        self.off = 0

    def f32(self, *shape):
        n = int(np.prod(shape))
        a = self.ap[:, self.off:self.off + n]
        self.off += n
        assert self.off <= self.words, ("arena overflow", self.off, self.words)
        if len(shape) == 2:
            return a.rearrange("p (a b) -> p a b", a=shape[0])
        if len(shape) == 3:
            return a.rearrange("p (a b c) -> p a b c", a=shape[0], b=shape[1])
        return a

    def bf(self, *shape):
        n = int(np.prod(shape))
        w = (n + 1) // 2
        a = self.ap[:, self.off:self.off + w].bitcast(BF)
        self.off += w
        assert self.off <= self.words, ("arena overflow", self.off, self.words)
        if len(shape) == 2:
            return a.rearrange("p (a b) -> p a b", a=shape[0])
        if len(shape) == 3:
            return a.rearrange("p (a b c) -> p a b c", a=shape[0], b=shape[1])
        return a


ARENA_WORDS = 52000


import os
ATT_STOP = int(os.environ.get('ATT_STOP', '9'))


def build_program(n_layers=L, stop_after=None, taps=False):
    nc = bass.Bass("TRN2", target_bir_lowering=False)
    dt_in = lambda name, shape, dt=F32: nc.dram_tensor(name, shape, dt, kind="ExternalInput").ap()
    xT = dt_in("xT", [D, T])
    LW = n_layers
    w_in = dt_in("w_in", [LW, IN_TILES, P, 16 * 128])
    w_out = dt_in("w_out", [LW, 16, P, 16 * 128])
    w_up = dt_in("w_up", [LW, 88, P, 16 * 128])
    w_dn = dt_in("w_dn", [LW, 16, P, NHT * 128])
    ptab_d = dt_in("ptab", [P, PT_COLS])
    relT = dt_in("relT", [8, P, 1024])
    cfar = dt_in("cfar", [8, T])
    c_ident = dt_in("c_ident", [P, P])
    c_tri = dt_in("c_tri", [P, P])
    c_caus = dt_in("c_caus", [P, P])
    c_oh = dt_in("c_oh", [16, 16 * 128])
    c_lmat = dt_in("c_lmat", [9, 16 * 128])
    yT = nc.dram_tensor("yT", [D, T], F32, kind="ExternalOutput").ap()
    tk = "ExternalOutput" if taps else "Internal"
    XA = nc.dram_tensor("XA", [D, T], F32, kind=tk).ap()
    XB = nc.dram_tensor("XB", [D, T], F32, kind="Internal").ap()
    MIX = nc.dram_tensor("MIX", [D, T], BF, kind=tk).ap()
    G = nc.dram_tensor("G", [HID, T], BF, kind=tk).ap()
    tapd = {}

    es = ExitStack()
    arena_t = es.enter_context(nc.sbuf_tensor("arena", [P, ARENA_WORDS], F32))
    A = Arena(arena_t, ARENA_WORDS)
    psum = [es.enter_context(nc.psum_tensor("ps%d" % i, [P, 512], F32)) for i in range(7)]
    psbf = es.enter_context(nc.psum_tensor("psbf", [P, 1024], BF))
    t = Trk(nc)
    PSB = [Buf("ps%d" % i, excl=True) for i in range(8)]

    hT = A.bf(16, T)
    hTb = [Buf("hT%d" % i) for i in range(NTT)]
    ptab = A.f32(PT_COLS)
    Bpt = Buf("ptab")
    ident = A.bf(P)
    ones_bf = A.bf(P)
    tri = A.f32(P)
    caus = A.f32(P)
    ones_f = A.f32(P)
    oh = A.f32(16, P)
    lmat = A.bf(16, P)
    Bc = Buf("consts")
    NWS = 4
    wslot = [A.bf(16, P) for _ in range(NWS)]
    Bws = [Buf("ws%d" % i) for i in range(NWS)]
    wctr = [0]
    PHASE0 = A.off

    def pcol(name, idx):
        o = PT_OFF[name] + idx
        return ptab[:, o:o + 1]

    t.dma("sp", ptab, ptab_d, w=[Bpt])
    t.dma("pool", ident, c_ident, w=[Bc])
    t.dma("sp", tri, c_tri, w=[Bc], owner=Bpt)
    t.dma("sp", caus, c_caus, w=[Bc], owner=Bpt)
    t.dma("sp", oh[0:16, :, :], c_oh.rearrange("r (a b) -> r a b", a=16), w=[Bc], owner=Bpt)
    t.dma("pool", lmat[0:9, :, :], c_lmat.rearrange("r (a b) -> r a b", a=16), w=[Bc])
    t.op("dve", lambda e: e.memset(ones_bf, 1.0), w=[Bc])
    t.op("dve", lambda e: e.memset(ones_f, 1.0), w=[Bc])
    t.barrier()

    def load_w(dram_tile):
        i = wctr[0] % NWS
        wctr[0] += 1
        t.dma("pool", wslot[i], dram_tile.rearrange("p (k n) -> p k n", k=16), w=[Bws[i]])
        return wslot[i], Bws[i]

    def xview(X):
        return X.rearrange("(f p) t -> p f t", p=P)

    def stage_n1(X, gname, l):
        A.off = PHASE0
        xt = A.f32(16, TT)
        Bxt = [Buf() for _ in range(4)]
        sq = [A.bf(TT) for _ in range(2)]
        Bsq = [Buf(), Buf()]
        rstd = A.f32(TT)
        Brs = Buf()
        for tt in range(NTT):
            ts = slice(tt * TT, (tt + 1) * TT)
            for q in range(4):
                t.dma("sp", xt[:, 4 * q:4 * q + 4, :], xview(X)[:, 4 * q:4 * q + 4, ts], w=[Bxt[q]])
            for ft in range(16):
                b = ft % 2
                t.op("act", lambda e, ft=ft, b=b: e.activation(out=sq[b], in_=xt[:, ft, :], func=AF.Square),
                     r=[Bxt[ft // 4]], w=[Bsq[b]])
                t.op("pe", lambda e, ft=ft, b=b: e.matmul(psum[6][:, :], lhsT=ones_bf, rhs=sq[b], start=(ft == 0), stop=(ft == 15)),
                     r=[Bsq[b], Bc], w=[PSB[6]])
            t.op("act", lambda e: e.activation(out=rstd, in_=psum[6][:, :], func=AF.Sqrt, bias=EPS, scale=1.0 / D),
                 r=[PSB[6]], w=[Brs])
            t.op("dve", lambda e: e.reciprocal(out=rstd, in_=rstd), r=[Brs], w=[Brs])
            for ft in range(16):
                t.op("dve", lambda e, ft=ft, ts=ts: e.scalar_tensor_tensor(
                    out=hT[:, ft, ts], in0=xt[:, ft, :], scalar=pcol(gname, l * 16 + ft), in1=rstd,
                    op0=OP.mult, op1=OP.mult), r=[Bxt[ft // 4], Brs, Bpt], w=[hTb[tt]])
        t.barrier()

    def proj_tile(wt, Bw, evac, kts=16):
        for tt in range(NTT):
            b = tt % 2
            ts = slice(tt * TT, (tt + 1) * TT)
            for kt in range(kts):
                t.op("pe", lambda e, kt=kt, b=b, ts=ts: e.matmul(psum[b][:, :], lhsT=wt[:, kt, :], rhs=hT[:, kt, ts],
                                                             start=(kt == 0), stop=(kt == kts - 1)),
                     r=[Bw, hTb[tt]], w=[PSB[b]])
            evac(tt, ts, psum[b], PSB[b])

    def stage_attn(l, h):
        A.off = PHASE0
        qT = A.bf(T); qTf = A.f32(T); kT = A.bf(T); vT = A.bf(T)
        Bq, Bqf, Bk, Bv = Buf(), Buf(), Buf(), Buf()
        Vtok = A.bf(16, P); BV = Buf()
        PTs = [A.bf(TT) for _ in range(2)]; BPT = [Buf(), Buf()]
        R = A.bf(T); BR = Buf()
        Th = A.bf(1024); BTh = Buf()
        rec = A.f32(TT); Brec = Buf()
        ob = [A.bf(TT) for _ in range(2)]; Bob = [Buf(), Buf()]
        ksum = A.f32(8); kmf = A.f32(8); Bks = Buf()
        gbuf = A.f32(8); mx = A.f32(8); Bg = Buf()
        nm = A.bf(16, 8); Bnm = Buf()
        sel = A.f32(8); Bsel = Buf()

        if not os.environ.get("NO_TH"):
            t.dma("pool", Th, relT[h], w=[BTh])
        if not os.environ.get("NO_R8"):
            t.dma("pool", R[8:9, :], cfar[h:h + 1, :], w=[BR])
        wq, Bwq = load_w(w_in[l, h])
        wk, Bwk = load_w(w_in[l, 8 + h])
        wv, Bwv = load_w(w_in[l, 16 + h])
        sc = float(128 ** -0.5)

        def ev_q(tt, ts, ps, pb):
            t.op("act", lambda e: e.activation(out=qT[:, ts], in_=ps[:, :], func=AF.Copy, scale=sc), r=[pb], w=[Bq])
            if not os.environ.get("NO_QF"):
                t.op("dve", lambda e: e.tensor_scalar(out=qTf[:, ts], in0=ps[:, :], scalar1=sc, scalar2=0.0, op0=OP.mult, op1=OP.add),
                     r=[pb], w=[Bqf])

        def ev_k(tt, ts, ps, pb):
            t.op("act", lambda e: e.activation(out=kT[:, ts], in_=ps[:, :], func=AF.Copy), r=[pb], w=[Bk])
            if not os.environ.get("NO_KS"):
              t.op("dve", lambda e: e.reduce_sum(out=ksum[:, 2 * tt:2 * tt + 2], in_=ps[:, :].rearrange("p (a b) -> p a b", a=2),
                                              axis=AX.X), r=[pb], w=[Bks])

        def ev_v(tt, ts, ps, pb):
            t.op("act", lambda e: e.activation(out=vT[:, ts], in_=ps[:, :], func=AF.Copy), r=[pb], w=[Bv])

        proj_tile(wq, Bwq, ev_q)
        proj_tile(wk, Bwk, ev_k)
        proj_tile(wv, Bwv, ev_v)
        if ATT_STOP <= 1:
            t.barrier(); return
        for i in range(16):
            sl = slice((i % 8) * P, (i % 8 + 1) * P)
            t.op("pe", lambda e, i=i, sl=sl: e.transpose(out=psbf[:, sl], in_=vT[:, i * P:(i + 1) * P], identity=ident),
                 r=[Bv, Bc], w=[PSB[7]])
            t.op("act", lambda e, i=i, sl=sl: e.activation(out=Vtok[:, i, :], in_=psbf[:, sl], func=AF.Copy), r=[PSB[7]], w=[BV])
        if ATT_STOP <= 2:
            t.barrier(); return
        t.op("dve", lambda e: e.tensor_scalar(out=kmf, in0=ksum, scalar1=1.0 / 256, scalar2=0.0, op0=OP.mult, op1=OP.add), r=[Bks], w=[Bks])
        t.op("dve", lambda e: e.memset(nm, 0.0), w=[Bnm])
        for jq in range(8, 16):
            own = jq // 2
            t.op("pe", lambda e, jq=jq: e.matmul(psum[6][:, 0:8], lhsT=qTf[:, jq * P:(jq + 1) * P], rhs=kmf, start=True, stop=True),
                 r=[Bqf, Bks], w=[PSB[6]])
            t.op("dve", lambda e: e.memset(gbuf, NEG), w=[Bg])
            t.op("dve", lambda e, own=own: e.tensor_copy(out=gbuf[:, 0:own], in_=psum[6][:, 0:own]), r=[PSB[6]], w=[Bg])
            t.op("dve", lambda e: e.max(out=mx, in_=gbuf), r=[Bg], w=[Bg])
            t.op("dve", lambda e, own=own: e.tensor_scalar(out=sel[:, 0:own], in0=gbuf[:, 0:own], scalar1=mx[:, 2:3], scalar2=1.0,
                                                          op0=OP.is_ge, op1=OP.subtract), r=[Bg], w=[Bsel])
            t.op("dve", lambda e, own=own, jq=jq: e.tensor_scalar(out=nm[:, jq, 0:own], in0=sel[:, 0:own], scalar1=-NEG, scalar2=None,
                                                                 op0=OP.mult), r=[Bsel], w=[Bnm])
        if ATT_STOP <= 3:
            t.barrier(); return
        for tt in range(NTT):
            for k in range(4):
                jq = tt * 4 + k
                t.op("pe", lambda e, jq=jq, k=k: e.matmul(psum[6][0:8, k * P:(k + 1) * P], lhsT=nm[:, jq, :], rhs=ident,
                                                         start=True, stop=True), r=[Bnm, Bc], w=[PSB[6]])
            t.op("act", lambda e, tt=tt: e.activation(out=R[0:8, tt * TT:(tt + 1) * TT], in_=psum[6][0:8, :], func=AF.Copy),
                 r=[PSB[6]], w=[BR])
        if ATT_STOP <= 4:
            t.barrier(); return
        for j in range(NTT):
            qs = slice(j * TT, (j + 1) * TT)
            ntile = 4 * j + 4

            def scores(i, j=j, qs=qs):
                b = 2 + (i % 2)
                n = i // 2
                d = 512 * j - 128 * i
                near = d <= 128
                lm = lmat[0:9, 2 * n + (0 if near else 1), :]
                t.op("pe", lambda e: e.matmul(psum[b][:, :], lhsT=kT[:, i * P:(i + 1) * P], rhs=qT[:, qs], start=True, stop=False),
                     r=[Bk, Bq], w=[PSB[b]])
                t.op("pe", lambda e: e.matmul(psum[b][:, :], lhsT=lm, rhs=R[0:9, qs], start=False, stop=(not near)),
                     r=[BR, Bc], w=[PSB[b]])
                if near:
                    c0 = d + 384
                    t.op("pe", lambda e: e.matmul(psum[b][:, :], lhsT=ident, rhs=Th[:, c0:c0 + TT], start=False, stop=True),
                         r=[BTh, Bc], w=[PSB[b]])
                t.op("act", lambda e: e.activation(out=PTs[i % 2], in_=psum[b][:, :], func=AF.Exp), r=[PSB[b]], w=[BPT[i % 2]])

            def pv(i):
                t.op("pe", lambda e: e.matmul(psum[4][:, :], lhsT=Vtok[:, i, :], rhs=PTs[i % 2], start=(i == 0), stop=(i == ntile - 1)),
                     r=[BV, BPT[i % 2]], w=[PSB[4]])
                t.op("pe", lambda e: e.matmul(psum[5][:, :], lhsT=ones_bf, rhs=PTs[i % 2], start=(i == 0), stop=(i == ntile - 1)),
                     r=[Bc, BPT[i % 2]], w=[PSB[5]])

            for i in range(ntile + 1):
                if i < ntile:
                    scores(i)
                if i > 0:
                    pv(i - 1)
            t.op("dve", lambda e: e.reciprocal(out=rec, in_=psum[5][:, :]), r=[PSB[5]], w=[Brec])
            t.op("dve", lambda e, j=j: e.tensor_tensor(out=ob[j % 2], in0=psum[4][:, :], in1=rec, op=OP.mult),
                 r=[PSB[4], Brec], w=[Bob[j % 2]])
            t.dma("sp", MIX[h * P:(h + 1) * P, qs], ob[j % 2], r=[Bob[j % 2]], owner=Bob[j % 2])
        t.barrier()

    def stage_ssm(l):
        A.off = PHASE0
        t1 = A.f32(256); dtT = A.f32(256); dAT = A.f32(256); acsT = A.f32(256); nacsT = A.f32(256)
        lastB = A.f32(256); cdB = A.f32(256); ds = A.f32(256); ea = A.f32(16)
        Bt1, Bdt, BdA, Bacs, Bnacs, Blast, Bcd, Bds, Bea = [Buf() for _ in range(9)]
        acsF = A.f32(T); BacsF = Buf()
        ubuf = A.f32(T + 4); Bu = Buf()
        acc = A.f32(T); Bacc = Buf()
        xTg = A.bf(4, T); BxT = Buf()
        BTg = A.bf(T); CTg = A.bf(T); BBT, BCT = Buf(), Buf()
        zsg = A.bf(4, T); Bzs = Buf()
        HT = A.f32(512); HTb = A.bf(512); BH, BHb = Buf(), Buf()
        xc = [A.bf(512) for _ in range(2)]; Bxc = [Buf(), Buf()]
        xcd = [A.bf(512) for _ in range(2)]; Bxcd = [Buf(), Buf()]
        Btok = [A.bf(P) for _ in range(2)]; BBtok = [Buf(), Buf()]
        CBm = [A.f32(P) for _ in range(2)]; BCBm = [Buf(), Buf()]
        tE = [A.f32(P) for _ in range(2)]; BtE = [Buf(), Buf()]
        E = [A.f32(P) for _ in range(2)]; BE = [Buf(), Buf()]
        W = [A.bf(P) for _ in range(2)]; BW = [Buf(), Buf()]
        eab = [A.bf(P) for _ in range(2)]; Beab = [Buf(), Buf()]
        Cs = [A.bf(P) for _ in range(2)]; BCs = [Buf(), Buf()]
        yv = A.f32(4, P); Byv = Buf()
        yz = A.f32(4, P); Byz = Buf()
        sq = [A.bf(P) for _ in range(2)]; Bsq = [Buf(), Buf()]
        rstd = A.f32(P); Brs = Buf()
        ob4 = [A.bf(4, TT) for _ in range(2)]; Bob4 = [Buf(), Buf()]

        wdt, Bwdt = load_w(w_in[l, 44])
        for c in range(16):
            for kt in range(16):
                t.op("pe", lambda e, c=c, kt=kt: e.matmul(psum[6][:, c * 16:(c + 1) * 16], lhsT=hT[:, kt, c * P:(c + 1) * P],
                                                         rhs=wdt[:, kt, 0:16], start=(kt == 0), stop=(kt == 15)),
                     r=[Bwdt, hTb[c // 4]], w=[PSB[6]])
            t.op("dve", lambda e, c=c: e.tensor_tensor(out=t1[:, c * 16:(c + 1) * 16], in0=psum[6][:, c * 16:(c + 1) * 16],
                                                      in1=ptab[:, PT_OFF["dtb"] + l * 16:PT_OFF["dtb"] + l * 16 + 16], op=OP.add),
                 r=[PSB[6], Bpt], w=[Bt1])
        t.op("act", lambda e: e.activation(out=t1, in_=t1, func=AF.Exp), r=[Bt1], w=[Bt1])
        t.op("act", lambda e: e.activation(out=dtT, in_=t1, func=AF.Ln, bias=1.0), r=[Bt1], w=[Bdt])
        t.op("act", lambda e: e.activation(out=ea, in_=ptab[:, PT_OFF["alog"] + l * 16:PT_OFF["alog"] + l * 16 + 16], func=AF.Exp),
             r=[Bpt], w=[Bea])
        for c in range(16):
            t.op("dve", lambda e, c=c: e.scalar_tensor_tensor(out=dAT[:, c * 16:(c + 1) * 16], in0=dtT[:, c * 16:(c + 1) * 16],
                                                             scalar=-1.0, in1=ea, op0=OP.mult, op1=OP.mult),
                 r=[Bdt, Bea], w=[BdA])
        for c in range(16):
            t.op("pe", lambda e, c=c: e.matmul(psum[2][:, c * 16:(c + 1) * 16], lhsT=tri, rhs=dAT[:, c * 16:(c + 1) * 16],
                                              start=True, stop=True), r=[BdA, Bc], w=[PSB[2]])
            t.op("pe", lambda e, c=c: e.matmul(psum[3][:, c * 16:(c + 1) * 16], lhsT=ones_f, rhs=dAT[:, c * 16:(c + 1) * 16],
                                              start=True, stop=True), r=[BdA, Bc], w=[PSB[3]])
        t.op("dve", lambda e: e.tensor_copy(out=acsT, in_=psum[2][:, 0:256]), r=[PSB[2]], w=[Bacs])
        t.op("dve", lambda e: e.tensor_scalar(out=nacsT, in0=psum[2][:, 0:256], scalar1=-1.0, scalar2=0.0, op0=OP.mult, op1=OP.add),
             r=[PSB[2]], w=[Bnacs])
        t.op("dve", lambda e: e.tensor_copy(out=lastB, in_=psum[3][:, 0:256]), r=[PSB[3]], w=[Blast])
        t.op("act", lambda e: e.activation(out=cdB, in_=psum[3][:, 0:256], func=AF.Exp), r=[PSB[3]], w=[Bcd])
        t.op("dve", lambda e: e.tensor_tensor(out=ds, in0=lastB, in1=acsT, op=OP.subtract), r=[Blast, Bacs], w=[Bds])
        t.op("act", lambda e: e.activation(out=ds, in_=ds, func=AF.Exp), r=[Bds], w=[Bds])
        for tt in range(NTT):
            for k in range(4):
                c = tt * 4 + k
                t.op("pe", lambda e, c=c, k=k: e.matmul(psum[4][0:16, k * P:(k + 1) * P], lhsT=dAT[:, c * 16:(c + 1) * 16], rhs=tri,
                                                       start=True, stop=True), r=[BdA, Bc], w=[PSB[4]])
            t.op("dve", lambda e, tt=tt: e.tensor_copy(out=acsF[0:16, tt * TT:(tt + 1) * TT], in_=psum[4][0:16, :]),
                 r=[PSB[4]], w=[BacsF])

        for g in range(2):
            t.op("dve", lambda e: e.memset(ubuf[:, 0:3], 0.0), w=[Bu])

            def conv_tile(ctile, out_ap, Bout):
                wt, Bw = load_w(w_in[l, 32 + ctile])

                def ev(tt, ts, ps, pb):
                    t.op("act", lambda e: e.activation(out=ubuf[:, 3 + tt * TT:3 + (tt + 1) * TT], in_=ps[:, :], func=AF.Copy),
                         r=[pb], w=[Bu])
                proj_tile(wt, Bw, ev)
                cw = PT_OFF["sconv_w"] + (l * 12 + ctile) * 4
                t.op("dve", lambda e: e.tensor_scalar(out=acc, in0=ubuf[:, 3:3 + T], scalar1=ptab[:, cw + 3:cw + 4], scalar2=None,
                                                      op0=OP.mult), r=[Bu, Bpt], w=[Bacc])
                for k in range(3):
                    t.op("dve", lambda e, k=k: e.scalar_tensor_tensor(out=acc, in0=ubuf[:, k:k + T], scalar=ptab[:, cw + k:cw + k + 1],
                                                                     in1=acc, op0=OP.mult, op1=OP.add), r=[Bu, Bpt, Bacc], w=[Bacc])
                t.op("act", lambda e: e.activation(out=out_ap, in_=acc, func=AF.Silu, bias=pcol("sconv_b", l * 12 + ctile)),
                     r=[Bacc, Bpt], w=[Bout])

            for ct in range(4):
                conv_tile(4 * g + ct, xTg[:, ct, :], BxT)
            conv_tile(8 + g, BTg, BBT)
            conv_tile(10 + g, CTg, BCT)
            for ct in range(4):
                wt, Bw = load_w(w_in[l, 24 + 4 * g + ct])

                def evz(tt, ts, ps, pb, ct=ct):
                    t.op("act", lambda e: e.activation(out=zsg[:, ct, ts], in_=ps[:, :], func=AF.Silu), r=[pb], w=[Bzs])
                proj_tile(wt, Bw, evz)
            t.op("dve", lambda e: e.memset(HT, 0.0), w=[BH])
            t.op("dve", lambda e: e.memset(HTb, 0.0), w=[BHb])
            for c in range(16):
                cs = slice(c * P, (c + 1) * P)
                cb = c % 2
                hs = slice(c * 16 + 8 * g, c * 16 + 8 * g + 8)
                for ct in range(4):
                    t.op("pe", lambda e, ct=ct, cs=cs: e.transpose(out=psbf[:, ct * P:(ct + 1) * P], in_=xTg[:, ct, cs], identity=ident),
                         r=[BxT, Bc], w=[PSB[7]])
                t.op("dve", lambda e, cb=cb, hs=hs: e.tensor_tensor(
                    out=xc[cb].rearrange("p (h d) -> p h d", h=8), in0=psbf[:, 0:512].rearrange("p (h d) -> p h d", h=8),
                    in1=dtT[:, hs].unsqueeze(2).to_broadcast([P, 8, 64]), op=OP.mult), r=[PSB[7], Bdt], w=[Bxc[cb]])
                t.op("dve", lambda e, cb=cb, hs=hs: e.tensor_tensor(
                    out=xcd[cb].rearrange("p (h d) -> p h d", h=8), in0=xc[cb].rearrange("p (h d) -> p h d", h=8),
                    in1=ds[:, hs].unsqueeze(2).to_broadcast([P, 8, 64]), op=OP.mult), r=[Bxc[cb], Bds], w=[Bxcd[cb]])
                t.op("pe", lambda e, cs=cs: e.transpose(out=psbf[:, 512:640], in_=BTg[:, cs], identity=ident), r=[BBT, Bc], w=[PSB[7]])
                t.op("act", lambda e, cb=cb: e.activation(out=Btok[cb], in_=psbf[:, 512:640], func=AF.Copy), r=[PSB[7]], w=[BBtok[cb]])
                t.op("pe", lambda e, cs=cs: e.matmul(psum[6][:, 0:P], lhsT=BTg[:, cs], rhs=CTg[:, cs], start=True, stop=True),
                     r=[BBT, BCT], w=[PSB[6]])
                t.op("dve", lambda e, cb=cb: e.tensor_tensor(out=CBm[cb], in0=psum[6][:, 0:P], in1=caus, op=OP.mult),
                     r=[PSB[6], Bc], w=[BCBm[cb]])
                for k in range(8):
                    hh = 8 * g + k
                    ct = k // 2
                    kb = k % 2
                    pb = 5
                    psl = slice((k % 4) * P, (k % 4 + 1) * P)
                    t.op("pe", lambda e, hh=hh, psl=psl, cs=cs: e.matmul(psum[5][:, psl], lhsT=oh[0:16, hh, :], rhs=acsF[0:16, cs],
                                                                        start=True, stop=True), r=[BacsF, Bc], w=[PSB[5]])
                    t.op("act", lambda e, kb=kb, psl=psl: e.activation(out=eab[kb], in_=psum[5][:, psl], func=AF.Exp),
                         r=[PSB[5]], w=[Beab[kb]])
                    t.op("dve", lambda e, kb=kb, psl=psl, hh=hh, c=c: e.tensor_scalar(
                        out=tE[kb], in0=psum[5][:, psl], scalar1=nacsT[:, c * 16 + hh:c * 16 + hh + 1], scalar2=0.0,
                        op0=OP.add, op1=OP.min), r=[PSB[5], Bnacs], w=[BtE[kb]])
                    t.op("act", lambda e, kb=kb: e.activation(out=E[kb], in_=tE[kb], func=AF.Exp), r=[BtE[kb]], w=[BE[kb]])
                    t.op("dve", lambda e, kb=kb, cb=cb: e.tensor_tensor(out=W[kb], in0=E[kb], in1=CBm[cb], op=OP.mult),
                         r=[BE[kb], BCBm[cb]], w=[BW[kb]])
                    t.op("dve", lambda e, kb=kb, cs=cs: e.tensor_tensor(out=Cs[kb], in0=CTg[:, cs], in1=eab[kb], op=OP.mult),
                         r=[BCT, Beab[kb]], w=[BCs[kb]])
                    yb = 2 + k // 4
                    t.op("pe", lambda e, yb=yb, psl=psl, ct=ct, cb=cb, kb=kb: e.matmul(
                        psum[yb][:, psl], lhsT=xc[cb][:, ct * P:(ct + 1) * P], rhs=W[kb], start=True, stop=False),
                        r=[Bxc[cb], BW[kb]], w=[PSB[yb]])
                    t.op("pe", lambda e, yb=yb, psl=psl, ct=ct, kb=kb: e.matmul(
                        psum[yb][:, psl], lhsT=HTb[:, ct * P:(ct + 1) * P], rhs=Cs[kb], start=False, stop=True),
                        r=[BHb, BCs[kb]], w=[PSB[yb]])
                for ct in range(4):
                    for half in range(2):
                        k = 2 * ct + half
                        yb = 2 + k // 4
                        psl = slice((k % 4) * P, (k % 4 + 1) * P)
                        pr = slice(half * 64, half * 64 + 64)
                        dcol = PT_OFF["dexp"] + l * 8 + 4 * g + ct
                        t.op("dve", lambda e, ct=ct, yb=yb, psl=psl, pr=pr, dcol=dcol, cs=cs: e.scalar_tensor_tensor(
                            out=yv[pr, ct, :], in0=xTg[pr, ct, cs], scalar=ptab[pr, dcol:dcol + 1], in1=psum[yb][pr, psl],
                            op0=OP.mult, op1=OP.add), r=[BxT, Bpt, PSB[yb]], w=[Byv])
                    t.op("dve", lambda e, ct=ct, cs=cs: e.tensor_tensor(out=yz[:, ct, :], in0=yv[:, ct, :], in1=zsg[:, ct, cs], op=OP.mult),
                         r=[Byv, Bzs], w=[Byz])
                    t.op("act", lambda e, ct=ct: e.activation(out=sq[ct % 2], in_=yz[:, ct, :], func=AF.Square), r=[Byz], w=[Bsq[ct % 2]])
                    t.op("pe", lambda e, ct=ct: e.matmul(psum[6][:, P:2 * P], lhsT=ones_bf, rhs=sq[ct % 2], start=(ct == 0), stop=(ct == 3)),
                         r=[Bsq[ct % 2], Bc], w=[PSB[6]])
                t.op("act", lambda e: e.activation(out=rstd, in_=psum[6][:, P:2 * P], func=AF.Sqrt, bias=EPS, scale=1.0 / 512),
                     r=[PSB[6]], w=[Brs])
                t.op("dve", lambda e: e.reciprocal(out=rstd, in_=rstd), r=[Brs], w=[Brs])
                o4 = (c // 4) % 2
                for ct in range(4):
                    ncol = PT_OFF["snw"] + l * 8 + 4 * g + ct
                    t.op("dve", lambda e, ct=ct, o4=o4, ncol=ncol, c=c: e.scalar_tensor_tensor(
                        out=ob4[o4][:, ct, (c % 4) * P:(c % 4 + 1) * P], in0=yz[:, ct, :], scalar=ptab[:, ncol:ncol + 1], in1=rstd,
                        op0=OP.mult, op1=OP.mult), r=[Byz, Bpt, Brs], w=[Bob4[o4]])
                if c % 4 == 3:
                    f0 = 1024 + 512 * g
                    t.dma("sp", MIX[f0:f0 + 512, (c // 4) * TT:(c // 4 + 1) * TT].rearrange("(a p) t -> p a t", p=P), ob4[o4],
                          r=[Bob4[o4]], owner=Bob4[o4])
                t.op("pe", lambda e, cb=cb: e.matmul(psum[4][:, :], lhsT=Btok[cb], rhs=xcd[cb], start=True, stop=True),
                     r=[BBtok[cb], Bxcd[cb]], w=[PSB[4]])
                t.op("dve", lambda e, hs=hs: e.tensor_tensor(out=HT.rearrange("p (h d) -> p h d", h=8), in0=HT.rearrange("p (h d) -> p h d", h=8),
                                                            in1=cdB[:, hs].unsqueeze(2).to_broadcast([P, 8, 64]), op=OP.mult),
                     r=[BH, Bcd], w=[BH])
                t.op("dve", lambda e: e.tensor_tensor(out=HT, in0=HT, in1=psum[4][:, :], op=OP.add), r=[BH, PSB[4]], w=[BH])
                t.op("act", lambda e: e.activation(out=HTb, in_=HT, func=AF.Copy), r=[BH], w=[BHb])
        t.barrier()

    def epilogue(m, Bm, psstat, Bpsstat, tq, Xin, Xout, gpost, l, gnext, lnext, bufs):
        xt, Bxt, sq, Bsq, rstd, Brs = bufs
        ts = slice(tq * TT, (tq + 1) * TT)
        t.op("act", lambda e: e.activation(out=rstd, in_=psstat, func=AF.Sqrt, bias=EPS, scale=1.0 / D), r=[Bpsstat], w=[Brs])
        t.op("dve", lambda e: e.reciprocal(out=rstd, in_=rstd), r=[Brs], w=[Brs])
        for q in range(4):
            t.dma("sp", xt, xview(Xin)[:, 4 * q:4 * q + 4, ts], w=[Bxt])
            for f in range(4):
                ft = 4 * q + f
                t.op("dve", lambda e, ft=ft: e.scalar_tensor_tensor(out=m[:, ft, :], in0=m[:, ft, :], scalar=pcol(gpost, l * 16 + ft),
                                                                   in1=rstd, op0=OP.mult, op1=OP.mult), r=[Bm, Bpt, Brs], w=[Bm])
                t.op("dve", lambda e, ft=ft, f=f: e.tensor_tensor(out=m[:, ft, :], in0=m[:, ft, :], in1=xt[:, f, :], op=OP.add),
                     r=[Bm, Bxt], w=[Bm])
        for q in range(4):
            t.dma("sp", xview(Xout)[:, 4 * q:4 * q + 4, ts], m[:, 4 * q:4 * q + 4, :], r=[Bm], owner=Bm)
        if gnext is not None:
            for ft in range(16):
                b = ft % 2
                t.op("act", lambda e, ft=ft, b=b: e.activation(out=sq[b], in_=m[:, ft, :], func=AF.Square), r=[Bm], w=[Bsq[b]])
                t.op("pe", lambda e, ft=ft, b=b: e.matmul(psum[6][:, :], lhsT=ones_bf, rhs=sq[b], start=(ft == 0), stop=(ft == 15)),
                     r=[Bsq[b], Bc], w=[PSB[6]])
            t.op("act", lambda e: e.activation(out=rstd, in_=psum[6][:, :], func=AF.Sqrt, bias=EPS, scale=1.0 / D), r=[PSB[6]], w=[Brs])
            t.op("dve", lambda e: e.reciprocal(out=rstd, in_=rstd), r=[Brs], w=[Brs])
            for ft in range(16):
                t.op("dve", lambda e, ft=ft: e.scalar_tensor_tensor(out=hT[:, ft, ts], in0=m[:, ft, :], scalar=pcol(gnext, lnext * 16 + ft),
                                                                   in1=rstd, op0=OP.mult, op1=OP.mult), r=[Bm, Bpt, Brs], w=[hTb[tq]])

    def stage_o(l, Xin, Xout):
        A.off = PHASE0
        mixt = A.bf(16, TT); Bmx = [Buf() for _ in range(4)]
        m = A.f32(16, TT); Bm = Buf()
        xt = A.f32(4, TT); Bxt = Buf()
        sq = [A.bf(TT) for _ in range(2)]; Bsq = [Buf(), Buf()]
        rstd = A.f32(TT); Brs = Buf()
        sq2 = [A.bf(TT) for _ in range(2)]; Bsq2 = [Buf(), Buf()]
        for tq in range(NTT):
            ts = slice(tq * TT, (tq + 1) * TT)
            for q in range(4):
                t.dma("sp", mixt[:, 4 * q:4 * q + 4, :], MIX.rearrange("(f p) t -> p f t", p=P)[:, 4 * q:4 * q + 4, ts], w=[Bmx[q]])
            for ft in range(16):
                wt, Bw = load_w(w_out[l, ft])
                b = ft % 2
                for kt in range(16):
                    t.op("pe", lambda e, kt=kt, b=b, wt=wt: e.matmul(psum[b][:, :], lhsT=wt[:, kt, :], rhs=mixt[:, kt, :],
                                                                    start=(kt == 0), stop=(kt == 15)), r=[Bw, Bmx[kt // 4]], w=[PSB[b]])
                t.op("act", lambda e, ft=ft, b=b: e.activation(out=m[:, ft, :], in_=psum[b][:, :], func=AF.Copy), r=[PSB[b]], w=[Bm])
                t.op("act", lambda e, ft=ft, b=b: e.activation(out=sq2[b], in_=psum[b][:, :], func=AF.Square), r=[PSB[b]], w=[Bsq2[b]])
                t.op("pe", lambda e, ft=ft, b=b: e.matmul(psum[5][:, :], lhsT=ones_bf, rhs=sq2[b], start=(ft == 0), stop=(ft == 15)),
                     r=[Bsq2[b], Bc], w=[PSB[5]])
            epilogue(m, Bm, psum[5][:, :], PSB[5], tq, Xin, Xout, "g_mix_post", l, "g_ffn_pre", l, (xt, Bxt, sq, Bsq, rstd, Brs))
        t.barrier()

    def stage_f1(l):
        A.off = PHASE0
        ug = A.f32(T + 2); uu = A.f32(T + 2); Bug, Buu = Buf(), Buf()
        ag = A.f32(T); au = A.f32(T); Bag, Bau = Buf(), Buf()
        gb = [A.bf(T) for _ in range(2)]; Bgb = [Buf(), Buf()]
        t.op("dve", lambda e: e.memset(ug[:, 0:2], 0.0), w=[Bug])
        t.op("dve", lambda e: e.memset(uu[:, 0:2], 0.0), w=[Buu])
        for j in range(NHT):
            for (tile, u, Bu_, a, Ba) in ((j, ug, Bug, ag, Bag), (NHT + j, uu, Buu, au, Bau)):
                wt, Bw = load_w(w_up[l, tile])

                def ev(tt, ts, ps, pb, u=u, Bu_=Bu_):
                    t.op("act", lambda e: e.activation(out=u[:, 2 + tt * TT:2 + (tt + 1) * TT], in_=ps[:, :], func=AF.Copy), r=[pb], w=[Bu_])
                proj_tile(wt, Bw, ev)
                cw = PT_OFF["fconv_w"] + (l * 88 + tile) * 3
                t.op("dve", lambda e, u=u, a=a, cw=cw, tile=tile: e.tensor_scalar(
                    out=a, in0=u[:, 2:2 + T], scalar1=ptab[:, cw + 2:cw + 3], scalar2=pcol("fconv_b", l * 88 + tile),
                    op0=OP.mult, op1=OP.add), r=[Bu_, Bpt], w=[Ba])
                for k in range(2):
                    t.op("dve", lambda e, u=u, a=a, cw=cw, k=k: e.scalar_tensor_tensor(
                        out=a, in0=u[:, k:k + T], scalar=ptab[:, cw + k:cw + k + 1], in1=a, op0=OP.mult, op1=OP.add),
                        r=[Bu_, Bpt, Ba], w=[Ba])
            t.op("act", lambda e: e.activation(out=ag, in_=ag, func=AF.Gelu_apprx_tanh), r=[Bag], w=[Bag])
            t.op("dve", lambda e, j=j: e.tensor_tensor(out=gb[j % 2], in0=ag, in1=au, op=OP.mult), r=[Bag, Bau], w=[Bgb[j % 2]])
            t.dma("sp", G[j * P:(j + 1) * P, :], gb[j % 2], r=[Bgb[j % 2]], owner=Bgb[j % 2])
        t.barrier()

    def stage_f2(l, Xin, Xout, gnext, lnext):
        A.off = PHASE0 - NWS * 1024
        gt = A.bf(NHT, TT); Bgt = [Buf() for _ in range(4)]
        m = A.f32(16, TT); Bm = Buf()
        xt = A.f32(4, TT); Bxt = Buf()
        sq = [A.bf(TT) for _ in range(2)]; Bsq = [Buf(), Buf()]
        rstd = A.f32(TT); Brs = Buf()
        sq2 = [A.bf(TT) for _ in range(2)]; Bsq2 = [Buf(), Buf()]
        wd = [A.bf(NHT, P) for _ in range(2)]; Bwd = [Buf(), Buf()]
        for tq in range(NTT):
            ts = slice(tq * TT, (tq + 1) * TT)
            for q in range(4):
                t.dma("sp", gt[:, 11 * q:11 * q + 11, :], G.rearrange("(f p) t -> p f t", p=P)[:, 11 * q:11 * q + 11, ts], w=[Bgt[q]])
            for ft in range(16):
                b = ft % 2
                t.dma("pool", wd[b], w_dn[l, ft].rearrange("p (k n) -> p k n", k=NHT), w=[Bwd[b]])
                for kt in range(NHT):
                    t.op("pe", lambda e, kt=kt, b=b: e.matmul(psum[b][:, :], lhsT=wd[b][:, kt, :], rhs=gt[:, kt, :],
                                                             start=(kt == 0), stop=(kt == NHT - 1)), r=[Bwd[b], Bgt[kt // 11]], w=[PSB[b]])
                t.op("act", lambda e, ft=ft, b=b: e.activation(out=m[:, ft, :], in_=psum[b][:, :], func=AF.Copy), r=[PSB[b]], w=[Bm])
                t.op("act", lambda e, ft=ft, b=b: e.activation(out=sq2[b], in_=psum[b][:, :], func=AF.Square), r=[PSB[b]], w=[Bsq2[b]])
                t.op("pe", lambda e, ft=ft, b=b: e.matmul(psum[5][:, :], lhsT=ones_bf, rhs=sq2[b], start=(ft == 0), stop=(ft == 15)),
                     r=[Bsq2[b], Bc], w=[PSB[5]])
            epilogue(m, Bm, psum[5][:, :], PSB[5], tq, Xin, Xout, "g_ffn_post", l, gnext, lnext, (xt, Bxt, sq, Bsq, rstd, Brs))
        t.barrier()

    Xcur = xT
    stage_n1(xT, "g_mix_pre", 0)
    done = False
    for l in range(n_layers):
        if stop_after == "n1":
            break
        for h in range(int(os.environ.get('NHEADS', '8'))):
            stage_attn(l, h)
        if stop_after == "attn":
            break
        stage_ssm(l)
        if stop_after == "ssm":
            break
        stage_o(l, Xcur, XA)
        if stop_after == "o":
            break
        stage_f1(l)
        if stop_after == "f1":
            break
        last = (l == n_layers - 1)
        Xn = yT if last else XB
        stage_f2(l, XA, Xn, None if last else "g_mix_pre", l + 1)
        Xcur = Xn
    t.barrier()
    t.emit()
    es.close()
    return nc, {"MIX": MIX, "XA": XA, "G": G}


def _t5_bucket_np(rel):
    n = np.maximum(rel, 0)
    max_exact = 16
    nf = np.maximum(n, 1).astype(np.float32)
    large = max_exact + (np.log(nf / max_exact) / np.log(128 / max_exact) * (32 - max_exact)).astype(np.int32)
    large = np.minimum(large, 31)
    return np.where(n < max_exact, n, large)


def _tile_w(w, kt, ntiles):
    K, N = w.shape
    if N < ntiles * 128:
        wp = np.zeros((K, ntiles * 128), np.float32)
        wp[:, :N] = w
        w = wp
    return np.ascontiguousarray(w.reshape(kt, 128, ntiles, 128).transpose(2, 1, 0, 3)).reshape(ntiles, 128, kt * 128)


def prep_shared(inp):
    f = np.float32
    sh = {}
    sh["w_in"] = np.stack([_tile_w(np.asarray(inp["w_in"][l], f), 16, IN_TILES) for l in range(L)])
    sh["w_out"] = np.stack([_tile_w(np.asarray(inp["w_out"][l], f), 16, 16) for l in range(L)])
    sh["w_up"] = np.stack([_tile_w(np.asarray(inp["w_ffn_up"][l], f), 16, 88) for l in range(L)])
    sh["w_dn"] = np.stack([_tile_w(np.asarray(inp["w_ffn_down"][l], f), NHT, 16) for l in range(L)])
    pt = np.zeros((P, PT_COLS), f)

    def put(name, arr):
        n = arr.shape[0]
        pt[:, PT_OFF[name]:PT_OFF[name] + n] = arr.T

    for name, key in (("g_mix_pre", "ln_mix_pre"), ("g_mix_post", "ln_mix_post"), ("g_ffn_pre", "ln_ffn_pre"), ("g_ffn_post", "ln_ffn_post")):
        put(name, np.asarray(inp[key], f).reshape(L * 16, P))
    scw = np.asarray(inp["ssm_conv_w"], f)
    put("sconv_w", scw.reshape(L, 4, 12, P).transpose(0, 2, 1, 3).reshape(L * 12 * 4, P))
    put("sconv_b", np.asarray(inp["ssm_conv_b"], f).reshape(L * 12, P))
    dsk = np.asarray(inp["d_skip"], f)
    put("dexp", np.repeat(dsk, 64, axis=1).reshape(L * 8, P))
    put("snw", np.asarray(inp["ssm_norm_w"], f).reshape(L * 8, P))
    fcw = np.asarray(inp["ffn_conv_w"], f)
    put("fconv_w", fcw.reshape(L, 3, 88, P).transpose(0, 2, 1, 3).reshape(L * 88 * 3, P))
    put("fconv_b", np.asarray(inp["ffn_conv_b"], f).reshape(L * 88, P))
    pt[:, PT_OFF["dtb"]:PT_OFF["dtb"] + L * 16] = np.asarray(inp["dt_bias"], f).reshape(1, L * 16)
    pt[:, PT_OFF["alog"]:PT_OFF["alog"] + L * 16] = np.asarray(inp["a_log"], f).reshape(1, L * 16)
    sh["ptab"] = pt
    rb = np.asarray(inp["rel_bias"], f)
    kp = np.arange(P)[:, None]
    cc = np.arange(1024)[None, :]
    rel = cc - 384 - kp
    idx = _t5_bucket_np(rel)
    tab = rb[idx, :]
    tab = np.where((rel >= 0)[:, :, None], tab, f(NEG))
    sh["relT"] = np.ascontiguousarray(tab.transpose(2, 0, 1)).astype(f)
    sh["cfar"] = np.ascontiguousarray(np.repeat(rb[31, :][:, None], T, axis=1)).astype(f)
    sh["c_ident"] = np.eye(P, dtype=f)
    ti = np.arange(P)
    sh["c_tri"] = (ti[:, None] <= ti[None, :]).astype(f)
    sh["c_caus"] = (ti[None, :] >= ti[:, None]).astype(f)
    ohm = np.zeros((16, 16, P), f)
    for hh in range(16):
        ohm[hh, hh, :] = 1.0
    sh["c_oh"] = ohm.reshape(16, 16 * P)
    lm = np.zeros((9, 16, P), f)
    for n in range(8):
        for far in range(2):
            lm[n, 2 * n + far, :] = 1.0
            lm[8, 2 * n + far, :] = float(far)
    sh["c_lmat"] = lm.reshape(9, 16 * P)
    return sh


_CACHE = {}


def kernel(**inputs):
    x = np.asarray(inputs["x"], np.float32)
    sh = prep_shared(inputs)
    if "nc" not in _CACHE:
        _CACHE["nc"] = build_program()[0]
    nc = _CACHE["nc"]
    in_maps = []
    for c in range(8):
        d = dict(sh)
        d["xT"] = np.ascontiguousarray(x[c % 4].T)
        in_maps.append(d)
    res = run_bass_kernel_spmd(nc, in_maps, core_ids=list(range(8)))
    out = np.stack([np.ascontiguousarray(res.results[b]["yT"].T) for b in range(4)], axis=0)
    return out.astype(np.float32)
```

```python
import os
import numpy as np
from contextlib import ExitStack
import concourse.bass as bass
import concourse.mybir as mybir
from concourse.bass_utils import run_bass_kernel_spmd

F32 = mybir.dt.float32
BF = mybir.dt.bfloat16
AF = mybir.ActivationFunctionType
OP = mybir.AluOpType
AX = mybir.AxisListType

P = 128
T = 2048
D = 2048
L = 4
NFT = 16
TT = 512
NTT = 4
HID = 5632
NHT = 44
IN_COLS = 5648
IN_TILES = 45
EPS = 1e-6
NEG = -1e30


class Buf:
    __slots__ = ("name", "writers", "readers", "sem", "semcnt", "excl")

    def __init__(self, name="", excl=False):
        self.name = name
        self.excl = excl
        self.writers = []
        self.readers = []
        self.sem = None
        self.semcnt = 0


class Trk:
    ENG = ("pe", "act", "dve", "pool", "sp")

    def __init__(self, nc):
        self.nc = nc
        self.ops = {e: [] for e in self.ENG}
        self.cnt = {e: 0 for e in self.ENG}
        self.known = {e: {} for e in self.ENG}
        self.nsem = 0
        self.semcount = {}
        self.free_sems = []
        self.owners = []

    def _need(self, eng, tok, waits):
        key, val, teng = tok
        if teng == eng and eng == "pe":
            return
        if self.known[eng].get(key, 0) >= val:
            return
        if waits.get(key, 0) < val:
            waits[key] = val

    def _flush(self, eng, waits):
        for k, v in waits.items():
            self.known[eng][k] = v
            self.ops[eng].append(("wait", k, v))

    def _deps(self, eng, r, w):
        waits = {}
        for b in r:
            for t in b.writers:
                self._need(eng, t, waits)
            if b.excl:
                for t in b.readers:
                    if t[2] != eng:
                        self._need(eng, t, waits)
        for b in w:
            for t in b.writers:
                self._need(eng, t, waits)
            for t in b.readers:
                self._need(eng, t, waits)
        self._flush(eng, waits)

    def _commit(self, tok, r, w):
        for b in r:
            b.readers.append(tok)
            if len(b.readers) > 64:
                last = {}
                for t in b.readers:
                    if last.get(t[0], (0, 0, 0))[1] < t[1]:
                        last[t[0]] = t
                b.readers = list(last.values())
        for b in w:
            b.writers = [tok]
            b.readers = []

    def op(self, eng, fn, r=(), w=()):
        self._deps(eng, r, w)
        self.cnt[eng] += 1
        tok = (eng, self.cnt[eng], eng)
        self.ops[eng].append(("op", fn))
        self._commit(tok, r, w)

    def dma(self, eng, out, in_, r=(), w=(), owner=None):
        self._deps(eng, r, w)
        owner = owner or (w[0] if w else r[0])
        if owner.sem is None:
            if self.free_sems:
                owner.sem = self.free_sems.pop()
                owner.semcnt = self.semcount.get(owner.sem, 0)
            else:
                owner.sem = ("d", self.nsem)
                self.nsem += 1
            self.owners.append(owner)
        owner.semcnt += 16
        self.semcount[owner.sem] = owner.semcnt
        tok = (owner.sem, owner.semcnt, "dma")
        self.ops[eng].append(("dma", out, in_, owner.sem))
        self._commit(tok, r, w)

    def barrier(self, pool=True):
        for e in self.ENG:
            if e == "pool" and not pool and os.environ.get("POOL_AHEAD"):
                continue
            waits = {}
            for f in self.ENG:
                if self.cnt[f] > 0 and not (e == f == "pe"):
                    self._need(e, (f, self.cnt[f], f), waits)
            for k, v in self.semcount.items():
                self._need(e, (k, v, "dma"), waits)
            self._flush(e, waits)
        keep = []
        for o in self.owners:
            if o.name:
                keep.append(o)
            else:
                self.free_sems.append(o.sem)
                o.sem = None
        self.owners = keep

    def emit(self):
        nc = self.nc
        with ExitStack() as es:
            sems = {}
            for e in self.ENG:
                sems[e] = es.enter_context(nc.semaphore("s_" + e))
            for i in range(self.nsem):
                sems[("d", i)] = es.enter_context(nc.semaphore("sd%d" % i))
            block = es.enter_context(nc.Block())
            engobj = {"pe": "tensor", "act": "scalar", "dve": "vector", "pool": "gpsimd", "sp": "sync"}

            def make(ename):
                def body(eng):
                    for o in self.ops[ename]:
                        if o[0] == "wait":
                            eng.wait_ge(sems[o[1]], o[2])
                        elif o[0] == "op":
                            o[1](eng).then_inc(sems[ename], 1)
                        else:
                            eng.dma_start(out=o[1], in_=o[2]).then_inc(sems[o[3]], 16)
                return body
            for ename in self.ENG:
                if self.ops[ename]:
                    getattr(block, engobj[ename])(make(ename))


def _ptab_layout():
    off = {}
    c = 0
    for name, n in (("g_mix_pre", L * 16), ("g_mix_post", L * 16), ("g_ffn_pre", L * 16), ("g_ffn_post", L * 16),
                    ("sconv_w", L * 12 * 4), ("sconv_b", L * 12), ("dexp", L * 8), ("snw", L * 8),
                    ("fconv_w", L * 88 * 3), ("fconv_b", L * 88), ("dtb", L * 16), ("alog", L * 16)):
        off[name] = c
        c += n
    return off, c


PT_OFF, PT_COLS = _ptab_layout()


class Arena:
    def __init__(self, ap, words):
        self.ap = ap
        self.words = words
        self.off = 0

    def f32(self, *shape):
        n = int(np.prod(shape))
        a = self.ap[:, self.off:self.off + n]
        self.off += n
        assert self.off <= self.words, ("arena overflow", self.off, self.words)
        if len(shape) == 2:
            return a.rearrange("p (a b) -> p a b", a=shape[0])
        if len(shape) == 3:
            return a.rearrange("p (a b c) -> p a b c", a=shape[0], b=shape[1])
        return a

    def bf(self, *shape):
        n = int(np.prod(shape))
        w = (n + 1) // 2
        a = self.ap[:, self.off:self.off + w].bitcast(BF)
        self.off += w
        assert self.off <= self.words, ("arena overflow", self.off, self.words)
        if len(shape) == 2:
            return a.rearrange("p (a b) -> p a b", a=shape[0])
        if len(shape) == 3:
            return a.rearrange("p (a b c) -> p a b c", a=shape[0], b=shape[1])
        return a


ARENA_WORDS = 53000


import os
ATT_STOP = int(os.environ.get('ATT_STOP', '9'))


def build_program(n_layers=L, stop_after=None, taps=False):
    nc = bass.Bass("TRN2", target_bir_lowering=False)
    dt_in = lambda name, shape, dt=F32: nc.dram_tensor(name, shape, dt, kind="ExternalInput").ap()
    xT = dt_in("xT", [D, T])
    LW = n_layers
    w_in = dt_in("w_in", [LW, IN_TILES, P, 16 * 128])
    w_out = dt_in("w_out", [LW, 16, P, 16 * 128])
    w_up = dt_in("w_up", [LW, 88, P, 16 * 128])
    w_dn = dt_in("w_dn", [LW, 16, P, NHT * 128])
    ptab_d = dt_in("ptab", [P, PT_COLS])
    relT = dt_in("relT", [8, P, 1024])
    cfar = dt_in("cfar", [8, T])
    c_ident = dt_in("c_ident", [P, P])
    c_tri = dt_in("c_tri", [P, P])
    c_caus = dt_in("c_caus", [P, P])
    c_oh = dt_in("c_oh", [16, 16 * 128])
    c_lmat = dt_in("c_lmat", [9, 16 * 128])
    yT = nc.dram_tensor("yT", [D, T], F32, kind="ExternalOutput").ap()
    tk = "ExternalOutput" if taps else "Internal"
    XA = nc.dram_tensor("XA", [D, T], F32, kind=tk).ap()
    XB = nc.dram_tensor("XB", [D, T], F32, kind="Internal").ap()
    MIX = nc.dram_tensor("MIX", [D, T], BF, kind=tk).ap()
    G = nc.dram_tensor("G", [HID, T], BF, kind=tk).ap()
    DBG = nc.dram_tensor("DBG", [P, 4096], F32, kind=tk).ap()
    tapd = {}

    es = ExitStack()
    arena_t = es.enter_context(nc.sbuf_tensor("arena", [P, ARENA_WORDS], F32))
    A = Arena(arena_t, ARENA_WORDS)
    psum = [es.enter_context(nc.psum_tensor("ps%d" % i, [P, 512], F32)) for i in range(7)]
    psbf = es.enter_context(nc.psum_tensor("psbf", [P, 1024], BF))
    t = Trk(nc)
    PSB = [Buf("ps%d" % i, excl=True) for i in range(8)]

    hT = A.bf(16, T)
    hTb = [Buf("hT%d" % i) for i in range(NTT)]
    ptab = A.f32(PT_COLS)
    Bpt = Buf("ptab")
    ident = A.bf(P)
    ones_bf = A.bf(P)
    tri = A.f32(P)
    caus = A.f32(P)
    ones_f = A.f32(P)
    Th_p = A.bf(1024)
    R_p = A.bf(T)
    BTh_p = Buf("Th")
    BR_p = Buf("R")
    lmat = A.bf(16, P)
    Bc = Buf("consts")
    NWS = 4
    wslot = [A.bf(16, P) for _ in range(NWS)]
    Bws = [Buf("ws%d" % i) for i in range(NWS)]
    wctr = [0]
    PHASE0 = A.off

    def pcol(name, idx):
        o = PT_OFF[name] + idx
        return ptab[:, o:o + 1]

    t.dma("sp", ptab, ptab_d, w=[Bpt])
    t.dma("pool", ident, c_ident, w=[Bc])
    t.dma("sp", tri, c_tri, w=[Bc], owner=Bpt)
    t.dma("sp", caus, c_caus, w=[Bc], owner=Bpt)
    t.dma("pool", lmat[0:9, :, :], c_lmat.rearrange("r (a b) -> r a b", a=16), w=[Bc])
    t.op("dve", lambda e: e.memset(ones_bf, 1.0), w=[Bc])
    t.op("dve", lambda e: e.memset(ones_f, 1.0), w=[Bc])
    t.barrier()

    def load_w(dram_tile):
        i = wctr[0] % NWS
        wctr[0] += 1
        t.dma("pool", wslot[i], dram_tile.rearrange("p (k n) -> p k n", k=16), w=[Bws[i]])
        return wslot[i], Bws[i]

    def xview(X):
        return X.rearrange("(f p) t -> p f t", p=P)

    def stage_n1(X, gname, l):
        A.off = PHASE0
        xt = A.f32(16, TT)
        Bxt = [Buf() for _ in range(4)]
        sq = [A.bf(TT) for _ in range(2)]
        Bsq = [Buf(), Buf()]
        rstd = A.f32(TT)
        Brs = Buf()
        for tt in range(NTT):
            ts = slice(tt * TT, (tt + 1) * TT)
            for q in range(4):
                t.dma("sp", xt[:, 4 * q:4 * q + 4, :], xview(X)[:, 4 * q:4 * q + 4, ts], w=[Bxt[q]])
            for ft in range(16):
                b = ft % 2
                t.op("act", lambda e, ft=ft, b=b: e.activation(out=sq[b], in_=xt[:, ft, :], func=AF.Square),
                     r=[Bxt[ft // 4]], w=[Bsq[b]])
                t.op("pe", lambda e, ft=ft, b=b: e.matmul(psum[6][:, :], lhsT=ones_bf, rhs=sq[b], start=(ft == 0), stop=(ft == 15)),
                     r=[Bsq[b], Bc], w=[PSB[6]])
            t.op("act", lambda e: e.activation(out=rstd, in_=psum[6][:, :], func=AF.Sqrt, bias=EPS, scale=1.0 / D),
                 r=[PSB[6]], w=[Brs])
            t.op("dve", lambda e: e.reciprocal(out=rstd, in_=rstd), r=[Brs], w=[Brs])
            for ft in range(16):
                t.op("dve", lambda e, ft=ft, ts=ts: e.scalar_tensor_tensor(
                    out=hT[:, ft, ts], in0=xt[:, ft, :], scalar=pcol(gname, l * 16 + ft), in1=rstd,
                    op0=OP.mult, op1=OP.mult), r=[Bxt[ft // 4], Brs, Bpt], w=[hTb[tt]])
        t.barrier(pool=False)

    def proj_tile(wt, Bw, evac, kts=16):
        for tt in range(NTT):
            b = tt % 2
            ts = slice(tt * TT, (tt + 1) * TT)
            for kt in range(kts):
                t.op("pe", lambda e, kt=kt, b=b, ts=ts: e.matmul(psum[b][:, :], lhsT=wt[:, kt, :], rhs=hT[:, kt, ts],
                                                             start=(kt == 0), stop=(kt == kts - 1)),
                     r=[Bw, hTb[tt]], w=[PSB[b]])
            evac(tt, ts, psum[b], PSB[b])

    def stage_attn(l, h):
        A.off = PHASE0
        qT = A.bf(T); qTf = A.f32(T); kT = A.bf(T); vT = A.bf(T)
        Bq, Bqf, Bk, Bv = Buf(), Buf(), Buf(), Buf()
        Vtok = A.bf(16, P); BV = Buf()
        PTs = [A.bf(TT) for _ in range(2)]; BPT = [Buf(), Buf()]
        R = R_p; BR = BR_p
        Th = Th_p; BTh = BTh_p
        rec = A.f32(TT); Brec = Buf()
        ob = [A.bf(TT) for _ in range(2)]; Bob = [Buf(), Buf()]
        ksum = A.f32(8); kmf = A.f32(8); Bks = Buf()
        gbuf = A.f32(8); mx = A.f32(8); Bg = Buf()
        nm = A.bf(16, 8); Bnm = Buf()
        sel = A.f32(8); Bsel = Buf()

        if not os.environ.get("NO_TH"):
            t.dma("pool", Th, relT[h], w=[BTh])
        if not os.environ.get("NO_R8"):
            t.dma("pool", R[8:9, :], cfar[h:h + 1, :], w=[BR])
        wq, Bwq = load_w(w_in[l, h])
        wk, Bwk = load_w(w_in[l, 8 + h])
        wv, Bwv = load_w(w_in[l, 16 + h])
        sc = float(128 ** -0.5)

        def ev_q(tt, ts, ps, pb):
            t.op("act", lambda e: e.activation(out=qT[:, ts], in_=ps[:, :], func=AF.Copy, scale=sc), r=[pb], w=[Bq])
            if not os.environ.get("NO_QF"):
                t.op("dve", lambda e: e.tensor_scalar(out=qTf[:, ts], in0=ps[:, :], scalar1=sc, scalar2=0.0, op0=OP.mult, op1=OP.add),
                     r=[pb], w=[Bqf])

        def ev_k(tt, ts, ps, pb):
            t.op("act", lambda e: e.activation(out=kT[:, ts], in_=ps[:, :], func=AF.Copy), r=[pb], w=[Bk])
            if not os.environ.get("NO_KS"):
              t.op("dve", lambda e: e.reduce_sum(out=ksum[:, 2 * tt:2 * tt + 2], in_=ps[:, :].rearrange("p (a b) -> p a b", a=2),
                                              axis=AX.X), r=[pb], w=[Bks])

        def ev_v(tt, ts, ps, pb):
            t.op("act", lambda e: e.activation(out=vT[:, ts], in_=ps[:, :], func=AF.Copy), r=[pb], w=[Bv])

        proj_tile(wq, Bwq, ev_q)
        proj_tile(wk, Bwk, ev_k)
        proj_tile(wv, Bwv, ev_v)
        if ATT_STOP <= 1:
            t.barrier(); return
        for i in range(16):
            sl = slice((i % 8) * P, (i % 8 + 1) * P)
            t.op("pe", lambda e, i=i, sl=sl: e.transpose(out=psbf[:, sl], in_=vT[:, i * P:(i + 1) * P], identity=ident),
                 r=[Bv, Bc], w=[PSB[7]])
            t.op("act", lambda e, i=i, sl=sl: e.activation(out=Vtok[:, i, :], in_=psbf[:, sl], func=AF.Copy), r=[PSB[7]], w=[BV])
        if ATT_STOP <= 2:
            t.barrier(); return
        t.op("dve", lambda e: e.tensor_scalar(out=kmf, in0=ksum, scalar1=1.0 / 256, scalar2=0.0, op0=OP.mult, op1=OP.add), r=[Bks], w=[Bks])
        t.op("dve", lambda e: e.memset(nm, 0.0), w=[Bnm])
        for jq in range(8, 16):
            own = jq // 2
            t.op("pe", lambda e, jq=jq: e.matmul(psum[6][:, 0:8], lhsT=qTf[:, jq * P:(jq + 1) * P], rhs=kmf, start=True, stop=True),
                 r=[Bqf, Bks], w=[PSB[6]])
            t.op("dve", lambda e: e.memset(gbuf, NEG), w=[Bg])
            t.op("dve", lambda e, own=own: e.tensor_copy(out=gbuf[:, 0:own], in_=psum[6][:, 0:own]), r=[PSB[6]], w=[Bg])
            t.op("dve", lambda e: e.max(out=mx, in_=gbuf), r=[Bg], w=[Bg])
            t.op("dve", lambda e, own=own: e.tensor_scalar(out=sel[:, 0:own], in0=gbuf[:, 0:own], scalar1=mx[:, 2:3], scalar2=1.0,
                                                          op0=OP.is_ge, op1=OP.subtract), r=[Bg], w=[Bsel])
            t.op("dve", lambda e, own=own, jq=jq: e.tensor_scalar(out=nm[:, jq, 0:own], in0=sel[:, 0:own], scalar1=-NEG, scalar2=None,
                                                                 op0=OP.mult), r=[Bsel], w=[Bnm])
        if ATT_STOP <= 3:
            t.barrier(); return
        for tt in range(NTT):
            for k in range(4):
                jq = tt * 4 + k
                t.op("pe", lambda e, jq=jq, k=k: e.matmul(psum[6][0:8, k * P:(k + 1) * P], lhsT=nm[:, jq, :], rhs=ident,
                                                         start=True, stop=True), r=[Bnm, Bc], w=[PSB[6]])
            t.op("act", lambda e, tt=tt: e.activation(out=R[0:8, tt * TT:(tt + 1) * TT], in_=psum[6][0:8, :], func=AF.Copy),
                 r=[PSB[6]], w=[BR])
        if ATT_STOP <= 4:
            t.barrier(); return
        for j in range(NTT):
            qs = slice(j * TT, (j + 1) * TT)
            ntile = 4 * j + 4

            def scores(i, j=j, qs=qs):
                b = 2 + (i % 2)
                n = i // 2
                d = 512 * j - 128 * i
                near = d <= 128
                lm = lmat[0:9, 2 * n + (0 if near else 1), :]
                t.op("pe", lambda e: e.matmul(psum[b][:, :], lhsT=kT[:, i * P:(i + 1) * P], rhs=qT[:, qs], start=True, stop=False),
                     r=[Bk, Bq], w=[PSB[b]])
                t.op("pe", lambda e: e.matmul(psum[b][:, :], lhsT=lm, rhs=R[0:9, qs], start=False, stop=(not near)),
                     r=[BR, Bc], w=[PSB[b]])
                if near:
                    c0 = d + 384
                    t.op("pe", lambda e: e.matmul(psum[b][:, :], lhsT=ident, rhs=Th[:, c0:c0 + TT], start=False, stop=True),
                         r=[BTh, Bc], w=[PSB[b]])
                t.op("act", lambda e: e.activation(out=PTs[i % 2], in_=psum[b][:, :], func=AF.Exp), r=[PSB[b]], w=[BPT[i % 2]])

            def pv(i):
                t.op("pe", lambda e: e.matmul(psum[4][:, :], lhsT=Vtok[:, i, :], rhs=PTs[i % 2], start=(i == 0), stop=(i == ntile - 1)),
                     r=[BV, BPT[i % 2]], w=[PSB[4]])
                t.op("pe", lambda e: e.matmul(psum[5][:, :], lhsT=ones_bf, rhs=PTs[i % 2], start=(i == 0), stop=(i == ntile - 1)),
                     r=[Bc, BPT[i % 2]], w=[PSB[5]])

            for i in range(ntile + 1):
                if i < ntile:
                    scores(i)
                if i > 0:
                    pv(i - 1)
            t.op("dve", lambda e: e.reciprocal(out=rec, in_=psum[5][:, :]), r=[PSB[5]], w=[Brec])
            t.op("dve", lambda e, j=j: e.tensor_tensor(out=ob[j % 2], in0=psum[4][:, :], in1=rec, op=OP.mult),
                 r=[PSB[4], Brec], w=[Bob[j % 2]])
            t.dma("sp", MIX[h * P:(h + 1) * P, qs], ob[j % 2], r=[Bob[j % 2]], owner=Bob[j % 2])
        t.barrier(pool=False)

    def stage_ssm(l):
        A.off = PHASE0
        t1 = A.f32(256); dtT = A.f32(256); dAT = A.f32(256); acsT = A.f32(256); nacsT = A.f32(256)
        lastB = A.f32(256); cdB = A.f32(256); ds = A.f32(256); ea = A.f32(16)
        Bt1, Bdt, BdA, Bacs, Bnacs, Blast, Bcd, Bds, Bea = [Buf() for _ in range(9)]
        acsF = A.f32(T); BacsF = Buf()
        xTg = A.bf(4, T); BxT = Buf()
        BTg = A.bf(T); CTg = A.bf(T); BBT, BCT = Buf(), Buf()
        zsg = A.bf(4, T); Bzs = Buf()
        HT = A.f32(512); HTb = A.bf(512); BH, BHb = Buf(), Buf()
        xc = [A.bf(512) for _ in range(2)]; Bxc = [Buf(), Buf()]
        xcd = [A.bf(512) for _ in range(2)]; Bxcd = [Buf(), Buf()]
        Btok = [A.bf(P) for _ in range(2)]; BBtok = [Buf(), Buf()]
        CBm = [A.f32(P) for _ in range(2)]; BCBm = [Buf(), Buf()]
        EAc = [A.f32(4, P) for _ in range(2)]; BEA = [Buf(), Buf()]
        Ebuf = [A.f32(8, P) for _ in range(2)]; BE = [Buf(), Buf()]
        sq = A.bf(4, P); Bsq = Buf()
        rstd = A.f32(P); Brs = Buf()
        ob4 = [A.bf(4, TT) for _ in range(2)]; Bob4 = [Buf(), Buf()]
        ohx = A.f32(8, P); Boh = Buf()
        t.dma("sp", ohx[0:16, :, :], c_oh.rearrange("r (a b) -> r a b", a=16)[:, 0:8, :], w=[Boh])
        OVL = A.off
        ubuf = A.f32(T + 4); Bu = Buf()
        acc = A.f32(T); Bacc = Buf()
        A.off = OVL
        Zb = A.f32(8, P); BZ = Buf()
        Wb = A.bf(8, P); BW = Buf()
        t0 = A.f32(4, P); Bt0 = Buf()
        xD = A.f32(4, P); BxD = Buf()
        A.off = max(A.off, OVL + 2 * T + 4)
        U = caus
        PL = os.environ.get("SSM_POOL", "pool")

        wdt, Bwdt = load_w(w_in[l, 44])
        for c in range(16):
            for kt in range(16):
                t.op("pe", lambda e, c=c, kt=kt: e.matmul(psum[6][:, c * 16:(c + 1) * 16], lhsT=hT[:, kt, c * P:(c + 1) * P],
                                                         rhs=wdt[:, kt, 0:16], start=(kt == 0), stop=(kt == 15)),
                     r=[Bwdt, hTb[c // 4]], w=[PSB[6]])
            t.op("dve", lambda e, c=c: e.tensor_tensor(out=t1[:, c * 16:(c + 1) * 16], in0=psum[6][:, c * 16:(c + 1) * 16],
                                                      in1=ptab[:, PT_OFF["dtb"] + l * 16:PT_OFF["dtb"] + l * 16 + 16], op=OP.add),
                 r=[PSB[6], Bpt], w=[Bt1])
        t.op("act", lambda e: e.activation(out=t1, in_=t1, func=AF.Exp), r=[Bt1], w=[Bt1])
        t.op("act", lambda e: e.activation(out=dtT, in_=t1, func=AF.Ln, bias=1.0), r=[Bt1], w=[Bdt])
        t.op("act", lambda e: e.activation(out=ea, in_=ptab[:, PT_OFF["alog"] + l * 16:PT_OFF["alog"] + l * 16 + 16], func=AF.Exp),
             r=[Bpt], w=[Bea])
        for c in range(16):
            t.op("dve", lambda e, c=c: e.scalar_tensor_tensor(out=dAT[:, c * 16:(c + 1) * 16], in0=dtT[:, c * 16:(c + 1) * 16],
                                                             scalar=-1.0, in1=ea, op0=OP.mult, op1=OP.mult),
                 r=[Bdt, Bea], w=[BdA])
        for c in range(16):
            t.op("pe", lambda e, c=c: e.matmul(psum[2][:, c * 16:(c + 1) * 16], lhsT=tri, rhs=dAT[:, c * 16:(c + 1) * 16],
                                              start=True, stop=True), r=[BdA, Bc], w=[PSB[2]])
            t.op("pe", lambda e, c=c: e.matmul(psum[3][:, c * 16:(c + 1) * 16], lhsT=ones_f, rhs=dAT[:, c * 16:(c + 1) * 16],
                                              start=True, stop=True), r=[BdA, Bc], w=[PSB[3]])
        t.op("dve", lambda e: e.tensor_copy(out=acsT, in_=psum[2][:, 0:256]), r=[PSB[2]], w=[Bacs])
        t.op("dve", lambda e: e.tensor_scalar(out=nacsT, in0=psum[2][:, 0:256], scalar1=-1.0, scalar2=0.0, op0=OP.mult, op1=OP.add),
             r=[PSB[2]], w=[Bnacs])
        t.op("dve", lambda e: e.tensor_copy(out=lastB, in_=psum[3][:, 0:256]), r=[PSB[3]], w=[Blast])
        t.op("act", lambda e: e.activation(out=cdB, in_=psum[3][:, 0:256], func=AF.Exp), r=[PSB[3]], w=[Bcd])
        t.op("dve", lambda e: e.tensor_tensor(out=ds, in0=lastB, in1=acsT, op=OP.subtract), r=[Blast, Bacs], w=[Bds])
        t.op("act", lambda e: e.activation(out=ds, in_=ds, func=AF.Exp), r=[Bds], w=[Bds])
        for tt in range(NTT):
            for k in range(4):
                c = tt * 4 + k
                t.op("pe", lambda e, c=c, k=k: e.matmul(psum[4][0:16, k * P:(k + 1) * P], lhsT=dAT[:, c * 16:(c + 1) * 16], rhs=tri,
                                                       start=True, stop=True), r=[BdA, Bc], w=[PSB[4]])
            t.op("dve", lambda e, tt=tt: e.tensor_copy(out=acsF[0:16, tt * TT:(tt + 1) * TT], in_=psum[4][0:16, :]),
                 r=[PSB[4]], w=[BacsF])

        GORD = (1, 0) if os.environ.get('SSM_SWAP') else (0, 1)
        for g in GORD:
            t.op("dve", lambda e: e.memset(ubuf[:, 0:3], 0.0), w=[Bu])

            def conv_tile(ctile, out_ap, Bout):
                wt, Bw = load_w(w_in[l, 32 + ctile])

                def ev(tt, ts, ps, pb):
                    t.op("act", lambda e: e.activation(out=ubuf[:, 3 + tt * TT:3 + (tt + 1) * TT], in_=ps[:, :], func=AF.Copy),
                         r=[pb], w=[Bu])
                proj_tile(wt, Bw, ev)
                cw = PT_OFF["sconv_w"] + (l * 12 + ctile) * 4
                t.op("dve", lambda e: e.tensor_scalar(out=acc, in0=ubuf[:, 3:3 + T], scalar1=ptab[:, cw + 3:cw + 4], scalar2=0.0,
                                                      op0=OP.mult, op1=OP.add), r=[Bu, Bpt], w=[Bacc])
                for k in range(3):
                    t.op("dve", lambda e, k=k: e.scalar_tensor_tensor(out=acc, in0=ubuf[:, k:k + T], scalar=ptab[:, cw + k:cw + k + 1],
                                                                     in1=acc, op0=OP.mult, op1=OP.add), r=[Bu, Bpt, Bacc], w=[Bacc])
                t.op("act", lambda e: e.activation(out=out_ap, in_=acc, func=AF.Silu, bias=pcol("sconv_b", l * 12 + ctile)),
                     r=[Bacc, Bpt], w=[Bout])

            for ct in range(4):
                conv_tile(4 * g + ct, xTg[:, ct, :], BxT)
            conv_tile(8 + g, BTg, BBT)
            conv_tile(10 + g, CTg, BCT)
            for ct in range(4):
                wt, Bw = load_w(w_in[l, 24 + 4 * g + ct])

                def evz(tt, ts, ps, pb, ct=ct):
                    t.op("act", lambda e: e.activation(out=zsg[:, ct, ts], in_=ps[:, :], func=AF.Silu), r=[pb], w=[Bzs])
                proj_tile(wt, Bw, evz)
            t.barrier()
            t.op("dve", lambda e: e.memset(HT, 0.0), w=[BH])
            t.op("dve", lambda e: e.memset(HTb, 0.0), w=[BHb])

            def front(c):
                cs = slice(c * P, (c + 1) * P)
                cb = c % 2
                hs = slice(c * 16 + 8 * g, c * 16 + 8 * g + 8)
                for ct in range(4):
                    t.op("pe", lambda e, ct=ct: e.transpose(out=psbf[:, ct * P:(ct + 1) * P], in_=xTg[:, ct, cs], identity=ident),
                         r=[BxT, Bc], w=[PSB[7]])
                t.op("pe", lambda e: e.transpose(out=psbf[:, 512:640], in_=BTg[:, cs], identity=ident), r=[BBT, Bc], w=[PSB[7]])
                t.op("dve", lambda e: e.tensor_tensor(
                    out=xc[cb].rearrange("p (h d) -> p h d", h=8), in0=psbf[:, 0:512].rearrange("p (h d) -> p h d", h=8),
                    in1=dtT[:, hs].unsqueeze(2).to_broadcast([P, 8, 64]), op=OP.mult), r=[PSB[7], Bdt], w=[Bxc[cb]])
                t.op("act", lambda e: e.activation(out=Btok[cb], in_=psbf[:, 512:640], func=AF.Copy), r=[PSB[7]], w=[BBtok[cb]])
                t.op(PL, lambda e: e.tensor_tensor(
                    out=xcd[cb].rearrange("p (h d) -> p h d", h=8), in0=xc[cb].rearrange("p (h d) -> p h d", h=8),
                    in1=ds[:, hs].unsqueeze(2).to_broadcast([P, 8, 64]), op=OP.mult), r=[Bxc[cb], Bds], w=[Bxcd[cb]])
                t.op("pe", lambda e: e.matmul(psum[6][:, 0:P], lhsT=BTg[:, cs], rhs=CTg[:, cs], start=True, stop=True),
                     r=[BBT, BCT], w=[PSB[6]])
                t.op("dve", lambda e: e.tensor_tensor(out=CBm[cb], in0=psum[6][:, 0:P], in1=tri, op=OP.mult),
                     r=[PSB[6], Bc], w=[BCBm[cb]])
                t.op("dve", lambda e: e.tensor_tensor(out=Zb, in0=tri.unsqueeze(1).to_broadcast([P, 8, P]),
                                                      in1=dAT[:, hs].unsqueeze(2).to_broadcast([P, 8, P]), op=OP.mult),
                     r=[Bc, BdA], w=[BZ])
                for hb in range(2):
                    t.op("pe", lambda e, hb=hb: e.matmul(psum[hb][:, :], lhsT=U, rhs=Zb[:, 4 * hb:4 * hb + 4, :], start=True, stop=True),
                         r=[BZ, Bc], w=[PSB[hb]])
                    t.op("act", lambda e, hb=hb: e.activation(out=Ebuf[cb][:, 4 * hb:4 * hb + 4, :], in_=psum[hb][:, :].rearrange("p (a b) -> p a b", a=4), func=AF.Exp),
                         r=[PSB[hb]], w=[BE[cb]])
                for ct in range(4):
                    t.op("pe", lambda e, ct=ct, g=g: e.matmul(psum[5][:, ct * P:(ct + 1) * P], lhsT=ohx[0:16, 4 * g + ct, :], rhs=acsF[0:16, cs],
                                                       start=True, stop=True), r=[BacsF, Boh], w=[PSB[5]])
                t.op("act", lambda e: e.activation(out=EAc[cb], in_=psum[5][:, :].rearrange("p (a b) -> p a b", a=4), func=AF.Exp), r=[PSB[5]], w=[BEA[cb]])

            def back(c):
                cs = slice(c * P, (c + 1) * P)
                cb = c % 2
                hs = slice(c * 16 + 8 * g, c * 16 + 8 * g + 8)
                t.op(PL, lambda e: e.tensor_tensor(out=Wb, in0=Ebuf[cb], in1=CBm[cb].unsqueeze(1).to_broadcast([P, 8, P]), op=OP.mult),
                     r=[BE[cb], BCBm[cb]], w=[BW])
                for ct in range(4):
                    t.op("pe", lambda e, ct=ct: e.matmul(psum[4][:, ct * P:(ct + 1) * P], lhsT=HTb[:, ct * P:(ct + 1) * P], rhs=CTg[:, cs],
                                                       start=True, stop=True), r=[BHb, BCT], w=[PSB[4]])
                for k in range(8):
                    ct = k // 2
                    t.op("pe", lambda e, k=k, ct=ct: e.matmul(psum[2 + k % 2][:, ct * P:(ct + 1) * P], lhsT=xc[cb][:, ct * P:(ct + 1) * P],
                                                             rhs=Wb[:, k, :], start=True, stop=True), r=[Bxc[cb], BW], w=[PSB[2 + k % 2]])
                dbg = taps and c == 1 and g == GORD[0]
                if dbg:
                    t.dma("sp", DBG[:, 0:512], EAc[cb].rearrange("p a b -> p (a b)"), r=[BEA[cb]], owner=BEA[cb])
                    t.dma("sp", DBG[:, 512:1024], HT, r=[BH], owner=BH)
                t.op("dve", lambda e: e.tensor_tensor(out=t0, in0=psum[4][:, :].rearrange("p (a b) -> p a b", a=4), in1=EAc[cb], op=OP.mult),
                     r=[PSB[4], BEA[cb]], w=[Bt0])
                if dbg:
                    t.dma("sp", DBG[:, 1024:1536], t0.rearrange("p a b -> p (a b)"), r=[Bt0], owner=Bt0)
                for half in range(2):
                    pr = slice(half * 64, half * 64 + 64)
                    t.op("dve", lambda e, half=half, pr=pr: e.tensor_tensor(
                        out=t0[pr], in0=t0[pr], in1=psum[2 + half][pr, :].rearrange("p (a b) -> p a b", a=4), op=OP.add),
                        r=[Bt0, PSB[2 + half]], w=[Bt0])
                dc0 = PT_OFF["dexp"] + l * 8 + 4 * g
                t.op(PL, lambda e: e.tensor_tensor(out=xD, in0=xTg[:, :, cs], in1=ptab[:, dc0:dc0 + 4].unsqueeze(2).to_broadcast([P, 4, P]),
                                                       op=OP.mult), r=[BxT, Bpt], w=[BxD])
                t.op("dve", lambda e: e.tensor_tensor(out=t0, in0=t0, in1=xD, op=OP.add), r=[Bt0, BxD], w=[Bt0])
                t.op("dve", lambda e: e.tensor_tensor(out=t0, in0=t0, in1=zsg[:, :, cs], op=OP.mult), r=[Bt0, Bzs], w=[Bt0])
                t.op("act", lambda e: e.activation(out=sq, in_=t0, func=AF.Square), r=[Bt0], w=[Bsq])
                for ct in range(4):
                    t.op("pe", lambda e, ct=ct: e.matmul(psum[6][:, P:2 * P], lhsT=ones_bf, rhs=sq[:, ct, :], start=(ct == 0), stop=(ct == 3)),
                         r=[Bsq, Bc], w=[PSB[6]])
                t.op("act", lambda e: e.activation(out=rstd, in_=psum[6][:, P:2 * P], func=AF.Sqrt, bias=EPS, scale=1.0 / 512),
                     r=[PSB[6]], w=[Brs])
                t.op("dve", lambda e: e.reciprocal(out=rstd, in_=rstd), r=[Brs], w=[Brs])
                o4 = (c // 4) % 2
                for ct in range(4):
                    ncol = PT_OFF["snw"] + l * 8 + 4 * g + ct
                    t.op("dve", lambda e, ct=ct, ncol=ncol: e.scalar_tensor_tensor(
                        out=ob4[o4][:, ct, (c % 4) * P:(c % 4 + 1) * P], in0=t0[:, ct, :], scalar=ptab[:, ncol:ncol + 1], in1=rstd,
                        op0=OP.mult, op1=OP.mult), r=[Bt0, Bpt, Brs], w=[Bob4[o4]])
                if c % 4 == 3:
                    f0 = 1024 + 512 * g
                    t.dma("sp", MIX[f0:f0 + 512, (c // 4) * TT:(c // 4 + 1) * TT].rearrange("(a p) t -> p a t", p=P), ob4[o4],
                          r=[Bob4[o4]], owner=Bob4[o4])
                t.op("pe", lambda e: e.matmul(psum[4][:, :], lhsT=Btok[cb], rhs=xcd[cb], start=True, stop=True),
                     r=[BBtok[cb], Bxcd[cb]], w=[PSB[4]])
                t.op("dve", lambda e: e.tensor_tensor(out=HT.rearrange("p (h d) -> p h d", h=8), in0=HT.rearrange("p (h d) -> p h d", h=8),
                                                      in1=cdB[:, hs].unsqueeze(2).to_broadcast([P, 8, 64]), op=OP.mult),
                     r=[BH, Bcd], w=[BH])
                t.op("dve", lambda e: e.tensor_tensor(out=HT, in0=HT, in1=psum[4][:, :], op=OP.add), r=[BH, PSB[4]], w=[BH])
                t.op("act", lambda e: e.activation(out=HTb, in_=HT, func=AF.Copy), r=[BH], w=[BHb])

            if not os.environ.get("SSM_NOPIPE"):
                front(0)
                for c in range(16):
                    if c + 1 < 16:
                        front(c + 1)
                    back(c)
            else:
                for c in range(16):
                    front(c)
                    back(c)
            t.barrier()
        t.barrier()

    def epilogue(m, Bm, psstat, Bpsstat, tq, Xin, Xout, gpost, l, gnext, lnext, bufs):
        xt, Bxt, sq, Bsq, rstd, Brs = bufs
        ts = slice(tq * TT, (tq + 1) * TT)
        t.op("act", lambda e: e.activation(out=rstd, in_=psstat, func=AF.Sqrt, bias=EPS, scale=1.0 / D), r=[Bpsstat], w=[Brs])
        t.op("dve", lambda e: e.reciprocal(out=rstd, in_=rstd), r=[Brs], w=[Brs])
        gsz = xt.shape[1]
        for q in range(16 // gsz):
            t.dma("sp", xt, xview(Xin)[:, gsz * q:gsz * q + gsz, ts], w=[Bxt])
            for f in range(gsz):
                ft = gsz * q + f
                t.op("dve", lambda e, ft=ft: e.scalar_tensor_tensor(out=m[:, ft, :], in0=m[:, ft, :], scalar=pcol(gpost, l * 16 + ft),
                                                                   in1=rstd, op0=OP.mult, op1=OP.mult), r=[Bm, Bpt, Brs], w=[Bm])
                t.op("dve", lambda e, ft=ft, f=f: e.tensor_tensor(out=m[:, ft, :], in0=m[:, ft, :], in1=xt[:, f, :], op=OP.add),
                     r=[Bm, Bxt], w=[Bm])
        for q in range(4):
            t.dma("sp", xview(Xout)[:, 4 * q:4 * q + 4, ts], m[:, 4 * q:4 * q + 4, :], r=[Bm], owner=Bm)
        if gnext is not None:
            for ft in range(16):
                b = ft % 2
                t.op("act", lambda e, ft=ft, b=b: e.activation(out=sq[b], in_=m[:, ft, :], func=AF.Square), r=[Bm], w=[Bsq[b]])
                t.op("pe", lambda e, ft=ft, b=b: e.matmul(psum[6][:, :], lhsT=ones_bf, rhs=sq[b], start=(ft == 0), stop=(ft == 15)),
                     r=[Bsq[b], Bc], w=[PSB[6]])
            t.op("act", lambda e: e.activation(out=rstd, in_=psum[6][:, :], func=AF.Sqrt, bias=EPS, scale=1.0 / D), r=[PSB[6]], w=[Brs])
            t.op("dve", lambda e: e.reciprocal(out=rstd, in_=rstd), r=[Brs], w=[Brs])
            for ft in range(16):
                t.op("dve", lambda e, ft=ft: e.scalar_tensor_tensor(out=hT[:, ft, ts], in0=m[:, ft, :], scalar=pcol(gnext, lnext * 16 + ft),
                                                                   in1=rstd, op0=OP.mult, op1=OP.mult), r=[Bm, Bpt, Brs], w=[hTb[tq]])

    def stage_o(l, Xin, Xout):
        A.off = PHASE0
        mixt = A.bf(16, 2 * TT); Bmx = [Buf() for _ in range(4)]
        m = A.f32(16, 2 * TT); Bm = [Buf(), Buf()]
        xt = A.f32(2, TT); Bxt = Buf()
        sq = [A.bf(TT) for _ in range(2)]; Bsq = [Buf(), Buf()]
        rstd = A.f32(TT); Brs = Buf()
        sq2 = sq; Bsq2 = Bsq
        for th in range(2):
            hs_ = slice(th * 2 * TT, (th + 1) * 2 * TT)
            for q in range(4):
                t.dma("sp", mixt[:, 4 * q:4 * q + 4, :], MIX.rearrange("(f p) t -> p f t", p=P)[:, 4 * q:4 * q + 4, hs_], w=[Bmx[q]])
            for ft in range(16):
                wt, Bw = load_w(w_out[l, ft])
                for sub in range(2):
                    b = sub
                    ss = slice(sub * TT, (sub + 1) * TT)
                    for kt in range(16):
                        t.op("pe", lambda e, kt=kt, b=b, wt=wt, ss=ss: e.matmul(psum[b][:, :], lhsT=wt[:, kt, :], rhs=mixt[:, kt, ss],
                                                                               start=(kt == 0), stop=(kt == 15)), r=[Bw, Bmx[kt // 4]], w=[PSB[b]])
                    t.op("act", lambda e, ft=ft, b=b, ss=ss: e.activation(out=m[:, ft, ss], in_=psum[b][:, :], func=AF.Copy), r=[PSB[b]], w=[Bm[sub]])
                    t.op("act", lambda e, ft=ft, b=b: e.activation(out=sq2[b], in_=psum[b][:, :], func=AF.Square), r=[PSB[b]], w=[Bsq2[b]])
                    t.op("pe", lambda e, ft=ft, b=b, sub=sub: e.matmul(psum[4 + sub][:, :], lhsT=ones_bf, rhs=sq2[b], start=(ft == 0), stop=(ft == 15)),
                         r=[Bsq2[b], Bc], w=[PSB[4 + sub]])
            for sub in range(2):
                ss = slice(sub * TT, (sub + 1) * TT)
                epilogue(m[:, :, ss], Bm[sub], psum[4 + sub][:, :], PSB[4 + sub], th * 2 + sub, Xin, Xout, "g_mix_post", l, "g_ffn_pre", l,
                         (xt, Bxt, sq, Bsq, rstd, Brs))
        t.barrier(pool=False)

    def stage_f1(l):
        A.off = PHASE0
        ug = A.f32(T + 2); uu = A.f32(T + 2); Bug, Buu = Buf(), Buf()
        ag = A.f32(T); au = A.f32(T); Bag, Bau = Buf(), Buf()
        gb = [A.bf(T) for _ in range(2)]; Bgb = [Buf(), Buf()]
        t.op("dve", lambda e: e.memset(ug[:, 0:2], 0.0), w=[Bug])
        t.op("dve", lambda e: e.memset(uu[:, 0:2], 0.0), w=[Buu])
        for j in range(NHT):
            for (tile, u, Bu_, a, Ba) in ((j, ug, Bug, ag, Bag), (NHT + j, uu, Buu, au, Bau)):
                wt, Bw = load_w(w_up[l, tile])

                def ev(tt, ts, ps, pb, u=u, Bu_=Bu_):
                    t.op("act", lambda e: e.activation(out=u[:, 2 + tt * TT:2 + (tt + 1) * TT], in_=ps[:, :], func=AF.Copy), r=[pb], w=[Bu_])
                proj_tile(wt, Bw, ev)
                cw = PT_OFF["fconv_w"] + (l * 88 + tile) * 3
                t.op("dve", lambda e, u=u, a=a, cw=cw, tile=tile: e.tensor_scalar(
                    out=a, in0=u[:, 2:2 + T], scalar1=ptab[:, cw + 2:cw + 3], scalar2=pcol("fconv_b", l * 88 + tile),
                    op0=OP.mult, op1=OP.add), r=[Bu_, Bpt], w=[Ba])
                for k in range(2):
                    t.op("dve", lambda e, u=u, a=a, cw=cw, k=k: e.scalar_tensor_tensor(
                        out=a, in0=u[:, k:k + T], scalar=ptab[:, cw + k:cw + k + 1], in1=a, op0=OP.mult, op1=OP.add),
                        r=[Bu_, Bpt, Ba], w=[Ba])
            t.op("act", lambda e: e.activation(out=ag, in_=ag, func=AF.Gelu_apprx_tanh), r=[Bag], w=[Bag])
            t.op("dve", lambda e, j=j: e.tensor_tensor(out=gb[j % 2], in0=ag, in1=au, op=OP.mult), r=[Bag, Bau], w=[Bgb[j % 2]])
            t.dma("sp", G[j * P:(j + 1) * P, :], gb[j % 2], r=[Bgb[j % 2]], owner=Bgb[j % 2])
        t.barrier()

    def stage_f2(l, Xin, Xout, gnext, lnext):
        A.off = PHASE0 - NWS * 1024
        gt = A.bf(NHT, TT); Bgt = [Buf() for _ in range(4)]
        m = A.f32(16, TT); Bm = Buf()
        xt = A.f32(4, TT); Bxt = Buf()
        sq = [A.bf(TT) for _ in range(2)]; Bsq = [Buf(), Buf()]
        rstd = A.f32(TT); Brs = Buf()
        sq2 = [A.bf(TT) for _ in range(2)]; Bsq2 = [Buf(), Buf()]
        wd = [A.bf(NHT, P) for _ in range(2)]; Bwd = [Buf(), Buf()]
        for tq in range(NTT):
            ts = slice(tq * TT, (tq + 1) * TT)
            for q in range(4):
                t.dma("sp", gt[:, 11 * q:11 * q + 11, :], G.rearrange("(f p) t -> p f t", p=P)[:, 11 * q:11 * q + 11, ts], w=[Bgt[q]])
            for ft in range(16):
                b = ft % 2
                t.dma("pool", wd[b], w_dn[l, ft].rearrange("p (k n) -> p k n", k=NHT), w=[Bwd[b]])
                for kt in range(NHT):
                    t.op("pe", lambda e, kt=kt, b=b: e.matmul(psum[b][:, :], lhsT=wd[b][:, kt, :], rhs=gt[:, kt, :],
                                                             start=(kt == 0), stop=(kt == NHT - 1)), r=[Bwd[b], Bgt[kt // 11]], w=[PSB[b]])
                t.op("act", lambda e, ft=ft, b=b: e.activation(out=m[:, ft, :], in_=psum[b][:, :], func=AF.Copy), r=[PSB[b]], w=[Bm])
                t.op("act", lambda e, ft=ft, b=b: e.activation(out=sq2[b], in_=psum[b][:, :], func=AF.Square), r=[PSB[b]], w=[Bsq2[b]])
                t.op("pe", lambda e, ft=ft, b=b: e.matmul(psum[5][:, :], lhsT=ones_bf, rhs=sq2[b], start=(ft == 0), stop=(ft == 15)),
                     r=[Bsq2[b], Bc], w=[PSB[5]])
            epilogue(m, Bm, psum[5][:, :], PSB[5], tq, Xin, Xout, "g_ffn_post", l, gnext, lnext, (xt, Bxt, sq, Bsq, rstd, Brs))
        t.barrier()

    Xcur = xT
    stage_n1(xT, "g_mix_pre", 0)
    done = False
    for l in range(n_layers):
        if stop_after == "n1":
            break
        for h in range(int(os.environ.get('NHEADS', '8'))):
            stage_attn(l, h)
        if stop_after == "attn":
            break
        stage_ssm(l)
        if stop_after == "ssm":
            break
        stage_o(l, Xcur, XA)
        if stop_after == "o":
            break
        stage_f1(l)
        if stop_after == "f1":
            break
        last = (l == n_layers - 1)
        Xn = yT if last else XB
        stage_f2(l, XA, Xn, None if last else "g_mix_pre", l + 1)
        Xcur = Xn
    t.barrier()
    t.emit()
    es.close()
    return nc, {"MIX": MIX, "XA": XA, "G": G}


def _t5_bucket_np(rel):
    n = np.maximum(rel, 0)
    max_exact = 16
    nf = np.maximum(n, 1).astype(np.float32)
    large = max_exact + (np.log(nf / max_exact) / np.log(128 / max_exact) * (32 - max_exact)).astype(np.int32)
    large = np.minimum(large, 31)
    return np.where(n < max_exact, n, large)


def _tile_w(w, kt, ntiles):
    K, N = w.shape
    if N < ntiles * 128:
        wp = np.zeros((K, ntiles * 128), np.float32)
        wp[:, :N] = w
        w = wp
    return np.ascontiguousarray(w.reshape(kt, 128, ntiles, 128).transpose(2, 1, 0, 3)).reshape(ntiles, 128, kt * 128)


def prep_shared(inp):
    f = np.float32
    sh = {}
    sh["w_in"] = np.stack([_tile_w(np.asarray(inp["w_in"][l], f), 16, IN_TILES) for l in range(L)])
    sh["w_out"] = np.stack([_tile_w(np.asarray(inp["w_out"][l], f), 16, 16) for l in range(L)])
    sh["w_up"] = np.stack([_tile_w(np.asarray(inp["w_ffn_up"][l], f), 16, 88) for l in range(L)])
    sh["w_dn"] = np.stack([_tile_w(np.asarray(inp["w_ffn_down"][l], f), NHT, 16) for l in range(L)])
    pt = np.zeros((P, PT_COLS), f)

    def put(name, arr):
        n = arr.shape[0]
        pt[:, PT_OFF[name]:PT_OFF[name] + n] = arr.T

    for name, key in (("g_mix_pre", "ln_mix_pre"), ("g_mix_post", "ln_mix_post"), ("g_ffn_pre", "ln_ffn_pre"), ("g_ffn_post", "ln_ffn_post")):
        put(name, np.asarray(inp[key], f).reshape(L * 16, P))
    scw = np.asarray(inp["ssm_conv_w"], f)
    put("sconv_w", scw.reshape(L, 4, 12, P).transpose(0, 2, 1, 3).reshape(L * 12 * 4, P))
    put("sconv_b", np.asarray(inp["ssm_conv_b"], f).reshape(L * 12, P))
    dsk = np.asarray(inp["d_skip"], f)
    put("dexp", np.repeat(dsk, 64, axis=1).reshape(L * 8, P))
    put("snw", np.asarray(inp["ssm_norm_w"], f).reshape(L * 8, P))
    fcw = np.asarray(inp["ffn_conv_w"], f)
    put("fconv_w", fcw.reshape(L, 3, 88, P).transpose(0, 2, 1, 3).reshape(L * 88 * 3, P))
    put("fconv_b", np.asarray(inp["ffn_conv_b"], f).reshape(L * 88, P))
    pt[:, PT_OFF["dtb"]:PT_OFF["dtb"] + L * 16] = np.asarray(inp["dt_bias"], f).reshape(1, L * 16)
    pt[:, PT_OFF["alog"]:PT_OFF["alog"] + L * 16] = np.asarray(inp["a_log"], f).reshape(1, L * 16)
    sh["ptab"] = pt
    rb = np.asarray(inp["rel_bias"], f)
    kp = np.arange(P)[:, None]
    cc = np.arange(1024)[None, :]
    rel = cc - 384 - kp
    idx = _t5_bucket_np(rel)
    tab = rb[idx, :]
    tab = np.where((rel >= 0)[:, :, None], tab, f(NEG))
    sh["relT"] = np.ascontiguousarray(tab.transpose(2, 0, 1)).astype(f)
    sh["cfar"] = np.ascontiguousarray(np.repeat(rb[31, :][:, None], T, axis=1)).astype(f)
    sh["c_ident"] = np.eye(P, dtype=f)
    ti = np.arange(P)
    sh["c_tri"] = (ti[:, None] <= ti[None, :]).astype(f)
    sh["c_caus"] = (ti[:, None] > ti[None, :]).astype(f)
    ohm = np.zeros((16, 16, P), f)
    for j in range(8):
        for m_ in range(P):
            ohm[2 * j + m_ // 64, j, m_] = 1.0
    sh["c_oh"] = ohm.reshape(16, 16 * P)
    lm = np.zeros((9, 16, P), f)
    for n in range(8):
        for far in range(2):
            lm[n, 2 * n + far, :] = 1.0
            lm[8, 2 * n + far, :] = float(far)
    sh["c_lmat"] = lm.reshape(9, 16 * P)
    return sh


_CACHE = {}


def kernel(**inputs):
    x = np.asarray(inputs["x"], np.float32)
    sh = prep_shared(inputs)
    if "nc" not in _CACHE:
        _CACHE["nc"] = build_program()[0]
    nc = _CACHE["nc"]
    in_maps = []
    for c in range(8):
        d = dict(sh)
        d["xT"] = np.ascontiguousarray(x[c % 4].T)
        in_maps.append(d)
    res = run_bass_kernel_spmd(nc, in_maps, core_ids=list(range(8)))
    out = np.stack([np.ascontiguousarray(res.results[b]["yT"].T) for b in range(4)], axis=0)
    return out.astype(np.float32)
```

```python
import os
import numpy as np
from contextlib import ExitStack
import concourse.bass as bass
import concourse.mybir as mybir
from concourse.bass_utils import run_bass_kernel_spmd

F32 = mybir.dt.float32
BF = mybir.dt.bfloat16
AF = mybir.ActivationFunctionType
OP = mybir.AluOpType
AX = mybir.AxisListType

P = 128
T = 2048
D = 2048
L = 4
NFT = 16
TT = 512
NTT = 4
HID = 5632
NHT = 44
IN_COLS = 5648
IN_TILES = 45
EPS = 1e-6
NEG = -1e30


class Buf:
    __slots__ = ("name", "writers", "readers", "sem", "semcnt", "excl")

    def __init__(self, name="", excl=False):
        self.name = name
        self.excl = excl
        self.writers = []
        self.readers = []
        self.sem = None
        self.semcnt = 0


class Trk:
    ENG = ("pe", "act", "dve", "pool", "sp")

    def __init__(self, nc):
        self.nc = nc
        self.ops = {e: [] for e in self.ENG}
        self.cnt = {e: 0 for e in self.ENG}
        self.known = {e: {} for e in self.ENG}
        self.nsem = 0
        self.semcount = {}
        self.free_sems = []
        self.owners = []

    def _need(self, eng, tok, waits):
        key, val, teng = tok
        if teng == eng and eng == "pe":
            return
        if self.known[eng].get(key, 0) >= val:
            return
        if waits.get(key, 0) < val:
            waits[key] = val

    def _flush(self, eng, waits):
        for k, v in waits.items():
            self.known[eng][k] = v
            self.ops[eng].append(("wait", k, v))

    def _deps(self, eng, r, w):
        waits = {}
        for b in r:
            for t in b.writers:
                self._need(eng, t, waits)
            if b.excl:
                for t in b.readers:
                    if t[2] != eng:
                        self._need(eng, t, waits)
        for b in w:
            for t in b.writers:
                self._need(eng, t, waits)
            for t in b.readers:
                self._need(eng, t, waits)
        self._flush(eng, waits)

    def _commit(self, tok, r, w):
        for b in r:
            b.readers.append(tok)
            if len(b.readers) > 64:
                last = {}
                for t in b.readers:
                    if last.get(t[0], (0, 0, 0))[1] < t[1]:
                        last[t[0]] = t
                b.readers = list(last.values())
        for b in w:
            b.writers = [tok]
            b.readers = []

    def op(self, eng, fn, r=(), w=()):
        self._deps(eng, r, w)
        self.cnt[eng] += 1
        tok = (eng, self.cnt[eng], eng)
        self.ops[eng].append(("op", fn))
        self._commit(tok, r, w)

    def dma(self, eng, out, in_, r=(), w=(), owner=None):
        self._deps(eng, r, w)
        owner = owner or (w[0] if w else r[0])
        if owner.sem is None:
            if self.free_sems:
                owner.sem = self.free_sems.pop()
                owner.semcnt = self.semcount.get(owner.sem, 0)
            else:
                owner.sem = ("d", self.nsem)
                self.nsem += 1
            self.owners.append(owner)
        owner.semcnt += 16
        self.semcount[owner.sem] = owner.semcnt
        tok = (owner.sem, owner.semcnt, "dma")
        self.ops[eng].append(("dma", out, in_, owner.sem))
        self._commit(tok, r, w)

    def barrier(self, pool=True):
        for e in self.ENG:
            if e == "pool" and not pool and os.environ.get("POOL_AHEAD"):
                continue
            waits = {}
            for f in self.ENG:
                if self.cnt[f] > 0 and not (e == f == "pe"):
                    self._need(e, (f, self.cnt[f], f), waits)
            for k, v in self.semcount.items():
                self._need(e, (k, v, "dma"), waits)
            self._flush(e, waits)
        keep = []
        for o in self.owners:
            if o.name:
                keep.append(o)
            else:
                self.free_sems.append(o.sem)
                o.sem = None
        self.owners = keep

    def emit(self):
        nc = self.nc
        with ExitStack() as es:
            sems = {}
            for e in self.ENG:
                sems[e] = es.enter_context(nc.semaphore("s_" + e))
            for i in range(self.nsem):
                sems[("d", i)] = es.enter_context(nc.semaphore("sd%d" % i))
            block = es.enter_context(nc.Block())
            engobj = {"pe": "tensor", "act": "scalar", "dve": "vector", "pool": "gpsimd", "sp": "sync"}

            def make(ename):
                def body(eng):
                    for o in self.ops[ename]:
                        if o[0] == "wait":
                            eng.wait_ge(sems[o[1]], o[2])
                        elif o[0] == "op":
                            o[1](eng).then_inc(sems[ename], 1)
                        else:
                            eng.dma_start(out=o[1], in_=o[2]).then_inc(sems[o[3]], 16)
                return body
            for ename in self.ENG:
                if self.ops[ename]:
                    getattr(block, engobj[ename])(make(ename))


def _ptab_layout():
    off = {}
    c = 0
    for name, n in (("g_mix_pre", L * 16), ("g_mix_post", L * 16), ("g_ffn_pre", L * 16), ("g_ffn_post", L * 16),
                    ("sconv_w", L * 12 * 4), ("sconv_b", L * 12), ("dexp", L * 8), ("snw", L * 8),
                    ("fconv_w", L * 88 * 3), ("fconv_b", L * 88), ("dtb", L * 16), ("alog", L * 16)):
        off[name] = c
        c += n
    return off, c


PT_OFF, PT_COLS = _ptab_layout()


class Arena:
    def __init__(self, ap, words):
        self.ap = ap
        self.words = words
        self.off = 0

    def f32(self, *shape):
        n = int(np.prod(shape))
        a = self.ap[:, self.off:self.off + n]
        self.off += n
        assert self.off <= self.words, ("arena overflow", self.off, self.words)
        if len(shape) == 2:
            return a.rearrange("p (a b) -> p a b", a=shape[0])
        if len(shape) == 3:
            return a.rearrange("p (a b c) -> p a b c", a=shape[0], b=shape[1])
        return a

    def bf(self, *shape):
        n = int(np.prod(shape))
        w = (n + 1) // 2
        a = self.ap[:, self.off:self.off + w].bitcast(BF)
        self.off += w
        assert self.off <= self.words, ("arena overflow", self.off, self.words)
        if len(shape) == 2:
            return a.rearrange("p (a b) -> p a b", a=shape[0])
        if len(shape) == 3:
            return a.rearrange("p (a b c) -> p a b c", a=shape[0], b=shape[1])
        return a


ARENA_WORDS = 53000


import os
ATT_STOP = int(os.environ.get('ATT_STOP', '9'))


def build_program(n_layers=L, stop_after=None, taps=False):
    nc = bass.Bass("TRN2", target_bir_lowering=False)
    dt_in = lambda name, shape, dt=F32: nc.dram_tensor(name, shape, dt, kind="ExternalInput").ap()
    xT = dt_in("xT", [D, T])
    LW = n_layers
    w_in = dt_in("w_in", [LW, IN_TILES, P, 16 * 128])
    w_out = dt_in("w_out", [LW, 16, P, 16 * 128])
    w_up = dt_in("w_up", [LW, 88, P, 16 * 128])
    w_dn = dt_in("w_dn", [LW, 16, P, NHT * 128])
    ptab_d = dt_in("ptab", [P, PT_COLS])
    relT = dt_in("relT", [8, P, 1024])
    cfar = dt_in("cfar", [8, T])
    c_ident = dt_in("c_ident", [P, P])
    c_tri = dt_in("c_tri", [P, P])
    c_caus = dt_in("c_caus", [P, P])
    c_oh = dt_in("c_oh", [16, 16 * 128])
    c_lmat = dt_in("c_lmat", [9, 16 * 128])
    yT = nc.dram_tensor("yT", [D, T], F32, kind="ExternalOutput").ap()
    tk = "ExternalOutput" if taps else "Internal"
    XA = nc.dram_tensor("XA", [D, T], F32, kind=tk).ap()
    XB = nc.dram_tensor("XB", [D, T], F32, kind="Internal").ap()
    MIX = nc.dram_tensor("MIX", [D, T], BF, kind=tk).ap()
    G = nc.dram_tensor("G", [HID, T], BF, kind=tk).ap()
    DBG = nc.dram_tensor("DBG", [P, 4096], F32, kind=tk).ap()
    WDB = nc.dram_tensor("WDB", [16, P, NHT * 128], BF, kind="Internal").ap()
    WOB = nc.dram_tensor("WOB", [16, P, 16 * 128], BF, kind="Internal").ap()
    tapd = {}

    es = ExitStack()
    arena_t = es.enter_context(nc.sbuf_tensor("arena", [P, ARENA_WORDS], F32))
    A = Arena(arena_t, ARENA_WORDS)
    psum = [es.enter_context(nc.psum_tensor("ps%d" % i, [P, 512], F32)) for i in range(7)]
    psbf = es.enter_context(nc.psum_tensor("psbf", [P, 1024], BF))
    t = Trk(nc)
    PSB = [Buf("ps%d" % i, excl=True) for i in range(8)]

    hT = A.bf(16, T)
    hTb = [Buf("hT%d" % i) for i in range(NTT)]
    ptab = A.f32(PT_COLS)
    Bpt = Buf("ptab")
    ident = A.bf(P)
    ones_bf = A.bf(P)
    tri = A.f32(P)
    caus = A.f32(P)
    ones_f = A.f32(P)
    Th_p = A.bf(1024)
    R_p = A.bf(T)
    BTh_p = Buf("Th")
    BR_p = Buf("R")
    lmat = A.bf(16, P)
    Bc = Buf("consts")
    NWS = 4
    wslot = [A.bf(16, P) for _ in range(NWS)]
    Bws = [Buf("ws%d" % i) for i in range(NWS)]
    wctr = [0]
    PHASE0 = A.off

    def pcol(name, idx):
        o = PT_OFF[name] + idx
        return ptab[:, o:o + 1]

    t.dma("sp", ptab, ptab_d, w=[Bpt])
    t.dma("pool", ident, c_ident, w=[Bc])
    t.dma("sp", tri, c_tri, w=[Bc], owner=Bpt)
    t.dma("sp", caus, c_caus, w=[Bc], owner=Bpt)
    t.dma("pool", lmat[0:9, :, :], c_lmat.rearrange("r (a b) -> r a b", a=16), w=[Bc])
    t.op("dve", lambda e: e.memset(ones_bf, 1.0), w=[Bc])
    t.op("dve", lambda e: e.memset(ones_f, 1.0), w=[Bc])
    t.barrier()

    def load_w(dram_tile):
        i = wctr[0] % NWS
        wctr[0] += 1
        t.dma("pool", wslot[i], dram_tile.rearrange("p (k n) -> p k n", k=16), w=[Bws[i]])
        return wslot[i], Bws[i]

    def load_w_bf(dram_tile, rbuf):
        i = wctr[0] % NWS
        wctr[0] += 1
        t.dma("pool", wslot[i], dram_tile.rearrange("p (k n) -> p k n", k=16), r=[rbuf], w=[Bws[i]], owner=Bws[i])
        return wslot[i], Bws[i]

    def xview(X):
        return X.rearrange("(f p) t -> p f t", p=P)

    def stage_n1(X, gname, l):
        A.off = PHASE0
        xt = A.f32(16, TT)
        Bxt = [Buf() for _ in range(4)]
        sq = [A.bf(TT) for _ in range(2)]
        Bsq = [Buf(), Buf()]
        rstd = A.f32(TT)
        Brs = Buf()
        for tt in range(NTT):
            ts = slice(tt * TT, (tt + 1) * TT)
            for q in range(4):
                t.dma("sp", xt[:, 4 * q:4 * q + 4, :], xview(X)[:, 4 * q:4 * q + 4, ts], w=[Bxt[q]])
            for ft in range(16):
                b = ft % 2
                t.op("act", lambda e, ft=ft, b=b: e.activation(out=sq[b], in_=xt[:, ft, :], func=AF.Square),
                     r=[Bxt[ft // 4]], w=[Bsq[b]])
                t.op("pe", lambda e, ft=ft, b=b: e.matmul(psum[6][:, :], lhsT=ones_bf, rhs=sq[b], start=(ft == 0), stop=(ft == 15)),
                     r=[Bsq[b], Bc], w=[PSB[6]])
            t.op("act", lambda e: e.activation(out=rstd, in_=psum[6][:, :], func=AF.Sqrt, bias=EPS, scale=1.0 / D),
                 r=[PSB[6]], w=[Brs])
            t.op("dve", lambda e: e.reciprocal(out=rstd, in_=rstd), r=[Brs], w=[Brs])
            for ft in range(16):
                t.op("dve", lambda e, ft=ft, ts=ts: e.scalar_tensor_tensor(
                    out=hT[:, ft, ts], in0=xt[:, ft, :], scalar=pcol(gname, l * 16 + ft), in1=rstd,
                    op0=OP.mult, op1=OP.mult), r=[Bxt[ft // 4], Brs, Bpt], w=[hTb[tt]])
        t.barrier(pool=False)

    def proj_tile(wt, Bw, evac, kts=16):
        for tt in range(NTT):
            b = tt % 2
            ts = slice(tt * TT, (tt + 1) * TT)
            for kt in range(kts):
                t.op("pe", lambda e, kt=kt, b=b, ts=ts: e.matmul(psum[b][:, :], lhsT=wt[:, kt, :], rhs=hT[:, kt, ts],
                                                             start=(kt == 0), stop=(kt == kts - 1)),
                     r=[Bw, hTb[tt]], w=[PSB[b]])
            evac(tt, ts, psum[b], PSB[b])

    def stage_attn(l, h):
        A.off = PHASE0
        qT = A.bf(T); qTf = A.f32(T); kT = A.bf(T); vT = A.bf(T)
        Bq, Bqf, Bk, Bv = Buf(), Buf(), Buf(), Buf()
        Vtok = A.bf(16, P); BV = Buf()
        PTs = [A.bf(TT) for _ in range(2)]; BPT = [Buf(), Buf()]
        R = R_p; BR = BR_p
        Th = Th_p; BTh = BTh_p
        rec = A.f32(TT); Brec = Buf()
        ob = [A.bf(TT) for _ in range(2)]; Bob = [Buf(), Buf()]
        ksum = A.f32(8); kmf = A.f32(8); Bks = Buf()
        gbuf = A.f32(8); mx = A.f32(8); Bg = Buf()
        nm = A.bf(16, 8); Bnm = Buf()
        sel = A.f32(8); Bsel = Buf()

        if not os.environ.get("NO_TH"):
            t.dma("pool", Th, relT[h], w=[BTh])
        if not os.environ.get("NO_R8"):
            t.dma("pool", R[8:9, :], cfar[h:h + 1, :], w=[BR])
        wq, Bwq = load_w(w_in[l, h])
        wk, Bwk = load_w(w_in[l, 8 + h])
        wv, Bwv = load_w(w_in[l, 16 + h])
        sc = float(128 ** -0.5)

        def ev_q(tt, ts, ps, pb):
            t.op("act", lambda e: e.activation(out=qT[:, ts], in_=ps[:, :], func=AF.Copy, scale=sc), r=[pb], w=[Bq])
            if not os.environ.get("NO_QF"):
                t.op("dve", lambda e: e.tensor_scalar(out=qTf[:, ts], in0=ps[:, :], scalar1=sc, scalar2=0.0, op0=OP.mult, op1=OP.add),
                     r=[pb], w=[Bqf])

        def ev_k(tt, ts, ps, pb):
            t.op("act", lambda e: e.activation(out=kT[:, ts], in_=ps[:, :], func=AF.Copy), r=[pb], w=[Bk])
            if not os.environ.get("NO_KS"):
              t.op("dve", lambda e: e.reduce_sum(out=ksum[:, 2 * tt:2 * tt + 2], in_=ps[:, :].rearrange("p (a b) -> p a b", a=2),
                                              axis=AX.X), r=[pb], w=[Bks])

        def ev_v(tt, ts, ps, pb):
            t.op("act", lambda e: e.activation(out=vT[:, ts], in_=ps[:, :], func=AF.Copy), r=[pb], w=[Bv])

        proj_tile(wq, Bwq, ev_q)
        proj_tile(wk, Bwk, ev_k)
        proj_tile(wv, Bwv, ev_v)
        if ATT_STOP <= 1:
            t.barrier(); return
        for rnd in range(2):
            for i8 in range(8):
                i = rnd * 8 + i8
                sl = slice(i8 * P, (i8 + 1) * P)
                t.op("pe", lambda e, i=i, sl=sl: e.transpose(out=psbf[:, sl], in_=vT[:, i * P:(i + 1) * P], identity=ident),
                     r=[Bv, Bc], w=[PSB[7]])
            t.op("act", lambda e, rnd=rnd: e.activation(out=Vtok[:, 8 * rnd:8 * rnd + 8, :],
                                                        in_=psbf[:, :].rearrange("p (a b) -> p a b", a=8), func=AF.Copy),
                 r=[PSB[7]], w=[BV])
        if ATT_STOP <= 2:
            t.barrier(); return
        t.op("dve", lambda e: e.tensor_scalar(out=kmf, in0=ksum, scalar1=1.0 / 256, scalar2=0.0, op0=OP.mult, op1=OP.add), r=[Bks], w=[Bks])
        t.op("dve", lambda e: e.memset(nm, 0.0), w=[Bnm])
        for jq in range(8, 16):
            own = jq // 2
            t.op("pe", lambda e, jq=jq: e.matmul(psum[6][:, 0:8], lhsT=qTf[:, jq * P:(jq + 1) * P], rhs=kmf, start=True, stop=True),
                 r=[Bqf, Bks], w=[PSB[6]])
            t.op("dve", lambda e: e.memset(gbuf, NEG), w=[Bg])
            t.op("dve", lambda e, own=own: e.tensor_copy(out=gbuf[:, 0:own], in_=psum[6][:, 0:own]), r=[PSB[6]], w=[Bg])
            t.op("dve", lambda e: e.max(out=mx, in_=gbuf), r=[Bg], w=[Bg])
            t.op("dve", lambda e, own=own: e.tensor_scalar(out=sel[:, 0:own], in0=gbuf[:, 0:own], scalar1=mx[:, 2:3], scalar2=1.0,
                                                          op0=OP.is_ge, op1=OP.subtract), r=[Bg], w=[Bsel])
            t.op("dve", lambda e, own=own, jq=jq: e.tensor_scalar(out=nm[:, jq, 0:own], in0=sel[:, 0:own], scalar1=-NEG, scalar2=None,
                                                                 op0=OP.mult), r=[Bsel], w=[Bnm])
        if ATT_STOP <= 3:
            t.barrier(); return
        for tt in range(NTT):
            for k in range(4):
                jq = tt * 4 + k
                t.op("pe", lambda e, jq=jq, k=k: e.matmul(psum[6][0:8, k * P:(k + 1) * P], lhsT=nm[:, jq, :], rhs=ident,
                                                         start=True, stop=True), r=[Bnm, Bc], w=[PSB[6]])
            t.op("act", lambda e, tt=tt: e.activation(out=R[0:8, tt * TT:(tt + 1) * TT], in_=psum[6][0:8, :], func=AF.Copy),
                 r=[PSB[6]], w=[BR])
        if ATT_STOP <= 4:
            t.barrier(); return
        for j in range(NTT):
            qs = slice(j * TT, (j + 1) * TT)
            ntile = 4 * j + 4

            def scores(i, j=j, qs=qs):
                b = 2 + (i % 2)
                n = i // 2
                d = 512 * j - 128 * i
                near = d <= 128
                lm = lmat[0:9, 2 * n + (0 if near else 1), :]
                t.op("pe", lambda e: e.matmul(psum[b][:, :], lhsT=kT[:, i * P:(i + 1) * P], rhs=qT[:, qs], start=True, stop=False),
                     r=[Bk, Bq], w=[PSB[b]])
                t.op("pe", lambda e: e.matmul(psum[b][:, :], lhsT=lm, rhs=R[0:9, qs], start=False, stop=(not near)),
                     r=[BR, Bc], w=[PSB[b]])
                if near:
                    c0 = d + 384
                    t.op("pe", lambda e: e.matmul(psum[b][:, :], lhsT=ident, rhs=Th[:, c0:c0 + TT], start=False, stop=True),
                         r=[BTh, Bc], w=[PSB[b]])
                t.op("act", lambda e: e.activation(out=PTs[i % 2], in_=psum[b][:, :], func=AF.Exp), r=[PSB[b]], w=[BPT[i % 2]])

            def pv(i):
                t.op("pe", lambda e: e.matmul(psum[4][:, :], lhsT=Vtok[:, i, :], rhs=PTs[i % 2], start=(i == 0), stop=(i == ntile - 1)),
                     r=[BV, BPT[i % 2]], w=[PSB[4]])
                t.op("pe", lambda e: e.matmul(psum[5][:, :], lhsT=ones_bf, rhs=PTs[i % 2], start=(i == 0), stop=(i == ntile - 1)),
                     r=[Bc, BPT[i % 2]], w=[PSB[5]])

            for i in range(ntile + 1):
                if i < ntile:
                    scores(i)
                if i > 0:
                    pv(i - 1)
            t.op("dve", lambda e: e.reciprocal(out=rec, in_=psum[5][:, :]), r=[PSB[5]], w=[Brec])
            t.op("dve", lambda e, j=j: e.tensor_tensor(out=ob[j % 2], in0=psum[4][:, :], in1=rec, op=OP.mult),
                 r=[PSB[4], Brec], w=[Bob[j % 2]])
            t.dma("sp", MIX[h * P:(h + 1) * P, qs], ob[j % 2], r=[Bob[j % 2]], owner=Bob[j % 2])
        t.barrier(pool=False)

    def stage_ssm(l):
        A.off = PHASE0
        t1 = A.f32(256); dtT = A.f32(256); dAT = A.f32(256); acsT = A.f32(256); nacsT = A.f32(256)
        lastB = A.f32(256); cdB = A.f32(256); ds = A.f32(256); ea = A.f32(16)
        Bt1, Bdt, BdA, Bacs, Bnacs, Blast, Bcd, Bds, Bea = [Buf() for _ in range(9)]
        acsF = A.f32(T); BacsF = Buf()
        xTg = A.bf(4, T); BxT = Buf()
        BTg = A.bf(T); CTg = A.bf(T); BBT, BCT = Buf(), Buf()
        zsg = A.bf(4, T); Bzs = Buf()
        HT = A.f32(512); HTb = A.bf(512); BH, BHb = Buf(), Buf()
        xc = [A.bf(512) for _ in range(2)]; Bxc = [Buf(), Buf()]
        xcd = [A.bf(512) for _ in range(2)]; Bxcd = [Buf(), Buf()]
        Btok = [A.bf(P) for _ in range(2)]; BBtok = [Buf(), Buf()]
        CBm = [A.f32(P) for _ in range(2)]; BCBm = [Buf(), Buf()]
        EAc = [A.f32(4, P) for _ in range(2)]; BEA = [Buf(), Buf()]
        Ebuf = [A.f32(8, P) for _ in range(2)]; BE = [Buf(), Buf()]
        sq = A.bf(4, P); Bsq = Buf()
        rstd = A.f32(P); Brs = Buf()
        ob4 = [A.bf(4, TT) for _ in range(2)]; Bob4 = [Buf(), Buf()]
        ohx = A.f32(8, P); Boh = Buf()
        t.dma("sp", ohx[0:16, :, :], c_oh.rearrange("r (a b) -> r a b", a=16)[:, 0:8, :], w=[Boh])
        OVL = A.off
        ubuf = A.f32(T + 4); Bu = Buf()
        acc = A.f32(T); Bacc = Buf()
        A.off = OVL
        Zb = A.f32(8, P); BZ = Buf()
        Wb = A.bf(8, P); BW = Buf()
        t0 = A.f32(4, P); Bt0 = Buf()
        xD = A.f32(4, P); BxD = Buf()
        A.off = max(A.off, OVL + 2 * T + 4)
        U = caus
        PL = os.environ.get("SSM_POOL", "pool")

        wdt, Bwdt = load_w(w_in[l, 44])
        for c in range(16):
            for kt in range(16):
                t.op("pe", lambda e, c=c, kt=kt: e.matmul(psum[6][:, c * 16:(c + 1) * 16], lhsT=hT[:, kt, c * P:(c + 1) * P],
                                                         rhs=wdt[:, kt, 0:16], start=(kt == 0), stop=(kt == 15)),
                     r=[Bwdt, hTb[c // 4]], w=[PSB[6]])
            t.op("dve", lambda e, c=c: e.tensor_tensor(out=t1[:, c * 16:(c + 1) * 16], in0=psum[6][:, c * 16:(c + 1) * 16],
                                                      in1=ptab[:, PT_OFF["dtb"] + l * 16:PT_OFF["dtb"] + l * 16 + 16], op=OP.add),
                 r=[PSB[6], Bpt], w=[Bt1])
        t.op("act", lambda e: e.activation(out=t1, in_=t1, func=AF.Exp), r=[Bt1], w=[Bt1])
        t.op("act", lambda e: e.activation(out=dtT, in_=t1, func=AF.Ln, bias=1.0), r=[Bt1], w=[Bdt])
        t.op("act", lambda e: e.activation(out=ea, in_=ptab[:, PT_OFF["alog"] + l * 16:PT_OFF["alog"] + l * 16 + 16], func=AF.Exp),
             r=[Bpt], w=[Bea])
        for c in range(16):
            t.op("dve", lambda e, c=c: e.scalar_tensor_tensor(out=dAT[:, c * 16:(c + 1) * 16], in0=dtT[:, c * 16:(c + 1) * 16],
                                                             scalar=-1.0, in1=ea, op0=OP.mult, op1=OP.mult),
                 r=[Bdt, Bea], w=[BdA])
        for c in range(16):
            t.op("pe", lambda e, c=c: e.matmul(psum[2][:, c * 16:(c + 1) * 16], lhsT=tri, rhs=dAT[:, c * 16:(c + 1) * 16],
                                              start=True, stop=True), r=[BdA, Bc], w=[PSB[2]])
            t.op("pe", lambda e, c=c: e.matmul(psum[3][:, c * 16:(c + 1) * 16], lhsT=ones_f, rhs=dAT[:, c * 16:(c + 1) * 16],
                                              start=True, stop=True), r=[BdA, Bc], w=[PSB[3]])
        t.op("dve", lambda e: e.tensor_copy(out=acsT, in_=psum[2][:, 0:256]), r=[PSB[2]], w=[Bacs])
        t.op("dve", lambda e: e.tensor_scalar(out=nacsT, in0=psum[2][:, 0:256], scalar1=-1.0, scalar2=0.0, op0=OP.mult, op1=OP.add),
             r=[PSB[2]], w=[Bnacs])
        t.op("dve", lambda e: e.tensor_copy(out=lastB, in_=psum[3][:, 0:256]), r=[PSB[3]], w=[Blast])
        t.op("act", lambda e: e.activation(out=cdB, in_=psum[3][:, 0:256], func=AF.Exp), r=[PSB[3]], w=[Bcd])
        t.op("dve", lambda e: e.tensor_tensor(out=ds, in0=lastB, in1=acsT, op=OP.subtract), r=[Blast, Bacs], w=[Bds])
        t.op("act", lambda e: e.activation(out=ds, in_=ds, func=AF.Exp), r=[Bds], w=[Bds])
        for tt in range(NTT):
            for k in range(4):
                c = tt * 4 + k
                t.op("pe", lambda e, c=c, k=k: e.matmul(psum[4][0:16, k * P:(k + 1) * P], lhsT=dAT[:, c * 16:(c + 1) * 16], rhs=tri,
                                                       start=True, stop=True), r=[BdA, Bc], w=[PSB[4]])
            t.op("dve", lambda e, tt=tt: e.tensor_copy(out=acsF[0:16, tt * TT:(tt + 1) * TT], in_=psum[4][0:16, :]),
                 r=[PSB[4]], w=[BacsF])

        GORD = (1, 0) if os.environ.get('SSM_SWAP') else (0, 1)
        for g in GORD:
            t.op("dve", lambda e: e.memset(ubuf[:, 0:3], 0.0), w=[Bu])

            def conv_tile(ctile, out_ap, Bout):
                wt, Bw = load_w(w_in[l, 32 + ctile])

                def ev(tt, ts, ps, pb):
                    t.op("act", lambda e: e.activation(out=ubuf[:, 3 + tt * TT:3 + (tt + 1) * TT], in_=ps[:, :], func=AF.Copy),
                         r=[pb], w=[Bu])
                proj_tile(wt, Bw, ev)
                cw = PT_OFF["sconv_w"] + (l * 12 + ctile) * 4
                t.op("dve", lambda e: e.tensor_scalar(out=acc, in0=ubuf[:, 3:3 + T], scalar1=ptab[:, cw + 3:cw + 4], scalar2=0.0,
                                                      op0=OP.mult, op1=OP.add), r=[Bu, Bpt], w=[Bacc])
                for k in range(3):
                    t.op("dve", lambda e, k=k: e.scalar_tensor_tensor(out=acc, in0=ubuf[:, k:k + T], scalar=ptab[:, cw + k:cw + k + 1],
                                                                     in1=acc, op0=OP.mult, op1=OP.add), r=[Bu, Bpt, Bacc], w=[Bacc])
                t.op("act", lambda e: e.activation(out=out_ap, in_=acc, func=AF.Silu, bias=pcol("sconv_b", l * 12 + ctile)),
                     r=[Bacc, Bpt], w=[Bout])

            for ct in range(4):
                conv_tile(4 * g + ct, xTg[:, ct, :], BxT)
            conv_tile(8 + g, BTg, BBT)
            conv_tile(10 + g, CTg, BCT)
            for ct in range(4):
                wt, Bw = load_w(w_in[l, 24 + 4 * g + ct])

                def evz(tt, ts, ps, pb, ct=ct):
                    t.op("act", lambda e: e.activation(out=zsg[:, ct, ts], in_=ps[:, :], func=AF.Silu), r=[pb], w=[Bzs])
                proj_tile(wt, Bw, evz)
            t.barrier()
            t.op("dve", lambda e: e.memset(HT, 0.0), w=[BH])
            t.op("dve", lambda e: e.memset(HTb, 0.0), w=[BHb])

            def front(c):
                cs = slice(c * P, (c + 1) * P)
                cb = c % 2
                hs = slice(c * 16 + 8 * g, c * 16 + 8 * g + 8)
                for ct in range(4):
                    t.op("pe", lambda e, ct=ct: e.transpose(out=psbf[:, ct * P:(ct + 1) * P], in_=xTg[:, ct, cs], identity=ident),
                         r=[BxT, Bc], w=[PSB[7]])
                t.op("pe", lambda e: e.transpose(out=psbf[:, 512:640], in_=BTg[:, cs], identity=ident), r=[BBT, Bc], w=[PSB[7]])
                t.op("dve", lambda e: e.tensor_tensor(
                    out=xc[cb].rearrange("p (h d) -> p h d", h=8), in0=psbf[:, 0:512].rearrange("p (h d) -> p h d", h=8),
                    in1=dtT[:, hs].unsqueeze(2).to_broadcast([P, 8, 64]), op=OP.mult), r=[PSB[7], Bdt], w=[Bxc[cb]])
                t.op("act", lambda e: e.activation(out=Btok[cb], in_=psbf[:, 512:640], func=AF.Copy), r=[PSB[7]], w=[BBtok[cb]])
                t.op(PL, lambda e: e.tensor_tensor(
                    out=xcd[cb].rearrange("p (h d) -> p h d", h=8), in0=xc[cb].rearrange("p (h d) -> p h d", h=8),
                    in1=ds[:, hs].unsqueeze(2).to_broadcast([P, 8, 64]), op=OP.mult), r=[Bxc[cb], Bds], w=[Bxcd[cb]])
                t.op("pe", lambda e: e.matmul(psum[6][:, 0:P], lhsT=BTg[:, cs], rhs=CTg[:, cs], start=True, stop=True),
                     r=[BBT, BCT], w=[PSB[6]])
                t.op("dve", lambda e: e.tensor_tensor(out=CBm[cb], in0=psum[6][:, 0:P], in1=tri, op=OP.mult),
                     r=[PSB[6], Bc], w=[BCBm[cb]])
                t.op("dve", lambda e: e.tensor_tensor(out=Zb, in0=tri.unsqueeze(1).to_broadcast([P, 8, P]),
                                                      in1=dAT[:, hs].unsqueeze(2).to_broadcast([P, 8, P]), op=OP.mult),
                     r=[Bc, BdA], w=[BZ])
                for hb in range(2):
                    t.op("pe", lambda e, hb=hb: e.matmul(psum[hb][:, :], lhsT=U, rhs=Zb[:, 4 * hb:4 * hb + 4, :], start=True, stop=True),
                         r=[BZ, Bc], w=[PSB[hb]])
                    t.op("act", lambda e, hb=hb: e.activation(out=Ebuf[cb][:, 4 * hb:4 * hb + 4, :], in_=psum[hb][:, :].rearrange("p (a b) -> p a b", a=4), func=AF.Exp),
                         r=[PSB[hb]], w=[BE[cb]])
                for ct in range(4):
                    t.op("pe", lambda e, ct=ct, g=g: e.matmul(psum[5][:, ct * P:(ct + 1) * P], lhsT=ohx[0:16, 4 * g + ct, :], rhs=acsF[0:16, cs],
                                                       start=True, stop=True), r=[BacsF, Boh], w=[PSB[5]])
                t.op("act", lambda e: e.activation(out=EAc[cb], in_=psum[5][:, :].rearrange("p (a b) -> p a b", a=4), func=AF.Exp), r=[PSB[5]], w=[BEA[cb]])

            def back(c):
                cs = slice(c * P, (c + 1) * P)
                cb = c % 2
                hs = slice(c * 16 + 8 * g, c * 16 + 8 * g + 8)
                t.op(PL, lambda e: e.tensor_tensor(out=Wb, in0=Ebuf[cb], in1=CBm[cb].unsqueeze(1).to_broadcast([P, 8, P]), op=OP.mult),
                     r=[BE[cb], BCBm[cb]], w=[BW])
                for ct in range(4):
                    t.op("pe", lambda e, ct=ct: e.matmul(psum[4][:, ct * P:(ct + 1) * P], lhsT=HTb[:, ct * P:(ct + 1) * P], rhs=CTg[:, cs],
                                                       start=True, stop=True), r=[BHb, BCT], w=[PSB[4]])
                for k in range(8):
                    ct = k // 2
                    t.op("pe", lambda e, k=k, ct=ct: e.matmul(psum[2 + k % 2][:, ct * P:(ct + 1) * P], lhsT=xc[cb][:, ct * P:(ct + 1) * P],
                                                             rhs=Wb[:, k, :], start=True, stop=True), r=[Bxc[cb], BW], w=[PSB[2 + k % 2]])
                dbg = taps and c == 1 and g == GORD[0]
                if dbg:
                    t.dma("sp", DBG[:, 0:512], EAc[cb].rearrange("p a b -> p (a b)"), r=[BEA[cb]], owner=BEA[cb])
                    t.dma("sp", DBG[:, 512:1024], HT, r=[BH], owner=BH)
                t.op("dve", lambda e: e.tensor_tensor(out=t0, in0=psum[4][:, :].rearrange("p (a b) -> p a b", a=4), in1=EAc[cb], op=OP.mult),
                     r=[PSB[4], BEA[cb]], w=[Bt0])
                if dbg:
                    t.dma("sp", DBG[:, 1024:1536], t0.rearrange("p a b -> p (a b)"), r=[Bt0], owner=Bt0)
                for half in range(2):
                    pr = slice(half * 64, half * 64 + 64)
                    t.op("dve", lambda e, half=half, pr=pr: e.tensor_tensor(
                        out=t0[pr], in0=t0[pr], in1=psum[2 + half][pr, :].rearrange("p (a b) -> p a b", a=4), op=OP.add),
                        r=[Bt0, PSB[2 + half]], w=[Bt0])
                dc0 = PT_OFF["dexp"] + l * 8 + 4 * g
                t.op(PL, lambda e: e.tensor_tensor(out=xD, in0=xTg[:, :, cs], in1=ptab[:, dc0:dc0 + 4].unsqueeze(2).to_broadcast([P, 4, P]),
                                                       op=OP.mult), r=[BxT, Bpt], w=[BxD])
                t.op("dve", lambda e: e.tensor_tensor(out=t0, in0=t0, in1=xD, op=OP.add), r=[Bt0, BxD], w=[Bt0])
                t.op("dve", lambda e: e.tensor_tensor(out=t0, in0=t0, in1=zsg[:, :, cs], op=OP.mult), r=[Bt0, Bzs], w=[Bt0])
                t.op("act", lambda e: e.activation(out=sq, in_=t0, func=AF.Square), r=[Bt0], w=[Bsq])
                for ct in range(4):
                    t.op("pe", lambda e, ct=ct: e.matmul(psum[6][:, P:2 * P], lhsT=ones_bf, rhs=sq[:, ct, :], start=(ct == 0), stop=(ct == 3)),
                         r=[Bsq, Bc], w=[PSB[6]])
                t.op("act", lambda e: e.activation(out=rstd, in_=psum[6][:, P:2 * P], func=AF.Sqrt, bias=EPS, scale=1.0 / 512),
                     r=[PSB[6]], w=[Brs])
                t.op("dve", lambda e: e.reciprocal(out=rstd, in_=rstd), r=[Brs], w=[Brs])
                o4 = (c // 4) % 2
                for ct in range(4):
                    ncol = PT_OFF["snw"] + l * 8 + 4 * g + ct
                    t.op("dve", lambda e, ct=ct, ncol=ncol: e.scalar_tensor_tensor(
                        out=ob4[o4][:, ct, (c % 4) * P:(c % 4 + 1) * P], in0=t0[:, ct, :], scalar=ptab[:, ncol:ncol + 1], in1=rstd,
                        op0=OP.mult, op1=OP.mult), r=[Bt0, Bpt, Brs], w=[Bob4[o4]])
                if c % 4 == 3:
                    f0 = 1024 + 512 * g
                    t.dma("sp", MIX[f0:f0 + 512, (c // 4) * TT:(c // 4 + 1) * TT].rearrange("(a p) t -> p a t", p=P), ob4[o4],
                          r=[Bob4[o4]], owner=Bob4[o4])
                t.op("pe", lambda e: e.matmul(psum[4][:, :], lhsT=Btok[cb], rhs=xcd[cb], start=True, stop=True),
                     r=[BBtok[cb], Bxcd[cb]], w=[PSB[4]])
                t.op("dve", lambda e: e.tensor_tensor(out=HT.rearrange("p (h d) -> p h d", h=8), in0=HT.rearrange("p (h d) -> p h d", h=8),
                                                      in1=cdB[:, hs].unsqueeze(2).to_broadcast([P, 8, 64]), op=OP.mult),
                     r=[BH, Bcd], w=[BH])
                t.op("dve", lambda e: e.tensor_tensor(out=HT, in0=HT, in1=psum[4][:, :], op=OP.add), r=[BH, PSB[4]], w=[BH])
                t.op("act", lambda e: e.activation(out=HTb, in_=HT, func=AF.Copy), r=[BH], w=[BHb])

            if not os.environ.get("SSM_NOPIPE"):
                front(0)
                for c in range(16):
                    if c + 1 < 16:
                        front(c + 1)
                    back(c)
            else:
                for c in range(16):
                    front(c)
                    back(c)
            t.barrier()
        t.barrier()

    def epilogue_gen(m, Bm, psstat, Bpsstat, tq, Xin, Xout, gpost, l, gnext, lnext, bufs):
        xt, Bxt, sq, Bsq, rstd, Brs = bufs
        ts = slice(tq * TT, (tq + 1) * TT)
        t.op("act", lambda e: e.activation(out=rstd, in_=psstat, func=AF.Sqrt, bias=EPS, scale=1.0 / D), r=[Bpsstat], w=[Brs])
        t.op("dve", lambda e: e.reciprocal(out=rstd, in_=rstd), r=[Brs], w=[Brs])
        yield
        xts, Bxts = (xt, Bxt) if isinstance(xt, list) else ([xt], [Bxt])
        gsz = xts[0].shape[1]
        for q in range(16 // gsz):
            xt, Bxt = xts[q % len(xts)], Bxts[q % len(xts)]
            t.dma("sp", xt, xview(Xin)[:, gsz * q:gsz * q + gsz, ts], w=[Bxt])
            for f in range(gsz):
                ft = gsz * q + f
                t.op("dve", lambda e, ft=ft: e.scalar_tensor_tensor(out=m[:, ft, :], in0=m[:, ft, :], scalar=pcol(gpost, l * 16 + ft),
                                                                   in1=rstd, op0=OP.mult, op1=OP.mult), r=[Bm, Bpt, Brs], w=[Bm])
                t.op("dve", lambda e, ft=ft, f=f, xt=xt: e.tensor_tensor(out=m[:, ft, :], in0=m[:, ft, :], in1=xt[:, f, :], op=OP.add),
                     r=[Bm, Bxt], w=[Bm])
                yield
        for q in range(4):
            t.dma("sp", xview(Xout)[:, 4 * q:4 * q + 4, ts], m[:, 4 * q:4 * q + 4, :], r=[Bm], owner=Bm)
        yield
        if gnext is not None:
            for ft in range(16):
                b = ft % 2
                t.op("act", lambda e, ft=ft, b=b: e.activation(out=sq[b], in_=m[:, ft, :], func=AF.Square), r=[Bm], w=[Bsq[b]])
                t.op("pe", lambda e, ft=ft, b=b: e.matmul(psum[6][:, :], lhsT=ones_bf, rhs=sq[b], start=(ft == 0), stop=(ft == 15)),
                     r=[Bsq[b], Bc], w=[PSB[6]])
                yield
            t.op("act", lambda e: e.activation(out=rstd, in_=psum[6][:, :], func=AF.Sqrt, bias=EPS, scale=1.0 / D), r=[PSB[6]], w=[Brs])
            t.op("dve", lambda e: e.reciprocal(out=rstd, in_=rstd), r=[Brs], w=[Brs])
            yield
            for ft in range(16):
                t.op("dve", lambda e, ft=ft: e.scalar_tensor_tensor(out=hT[:, ft, ts], in0=m[:, ft, :], scalar=pcol(gnext, lnext * 16 + ft),
                                                                   in1=rstd, op0=OP.mult, op1=OP.mult), r=[Bm, Bpt, Brs], w=[hTb[tq]])
                yield

    def epilogue(*a):
        for _ in epilogue_gen(*a):
            pass

    def stage_o(l, Xin, Xout):
        A.off = PHASE0
        mixt = A.bf(16, TT); Bmx = [Buf() for _ in range(4)]
        m = [A.f32(16, TT) for _ in range(2)]; Bm = [Buf(), Buf()]
        xt = [A.f32(4, TT) for _ in range(2)]; Bxt = [Buf(), Buf()]
        sq = [A.bf(TT) for _ in range(2)]; Bsq = [Buf(), Buf()]
        rstd = A.f32(TT); Brs = Buf()
        sq2 = [A.bf(TT) for _ in range(2)]; Bsq2 = [Buf(), Buf()]
        pend = [None]
        BWO = [Buf() for _ in range(16)]

        def load_mix(tq):
            ts = slice(tq * TT, (tq + 1) * TT)
            for q in range(4):
                t.dma("sp", mixt[:, 4 * q:4 * q + 4, :], MIX.rearrange("(f p) t -> p f t", p=P)[:, 4 * q:4 * q + 4, ts], w=[Bmx[q]])
        load_mix(0)
        for tq in range(NTT):
            pb_ = tq % 2
            for ft in range(16):
                if tq == 0:
                    wt, Bw = load_w(w_out[l, ft])
                    t.dma("sp", WOB[ft].rearrange("p (k n) -> p k n", k=16), wt, r=[Bw], w=[BWO[ft]])
                else:
                    wt, Bw = load_w_bf(WOB[ft], BWO[ft])
                b = ft % 2
                for kt in range(16):
                    t.op("pe", lambda e, kt=kt, b=b, wt=wt: e.matmul(psum[b][:, :], lhsT=wt[:, kt, :], rhs=mixt[:, kt, :],
                                                                    start=(kt == 0), stop=(kt == 15)), r=[Bw, Bmx[kt // 4]], w=[PSB[b]])
                t.op("act", lambda e, ft=ft, b=b, pb_=pb_: e.activation(out=m[pb_][:, ft, :], in_=psum[b][:, :], func=AF.Copy), r=[PSB[b]], w=[Bm[pb_]])
                t.op("act", lambda e, ft=ft, b=b: e.activation(out=sq[b], in_=psum[b][:, :], func=AF.Square), r=[PSB[b]], w=[Bsq[b]])
                t.op("pe", lambda e, ft=ft, b=b, pb_=pb_: e.matmul(psum[4 + pb_][:, :], lhsT=ones_bf, rhs=sq[b], start=(ft == 0), stop=(ft == 15)),
                     r=[Bsq[b], Bc], w=[PSB[4 + pb_]])
                if pend[0] is not None:
                    for _ in range(4):
                        next(pend[0], None)
            if pend[0] is not None:
                for _ in pend[0]:
                    pass
            if tq + 1 < NTT:
                load_mix(tq + 1)
            pend[0] = epilogue_gen(m[pb_], Bm[pb_], psum[4 + pb_][:, :], PSB[4 + pb_], tq, Xin, Xout, "g_mix_post", l, "g_ffn_pre", l,
                                   (xt, Bxt, sq2, Bsq2, rstd, Brs))
        for _ in pend[0]:
            pass
        t.barrier()

    def stage_f1(l):
        A.off = PHASE0
        ug = A.f32(T + 2); uu = A.f32(T + 2); Bug, Buu = Buf(), Buf()
        ag = A.f32(T); au = A.f32(T); Bag, Bau = Buf(), Buf()
        gb = [A.bf(T) for _ in range(2)]; Bgb = [Buf(), Buf()]
        t.op("dve", lambda e: e.memset(ug[:, 0:2], 0.0), w=[Bug])
        t.op("dve", lambda e: e.memset(uu[:, 0:2], 0.0), w=[Buu])
        for j in range(NHT):
            for (tile, u, Bu_, a, Ba) in ((j, ug, Bug, ag, Bag), (NHT + j, uu, Buu, au, Bau)):
                wt, Bw = load_w(w_up[l, tile])

                def ev(tt, ts, ps, pb, u=u, Bu_=Bu_):
                    t.op("act", lambda e: e.activation(out=u[:, 2 + tt * TT:2 + (tt + 1) * TT], in_=ps[:, :], func=AF.Copy), r=[pb], w=[Bu_])
                proj_tile(wt, Bw, ev)
                cw = PT_OFF["fconv_w"] + (l * 88 + tile) * 3
                t.op("dve", lambda e, u=u, a=a, cw=cw, tile=tile: e.tensor_scalar(
                    out=a, in0=u[:, 2:2 + T], scalar1=ptab[:, cw + 2:cw + 3], scalar2=pcol("fconv_b", l * 88 + tile),
                    op0=OP.mult, op1=OP.add), r=[Bu_, Bpt], w=[Ba])
                for k in range(2):
                    t.op("dve", lambda e, u=u, a=a, cw=cw, k=k: e.scalar_tensor_tensor(
                        out=a, in0=u[:, k:k + T], scalar=ptab[:, cw + k:cw + k + 1], in1=a, op0=OP.mult, op1=OP.add),
                        r=[Bu_, Bpt, Ba], w=[Ba])
            t.op("act", lambda e: e.activation(out=ag, in_=ag, func=AF.Gelu_apprx_tanh), r=[Bag], w=[Bag])
            t.op("dve", lambda e, j=j: e.tensor_tensor(out=gb[j % 2], in0=ag, in1=au, op=OP.mult), r=[Bag, Bau], w=[Bgb[j % 2]])
            t.dma("sp", G[j * P:(j + 1) * P, :], gb[j % 2], r=[Bgb[j % 2]], owner=Bgb[j % 2])
        t.barrier()

    def stage_f2(l, Xin, Xout, gnext, lnext):
        A.off = PHASE0 - NWS * 1024
        gt = A.bf(NHT, TT); Bgt = [Buf() for _ in range(4)]
        m = A.f32(16, TT); Bm = Buf()
        xt = [A.f32(4, TT) for _ in range(2)]; Bxt = [Buf(), Buf()]
        sq = [A.bf(TT) for _ in range(2)]; Bsq = [Buf(), Buf()]
        rstd = A.f32(TT); Brs = Buf()
        sq2 = [A.bf(TT) for _ in range(2)]; Bsq2 = [Buf(), Buf()]
        wd = [A.bf(NHT, P) for _ in range(2)]; Bwd = [Buf(), Buf()]
        BWDB = [Buf() for _ in range(16)]
        def load_g(tq):
            ts = slice(tq * TT, (tq + 1) * TT)
            for q in range(4):
                t.dma("sp", gt[:, 11 * q:11 * q + 11, :], G.rearrange("(f p) t -> p f t", p=P)[:, 11 * q:11 * q + 11, ts], w=[Bgt[q]])
        load_g(0)
        for tq in range(NTT):
            for ft in range(16):
                b = ft % 2
                if tq == 0:
                    t.dma("pool", wd[b], w_dn[l, ft].rearrange("p (k n) -> p k n", k=NHT), w=[Bwd[b]])
                    t.dma("sp", WDB[ft].rearrange("p (k n) -> p k n", k=NHT), wd[b], r=[Bwd[b]], w=[BWDB[ft]])
                else:
                    t.dma("pool", wd[b], WDB[ft].rearrange("p (k n) -> p k n", k=NHT), r=[BWDB[ft]], w=[Bwd[b]], owner=Bwd[b])
                for kt in range(NHT):
                    t.op("pe", lambda e, kt=kt, b=b: e.matmul(psum[b][:, :], lhsT=wd[b][:, kt, :], rhs=gt[:, kt, :],
                                                             start=(kt == 0), stop=(kt == NHT - 1)), r=[Bwd[b], Bgt[kt // 11]], w=[PSB[b]])
                t.op("act", lambda e, ft=ft, b=b: e.activation(out=m[:, ft, :], in_=psum[b][:, :], func=AF.Copy), r=[PSB[b]], w=[Bm])
                t.op("act", lambda e, ft=ft, b=b: e.activation(out=sq2[b], in_=psum[b][:, :], func=AF.Square), r=[PSB[b]], w=[Bsq2[b]])
                t.op("pe", lambda e, ft=ft, b=b: e.matmul(psum[5][:, :], lhsT=ones_bf, rhs=sq2[b], start=(ft == 0), stop=(ft == 15)),
                     r=[Bsq2[b], Bc], w=[PSB[5]])
            if tq + 1 < NTT:
                load_g(tq + 1)
            epilogue(m, Bm, psum[5][:, :], PSB[5], tq, Xin, Xout, "g_ffn_post", l, gnext, lnext, (xt, Bxt, sq, Bsq, rstd, Brs))
        t.barrier()

    Xcur = xT
    stage_n1(xT, "g_mix_pre", 0)
    done = False
    for l in range(n_layers):
        if stop_after == "n1":
            break
        for h in range(int(os.environ.get('NHEADS', '8'))):
            stage_attn(l, h)
        if stop_after == "attn":
            break
        stage_ssm(l)
        if stop_after == "ssm":
            break
        stage_o(l, Xcur, XA)
        if stop_after == "o":
            break
        stage_f1(l)
        if stop_after == "f1":
            break
        last = (l == n_layers - 1)
        Xn = yT if last else XB
        stage_f2(l, XA, Xn, None if last else "g_mix_pre", l + 1)
        Xcur = Xn
    t.barrier()
    t.emit()
    es.close()
    return nc, {"MIX": MIX, "XA": XA, "G": G}


def _t5_bucket_np(rel):
    n = np.maximum(rel, 0)
    max_exact = 16
    nf = np.maximum(n, 1).astype(np.float32)
    large = max_exact + (np.log(nf / max_exact) / np.log(128 / max_exact) * (32 - max_exact)).astype(np.int32)
    large = np.minimum(large, 31)
    return np.where(n < max_exact, n, large)


def _tile_w(w, kt, ntiles):
    K, N = w.shape
    if N < ntiles * 128:
        wp = np.zeros((K, ntiles * 128), np.float32)
        wp[:, :N] = w
        w = wp
    return np.ascontiguousarray(w.reshape(kt, 128, ntiles, 128).transpose(2, 1, 0, 3)).reshape(ntiles, 128, kt * 128)


def prep_shared(inp):
    f = np.float32
    sh = {}
    sh["w_in"] = np.stack([_tile_w(np.asarray(inp["w_in"][l], f), 16, IN_TILES) for l in range(L)])
    sh["w_out"] = np.stack([_tile_w(np.asarray(inp["w_out"][l], f), 16, 16) for l in range(L)])
    sh["w_up"] = np.stack([_tile_w(np.asarray(inp["w_ffn_up"][l], f), 16, 88) for l in range(L)])
    sh["w_dn"] = np.stack([_tile_w(np.asarray(inp["w_ffn_down"][l], f), NHT, 16) for l in range(L)])
    pt = np.zeros((P, PT_COLS), f)

    def put(name, arr):
        n = arr.shape[0]
        pt[:, PT_OFF[name]:PT_OFF[name] + n] = arr.T

    for name, key in (("g_mix_pre", "ln_mix_pre"), ("g_mix_post", "ln_mix_post"), ("g_ffn_pre", "ln_ffn_pre"), ("g_ffn_post", "ln_ffn_post")):
        put(name, np.asarray(inp[key], f).reshape(L * 16, P))
    scw = np.asarray(inp["ssm_conv_w"], f)
    put("sconv_w", scw.reshape(L, 4, 12, P).transpose(0, 2, 1, 3).reshape(L * 12 * 4, P))
    put("sconv_b", np.asarray(inp["ssm_conv_b"], f).reshape(L * 12, P))
    dsk = np.asarray(inp["d_skip"], f)
    put("dexp", np.repeat(dsk, 64, axis=1).reshape(L * 8, P))
    put("snw", np.asarray(inp["ssm_norm_w"], f).reshape(L * 8, P))
    fcw = np.asarray(inp["ffn_conv_w"], f)
    put("fconv_w", fcw.reshape(L, 3, 88, P).transpose(0, 2, 1, 3).reshape(L * 88 * 3, P))
    put("fconv_b", np.asarray(inp["ffn_conv_b"], f).reshape(L * 88, P))
    pt[:, PT_OFF["dtb"]:PT_OFF["dtb"] + L * 16] = np.asarray(inp["dt_bias"], f).reshape(1, L * 16)
    pt[:, PT_OFF["alog"]:PT_OFF["alog"] + L * 16] = np.asarray(inp["a_log"], f).reshape(1, L * 16)
    sh["ptab"] = pt
    rb = np.asarray(inp["rel_bias"], f)
    kp = np.arange(P)[:, None]
    cc = np.arange(1024)[None, :]
    rel = cc - 384 - kp
    idx = _t5_bucket_np(rel)
    tab = rb[idx, :]
    tab = np.where((rel >= 0)[:, :, None], tab, f(NEG))
    sh["relT"] = np.ascontiguousarray(tab.transpose(2, 0, 1)).astype(f)
    sh["cfar"] = np.ascontiguousarray(np.repeat(rb[31, :][:, None], T, axis=1)).astype(f)
    sh["c_ident"] = np.eye(P, dtype=f)
    ti = np.arange(P)
    sh["c_tri"] = (ti[:, None] <= ti[None, :]).astype(f)
    sh["c_caus"] = (ti[:, None] > ti[None, :]).astype(f)
    ohm = np.zeros((16, 16, P), f)
    for j in range(8):
        for m_ in range(P):
            ohm[2 * j + m_ // 64, j, m_] = 1.0
    sh["c_oh"] = ohm.reshape(16, 16 * P)
    lm = np.zeros((9, 16, P), f)
    for n in range(8):
        for far in range(2):
            lm[n, 2 * n + far, :] = 1.0
            lm[8, 2 * n + far, :] = float(far)
    sh["c_lmat"] = lm.reshape(9, 16 * P)
    return sh


_CACHE = {}


def kernel(**inputs):
    x = np.asarray(inputs["x"], np.float32)
    sh = prep_shared(inputs)
    if "nc" not in _CACHE:
        _CACHE["nc"] = build_program()[0]
    nc = _CACHE["nc"]
    in_maps = []
    for c in range(8):
        d = dict(sh)
        d["xT"] = np.ascontiguousarray(x[c % 4].T)
        in_maps.append(d)
    res = run_bass_kernel_spmd(nc, in_maps, core_ids=list(range(8)))
    out = np.stack([np.ascontiguousarray(res.results[b]["yT"].T) for b in range(4)], axis=0)
    return out.astype(np.float32)
```

```python
import os
import numpy as np
from contextlib import ExitStack
import concourse.bass as bass
import concourse.mybir as mybir
from concourse.bass_utils import run_bass_kernel_spmd

F32 = mybir.dt.float32
BF = mybir.dt.bfloat16
AF = mybir.ActivationFunctionType
OP = mybir.AluOpType
AX = mybir.AxisListType

P = 128
T = 2048
D = 2048
L = 4
NFT = 16
TT = 512
NTT = 4
HID = 5632
NHT = 44
IN_COLS = 5648
IN_TILES = 45
EPS = 1e-6
NEG = -1e30


class Buf:
    __slots__ = ("name", "writers", "readers", "sem", "semcnt", "excl")

    def __init__(self, name="", excl=False):
        self.name = name
        self.excl = excl
        self.writers = []
        self.readers = []
        self.sem = None
        self.semcnt = 0


class Trk:
    ENG = ("pe", "act", "dve", "pool", "sp")

    def __init__(self, nc):
        self.nc = nc
        self.ops = {e: [] for e in self.ENG}
        self.cnt = {e: 0 for e in self.ENG}
        self.known = {e: {} for e in self.ENG}
        self.nsem = 0
        self.semcount = {}
        self.free_sems = []
        self.owners = []

    def _need(self, eng, tok, waits):
        key, val, teng = tok
        if teng == eng and eng == "pe":
            return
        if self.known[eng].get(key, 0) >= val:
            return
        if waits.get(key, 0) < val:
            waits[key] = val

    def _flush(self, eng, waits):
        for k, v in waits.items():
            self.known[eng][k] = v
            self.ops[eng].append(("wait", k, v))

    def _deps(self, eng, r, w):
        waits = {}
        for b in r:
            for t in b.writers:
                self._need(eng, t, waits)
            if b.excl:
                for t in b.readers:
                    if t[2] != eng:
                        self._need(eng, t, waits)
        for b in w:
            for t in b.writers:
                self._need(eng, t, waits)
            for t in b.readers:
                self._need(eng, t, waits)
        self._flush(eng, waits)

    def _commit(self, tok, r, w):
        for b in r:
            b.readers.append(tok)
            if len(b.readers) > 64:
                last = {}
                for t in b.readers:
                    if last.get(t[0], (0, 0, 0))[1] < t[1]:
                        last[t[0]] = t
                b.readers = list(last.values())
        for b in w:
            b.writers = [tok]
            b.readers = []

    def op(self, eng, fn, r=(), w=()):
        self._deps(eng, r, w)
        self.cnt[eng] += 1
        tok = (eng, self.cnt[eng], eng)
        self.ops[eng].append(("op", fn))
        self._commit(tok, r, w)

    def dma(self, eng, out, in_, r=(), w=(), owner=None):
        self._deps(eng, r, w)
        owner = owner or (w[0] if w else r[0])
        if owner.sem is None:
            if self.free_sems:
                owner.sem = self.free_sems.pop()
                owner.semcnt = self.semcount.get(owner.sem, 0)
            else:
                owner.sem = ("d", self.nsem)
                self.nsem += 1
            self.owners.append(owner)
        owner.semcnt += 16
        self.semcount[owner.sem] = owner.semcnt
        tok = (owner.sem, owner.semcnt, "dma")
        self.ops[eng].append(("dma", out, in_, owner.sem))
        self._commit(tok, r, w)

    def barrier(self, pool=True):
        for e in self.ENG:
            if e == "pool" and not pool and os.environ.get("POOL_AHEAD"):
                continue
            waits = {}
            for f in self.ENG:
                if self.cnt[f] > 0 and not (e == f == "pe"):
                    self._need(e, (f, self.cnt[f], f), waits)
            for k, v in self.semcount.items():
                self._need(e, (k, v, "dma"), waits)
            self._flush(e, waits)
        keep = []
        for o in self.owners:
            if o.name:
                keep.append(o)
            else:
                self.free_sems.append(o.sem)
                o.sem = None
        self.owners = keep

    def emit(self):
        nc = self.nc
        with ExitStack() as es:
            sems = {}
            for e in self.ENG:
                sems[e] = es.enter_context(nc.semaphore("s_" + e))
            for i in range(self.nsem):
                sems[("d", i)] = es.enter_context(nc.semaphore("sd%d" % i))
            block = es.enter_context(nc.Block())
            engobj = {"pe": "tensor", "act": "scalar", "dve": "vector", "pool": "gpsimd", "sp": "sync"}

            def make(ename):
                def body(eng):
                    for o in self.ops[ename]:
                        if o[0] == "wait":
                            eng.wait_ge(sems[o[1]], o[2])
                        elif o[0] == "op":
                            o[1](eng).then_inc(sems[ename], 1)
                        else:
                            eng.dma_start(out=o[1], in_=o[2]).then_inc(sems[o[3]], 16)
                return body
            for ename in self.ENG:
                if self.ops[ename]:
                    getattr(block, engobj[ename])(make(ename))


def _ptab_layout():
    off = {}
    c = 0
    for name, n in (("g_mix_pre", L * 16), ("g_mix_post", L * 16), ("g_ffn_pre", L * 16), ("g_ffn_post", L * 16),
                    ("sconv_w", L * 12 * 4), ("sconv_b", L * 12), ("dexp", L * 8), ("snw", L * 8),
                    ("fconv_w", L * 88 * 3), ("fconv_b", L * 88), ("dtb", L * 16), ("alog", L * 16)):
        off[name] = c
        c += n
    return off, c


PT_OFF, PT_COLS = _ptab_layout()


class Arena:
    def __init__(self, ap, words):
        self.ap = ap
        self.words = words
        self.off = 0

    def f32(self, *shape):
        n = int(np.prod(shape))
        a = self.ap[:, self.off:self.off + n]
        self.off += n
        assert self.off <= self.words, ("arena overflow", self.off, self.words)
        if len(shape) == 2:
            return a.rearrange("p (a b) -> p a b", a=shape[0])
        if len(shape) == 3:
            return a.rearrange("p (a b c) -> p a b c", a=shape[0], b=shape[1])
        return a

    def bf(self, *shape):
        n = int(np.prod(shape))
        w = (n + 1) // 2
        a = self.ap[:, self.off:self.off + w].bitcast(BF)
        self.off += w
        assert self.off <= self.words, ("arena overflow", self.off, self.words)
        if len(shape) == 2:
            return a.rearrange("p (a b) -> p a b", a=shape[0])
        if len(shape) == 3:
            return a.rearrange("p (a b c) -> p a b c", a=shape[0], b=shape[1])
        return a


ARENA_WORDS = 53000


import os
ATT_STOP = int(os.environ.get('ATT_STOP', '9'))


def build_program(n_layers=L, stop_after=None, taps=False):
    nc = bass.Bass("TRN2", target_bir_lowering=False)
    dt_in = lambda name, shape, dt=F32: nc.dram_tensor(name, shape, dt, kind="ExternalInput").ap()
    xT = dt_in("xT", [D, T])
    LW = n_layers
    w_in = dt_in("w_in", [LW, IN_TILES, P, 16 * 128])
    w_out = dt_in("w_out", [LW, 16, P, 16 * 128])
    w_up = dt_in("w_up", [LW, 88, P, 16 * 128])
    w_dn = dt_in("w_dn", [LW, 16, P, NHT * 128])
    ptab_d = dt_in("ptab", [P, PT_COLS])
    relT = dt_in("relT", [8, P, 1024])
    cfar = dt_in("cfar", [8, T])
    c_ident = dt_in("c_ident", [P, P])
    c_tri = dt_in("c_tri", [P, P])
    c_caus = dt_in("c_caus", [P, P])
    c_oh = dt_in("c_oh", [16, 16 * 128])
    c_lmat = dt_in("c_lmat", [9, 16 * 128])
    yT = nc.dram_tensor("yT", [D, T], F32, kind="ExternalOutput").ap()
    tk = "ExternalOutput" if taps else "Internal"
    XA = nc.dram_tensor("XA", [D, T], F32, kind=tk).ap()
    XB = nc.dram_tensor("XB", [D, T], F32, kind="Internal").ap()
    MIX = nc.dram_tensor("MIX", [D, T], BF, kind=tk).ap()
    G = nc.dram_tensor("G", [HID, T], BF, kind=tk).ap()
    DBG = nc.dram_tensor("DBG", [P, 4096], F32, kind=tk).ap()
    WDB = nc.dram_tensor("WDB", [16, P, NHT * 128], BF, kind="Internal").ap()
    WOB = nc.dram_tensor("WOB", [16, P, 16 * 128], BF, kind="Internal").ap()
    tapd = {}

    es = ExitStack()
    arena_t = es.enter_context(nc.sbuf_tensor("arena", [P, ARENA_WORDS], F32))
    A = Arena(arena_t, ARENA_WORDS)
    psum = [es.enter_context(nc.psum_tensor("ps%d" % i, [P, 512], F32)) for i in range(7)]
    psbf = es.enter_context(nc.psum_tensor("psbf", [P, 1024], BF))
    t = Trk(nc)
    PSB = [Buf("ps%d" % i, excl=True) for i in range(8)]

    hT = A.bf(16, T)
    hTb = [Buf("hT%d" % i) for i in range(NTT)]
    ptab = A.f32(PT_COLS)
    Bpt = Buf("ptab")
    ident = A.bf(P)
    ones_bf = A.bf(P)
    tri = A.f32(P)
    caus = A.f32(P)
    ones_f = A.f32(P)
    Th_p = A.bf(1024)
    R_p = A.bf(T)
    BTh_p = Buf("Th")
    BR_p = Buf("R")
    lmat = A.bf(16, P)
    Bc = Buf("consts")
    NWS = 4
    wslot = [A.bf(16, P) for _ in range(NWS)]
    Bws = [Buf("ws%d" % i) for i in range(NWS)]
    wctr = [0]
    PHASE0 = A.off

    def pcol(name, idx):
        o = PT_OFF[name] + idx
        return ptab[:, o:o + 1]

    t.dma("sp", ptab, ptab_d, w=[Bpt])
    t.dma("pool", ident, c_ident, w=[Bc])
    t.dma("sp", tri, c_tri, w=[Bc], owner=Bpt)
    t.dma("sp", caus, c_caus, w=[Bc], owner=Bpt)
    t.dma("pool", lmat[0:9, :, :], c_lmat.rearrange("r (a b) -> r a b", a=16), w=[Bc])
    t.op("dve", lambda e: e.memset(ones_bf, 1.0), w=[Bc])
    t.op("dve", lambda e: e.memset(ones_f, 1.0), w=[Bc])
    t.barrier()

    def load_w(dram_tile):
        i = wctr[0] % NWS
        wctr[0] += 1
        t.dma("pool", wslot[i], dram_tile.rearrange("p (k n) -> p k n", k=16), w=[Bws[i]])
        return wslot[i], Bws[i]

    def load_w_bf(dram_tile, rbuf):
        i = wctr[0] % NWS
        wctr[0] += 1
        t.dma("pool", wslot[i], dram_tile.rearrange("p (k n) -> p k n", k=16), r=[rbuf], w=[Bws[i]], owner=Bws[i])
        return wslot[i], Bws[i]

    def xview(X):
        return X.rearrange("(f p) t -> p f t", p=P)

    def stage_n1(X, gname, l):
        A.off = PHASE0
        xt = A.f32(16, TT)
        Bxt = [Buf() for _ in range(4)]
        sq = [A.bf(TT) for _ in range(2)]
        Bsq = [Buf(), Buf()]
        rstd = A.f32(TT)
        Brs = Buf()
        for tt in range(NTT):
            ts = slice(tt * TT, (tt + 1) * TT)
            for q in range(4):
                t.dma("sp", xt[:, 4 * q:4 * q + 4, :], xview(X)[:, 4 * q:4 * q + 4, ts], w=[Bxt[q]])
            for ft in range(16):
                b = ft % 2
                t.op("act", lambda e, ft=ft, b=b: e.activation(out=sq[b], in_=xt[:, ft, :], func=AF.Square),
                     r=[Bxt[ft // 4]], w=[Bsq[b]])
                t.op("pe", lambda e, ft=ft, b=b: e.matmul(psum[6][:, :], lhsT=ones_bf, rhs=sq[b], start=(ft == 0), stop=(ft == 15)),
                     r=[Bsq[b], Bc], w=[PSB[6]])
            t.op("act", lambda e: e.activation(out=rstd, in_=psum[6][:, :], func=AF.Sqrt, bias=EPS, scale=1.0 / D),
                 r=[PSB[6]], w=[Brs])
            t.op("dve", lambda e: e.reciprocal(out=rstd, in_=rstd), r=[Brs], w=[Brs])
            for ft in range(16):
                t.op("dve", lambda e, ft=ft, ts=ts: e.scalar_tensor_tensor(
                    out=hT[:, ft, ts], in0=xt[:, ft, :], scalar=pcol(gname, l * 16 + ft), in1=rstd,
                    op0=OP.mult, op1=OP.mult), r=[Bxt[ft // 4], Brs, Bpt], w=[hTb[tt]])
        t.barrier(pool=False)

    def proj_tile(wt, Bw, evac, kts=16):
        for tt in range(NTT):
            b = tt % 2
            ts = slice(tt * TT, (tt + 1) * TT)
            for kt in range(kts):
                t.op("pe", lambda e, kt=kt, b=b, ts=ts: e.matmul(psum[b][:, :], lhsT=wt[:, kt, :], rhs=hT[:, kt, ts],
                                                             start=(kt == 0), stop=(kt == kts - 1)),
                     r=[Bw, hTb[tt]], w=[PSB[b]])
            evac(tt, ts, psum[b], PSB[b])

    def stage_attn(l, h):
        A.off = PHASE0
        qT = A.bf(T); qTf = A.f32(T); kT = A.bf(T); vT = A.bf(T)
        Bq, Bqf, Bk, Bv = Buf(), Buf(), Buf(), Buf()
        Vtok = A.bf(16, P); BV = Buf()
        PTs = [A.bf(TT) for _ in range(2)]; BPT = [Buf(), Buf()]
        R = R_p; BR = BR_p
        Th = Th_p; BTh = BTh_p
        rec = A.f32(TT); Brec = Buf()
        ob = [A.bf(TT) for _ in range(2)]; Bob = [Buf(), Buf()]
        ksum = A.f32(8); kmf = A.f32(8); Bks = Buf()
        gbuf = A.f32(8); mx = A.f32(8); Bg = Buf()
        nm = A.bf(16, 8); Bnm = Buf()
        sel = A.f32(8); Bsel = Buf()

        if not os.environ.get("NO_TH"):
            t.dma("pool", Th, relT[h], w=[BTh])
        if not os.environ.get("NO_R8"):
            t.dma("pool", R[8:9, :], cfar[h:h + 1, :], w=[BR])
        wq, Bwq = load_w(w_in[l, h])
        wk, Bwk = load_w(w_in[l, 8 + h])
        wv, Bwv = load_w(w_in[l, 16 + h])
        sc = float(128 ** -0.5)

        def ev_q(tt, ts, ps, pb):
            t.op("act", lambda e: e.activation(out=qT[:, ts], in_=ps[:, :], func=AF.Copy, scale=sc), r=[pb], w=[Bq])
            if not os.environ.get("NO_QF"):
                t.op("dve", lambda e: e.tensor_scalar(out=qTf[:, ts], in0=ps[:, :], scalar1=sc, scalar2=0.0, op0=OP.mult, op1=OP.add),
                     r=[pb], w=[Bqf])

        def ev_k(tt, ts, ps, pb):
            t.op("act", lambda e: e.activation(out=kT[:, ts], in_=ps[:, :], func=AF.Copy), r=[pb], w=[Bk])
            if not os.environ.get("NO_KS"):
              t.op("dve", lambda e: e.reduce_sum(out=ksum[:, 2 * tt:2 * tt + 2], in_=ps[:, :].rearrange("p (a b) -> p a b", a=2),
                                              axis=AX.X), r=[pb], w=[Bks])

        def ev_v(tt, ts, ps, pb):
            t.op("act", lambda e: e.activation(out=vT[:, ts], in_=ps[:, :], func=AF.Copy), r=[pb], w=[Bv])

        proj_tile(wq, Bwq, ev_q)
        proj_tile(wk, Bwk, ev_k)
        proj_tile(wv, Bwv, ev_v)
        if ATT_STOP <= 1:
            t.barrier(); return
        for rnd in range(2):
            for i8 in range(8):
                i = rnd * 8 + i8
                sl = slice(i8 * P, (i8 + 1) * P)
                t.op("pe", lambda e, i=i, sl=sl: e.transpose(out=psbf[:, sl], in_=vT[:, i * P:(i + 1) * P], identity=ident),
                     r=[Bv, Bc], w=[PSB[7]])
            t.op("act", lambda e, rnd=rnd: e.activation(out=Vtok[:, 8 * rnd:8 * rnd + 8, :],
                                                        in_=psbf[:, :].rearrange("p (a b) -> p a b", a=8), func=AF.Copy),
                 r=[PSB[7]], w=[BV])
        if ATT_STOP <= 2:
            t.barrier(); return
        t.op("dve", lambda e: e.tensor_scalar(out=kmf, in0=ksum, scalar1=1.0 / 256, scalar2=0.0, op0=OP.mult, op1=OP.add), r=[Bks], w=[Bks])
        t.op("dve", lambda e: e.memset(nm, 0.0), w=[Bnm])
        for jq in range(8, 16):
            gs = slice((jq - 8) * 8, (jq - 8) * 8 + 8)
            t.op("pe", lambda e, jq=jq, gs=gs: e.matmul(psum[6][:, gs], lhsT=qTf[:, jq * P:(jq + 1) * P], rhs=kmf, start=True, stop=True),
                 r=[Bqf, Bks], w=[PSB[6]])
        for jq in range(8, 16):
            own = jq // 2
            g0 = (jq - 8) * 8
            t.op("dve", lambda e: e.memset(gbuf, NEG), w=[Bg])
            t.op("dve", lambda e, own=own, g0=g0: e.tensor_copy(out=gbuf[:, 0:own], in_=psum[6][:, g0:g0 + own]), r=[PSB[6]], w=[Bg])
            t.op("dve", lambda e: e.max(out=mx, in_=gbuf), r=[Bg], w=[Bg])
            t.op("dve", lambda e, own=own: e.tensor_scalar(out=sel[:, 0:own], in0=gbuf[:, 0:own], scalar1=mx[:, 2:3], scalar2=1.0,
                                                          op0=OP.is_ge, op1=OP.subtract), r=[Bg], w=[Bsel])
            t.op("dve", lambda e, own=own, jq=jq: e.tensor_scalar(out=nm[:, jq, 0:own], in0=sel[:, 0:own], scalar1=-NEG, scalar2=0.0,
                                                                 op0=OP.mult, op1=OP.add), r=[Bsel], w=[Bnm])
        if ATT_STOP <= 3:
            t.barrier(); return
        for tt in range(NTT):
            for k in range(4):
                jq = tt * 4 + k
                t.op("pe", lambda e, jq=jq, k=k: e.matmul(psum[6][0:8, k * P:(k + 1) * P], lhsT=nm[:, jq, :], rhs=ident,
                                                         start=True, stop=True), r=[Bnm, Bc], w=[PSB[6]])
            t.op("act", lambda e, tt=tt: e.activation(out=R[0:8, tt * TT:(tt + 1) * TT], in_=psum[6][0:8, :], func=AF.Copy),
                 r=[PSB[6]], w=[BR])
        if ATT_STOP <= 4:
            t.barrier(); return
        for j in range(NTT):
            qs = slice(j * TT, (j + 1) * TT)
            ntile = 4 * j + 4

            def scores(i, j=j, qs=qs):
                b = 2 + (i % 2)
                n = i // 2
                d = 512 * j - 128 * i
                near = d <= 128
                lm = lmat[0:9, 2 * n + (0 if near else 1), :]
                t.op("pe", lambda e: e.matmul(psum[b][:, :], lhsT=kT[:, i * P:(i + 1) * P], rhs=qT[:, qs], start=True, stop=False),
                     r=[Bk, Bq], w=[PSB[b]])
                t.op("pe", lambda e: e.matmul(psum[b][:, :], lhsT=lm, rhs=R[0:9, qs], start=False, stop=(not near)),
                     r=[BR, Bc], w=[PSB[b]])
                if near:
                    c0 = d + 384
                    t.op("pe", lambda e: e.matmul(psum[b][:, :], lhsT=ident, rhs=Th[:, c0:c0 + TT], start=False, stop=True),
                         r=[BTh, Bc], w=[PSB[b]])
                t.op("act", lambda e: e.activation(out=PTs[i % 2], in_=psum[b][:, :], func=AF.Exp), r=[PSB[b]], w=[BPT[i % 2]])

            def pv(i):
                t.op("pe", lambda e: e.matmul(psum[4][:, :], lhsT=Vtok[:, i, :], rhs=PTs[i % 2], start=(i == 0), stop=(i == ntile - 1)),
                     r=[BV, BPT[i % 2]], w=[PSB[4]])
                t.op("pe", lambda e: e.matmul(psum[5][:, :], lhsT=ones_bf, rhs=PTs[i % 2], start=(i == 0), stop=(i == ntile - 1)),
                     r=[Bc, BPT[i % 2]], w=[PSB[5]])

            for i in range(ntile + 1):
                if i < ntile:
                    scores(i)
                if i > 0:
                    pv(i - 1)
            t.op("dve", lambda e: e.reciprocal(out=rec, in_=psum[5][:, :]), r=[PSB[5]], w=[Brec])
            t.op("dve", lambda e, j=j: e.tensor_tensor(out=ob[j % 2], in0=psum[4][:, :], in1=rec, op=OP.mult),
                 r=[PSB[4], Brec], w=[Bob[j % 2]])
            t.dma("sp", MIX[h * P:(h + 1) * P, qs], ob[j % 2], r=[Bob[j % 2]], owner=Bob[j % 2])
        t.barrier(pool=False)

    def stage_ssm(l):
        A.off = PHASE0
        t1 = A.f32(256); dtT = A.f32(256); dAT = A.f32(256); acsT = A.f32(256); nacsT = A.f32(256)
        lastB = A.f32(256); cdB = A.f32(256); ds = A.f32(256); ea = A.f32(16)
        Bt1, Bdt, BdA, Bacs, Bnacs, Blast, Bcd, Bds, Bea = [Buf() for _ in range(9)]
        acsF = A.f32(T); BacsF = Buf()
        xTg = A.bf(4, T); BxT = Buf()
        BTg = A.bf(T); CTg = A.bf(T); BBT, BCT = Buf(), Buf()
        zsg = A.bf(4, T); Bzs = Buf()
        HT = A.f32(512); HTb = A.bf(512); BH, BHb = Buf(), Buf()
        xc = [A.bf(512) for _ in range(2)]; Bxc = [Buf(), Buf()]
        xcd = [A.bf(512) for _ in range(2)]; Bxcd = [Buf(), Buf()]
        Btok = [A.bf(P) for _ in range(2)]; BBtok = [Buf(), Buf()]
        CBm = [A.f32(P) for _ in range(2)]; BCBm = [Buf(), Buf()]
        EAc = [A.f32(4, P) for _ in range(2)]; BEA = [Buf(), Buf()]
        Ebuf = [A.f32(8, P) for _ in range(2)]; BE = [Buf(), Buf()]
        sq = A.bf(4, P); Bsq = Buf()
        rstd = A.f32(P); Brs = Buf()
        ob4 = [A.bf(4, TT) for _ in range(2)]; Bob4 = [Buf(), Buf()]
        ohx = A.f32(8, P); Boh = Buf()
        t.dma("sp", ohx[0:16, :, :], c_oh.rearrange("r (a b) -> r a b", a=16)[:, 0:8, :], w=[Boh])
        OVL = A.off
        ubuf = A.f32(T + 4); Bu = Buf()
        acc = A.f32(T); Bacc = Buf()
        A.off = OVL
        Zb = A.f32(8, P); BZ = Buf()
        Wb = A.bf(8, P); BW = Buf()
        t0 = A.f32(4, P); Bt0 = Buf()
        xD = A.f32(4, P); BxD = Buf()
        A.off = max(A.off, OVL + 2 * T + 4)
        U = caus
        PL = os.environ.get("SSM_POOL", "pool")

        wdt, Bwdt = load_w(w_in[l, 44])
        for c in range(16):
            for kt in range(16):
                t.op("pe", lambda e, c=c, kt=kt: e.matmul(psum[6][:, c * 16:(c + 1) * 16], lhsT=hT[:, kt, c * P:(c + 1) * P],
                                                         rhs=wdt[:, kt, 0:16], start=(kt == 0), stop=(kt == 15)),
                     r=[Bwdt, hTb[c // 4]], w=[PSB[6]])
            t.op("dve", lambda e, c=c: e.tensor_tensor(out=t1[:, c * 16:(c + 1) * 16], in0=psum[6][:, c * 16:(c + 1) * 16],
                                                      in1=ptab[:, PT_OFF["dtb"] + l * 16:PT_OFF["dtb"] + l * 16 + 16], op=OP.add),
                 r=[PSB[6], Bpt], w=[Bt1])
        t.op("act", lambda e: e.activation(out=t1, in_=t1, func=AF.Exp), r=[Bt1], w=[Bt1])
        t.op("act", lambda e: e.activation(out=dtT, in_=t1, func=AF.Ln, bias=1.0), r=[Bt1], w=[Bdt])
        t.op("act", lambda e: e.activation(out=ea, in_=ptab[:, PT_OFF["alog"] + l * 16:PT_OFF["alog"] + l * 16 + 16], func=AF.Exp),
             r=[Bpt], w=[Bea])
        for c in range(16):
            t.op("dve", lambda e, c=c: e.scalar_tensor_tensor(out=dAT[:, c * 16:(c + 1) * 16], in0=dtT[:, c * 16:(c + 1) * 16],
                                                             scalar=-1.0, in1=ea, op0=OP.mult, op1=OP.mult),
                 r=[Bdt, Bea], w=[BdA])
        for c in range(16):
            t.op("pe", lambda e, c=c: e.matmul(psum[2][:, c * 16:(c + 1) * 16], lhsT=tri, rhs=dAT[:, c * 16:(c + 1) * 16],
                                              start=True, stop=True), r=[BdA, Bc], w=[PSB[2]])
            t.op("pe", lambda e, c=c: e.matmul(psum[3][:, c * 16:(c + 1) * 16], lhsT=ones_f, rhs=dAT[:, c * 16:(c + 1) * 16],
                                              start=True, stop=True), r=[BdA, Bc], w=[PSB[3]])
        t.op("dve", lambda e: e.tensor_copy(out=acsT, in_=psum[2][:, 0:256]), r=[PSB[2]], w=[Bacs])
        t.op("dve", lambda e: e.tensor_scalar(out=nacsT, in0=psum[2][:, 0:256], scalar1=-1.0, scalar2=0.0, op0=OP.mult, op1=OP.add),
             r=[PSB[2]], w=[Bnacs])
        t.op("dve", lambda e: e.tensor_copy(out=lastB, in_=psum[3][:, 0:256]), r=[PSB[3]], w=[Blast])
        t.op("act", lambda e: e.activation(out=cdB, in_=psum[3][:, 0:256], func=AF.Exp), r=[PSB[3]], w=[Bcd])
        t.op("dve", lambda e: e.tensor_tensor(out=ds, in0=lastB, in1=acsT, op=OP.subtract), r=[Blast, Bacs], w=[Bds])
        t.op("act", lambda e: e.activation(out=ds, in_=ds, func=AF.Exp), r=[Bds], w=[Bds])
        for tt in range(NTT):
            for k in range(4):
                c = tt * 4 + k
                t.op("pe", lambda e, c=c, k=k: e.matmul(psum[4][0:16, k * P:(k + 1) * P], lhsT=dAT[:, c * 16:(c + 1) * 16], rhs=tri,
                                                       start=True, stop=True), r=[BdA, Bc], w=[PSB[4]])
            t.op("dve", lambda e, tt=tt: e.tensor_copy(out=acsF[0:16, tt * TT:(tt + 1) * TT], in_=psum[4][0:16, :]),
                 r=[PSB[4]], w=[BacsF])

        GORD = (1, 0) if os.environ.get('SSM_SWAP') else (0, 1)
        for g in GORD:
            t.op("dve", lambda e: e.memset(ubuf[:, 0:3], 0.0), w=[Bu])

            def conv_tile(ctile, out_ap, Bout):
                wt, Bw = load_w(w_in[l, 32 + ctile])

                def ev(tt, ts, ps, pb):
                    t.op("act", lambda e: e.activation(out=ubuf[:, 3 + tt * TT:3 + (tt + 1) * TT], in_=ps[:, :], func=AF.Copy),
                         r=[pb], w=[Bu])
                proj_tile(wt, Bw, ev)
                cw = PT_OFF["sconv_w"] + (l * 12 + ctile) * 4
                t.op("dve", lambda e: e.tensor_scalar(out=acc, in0=ubuf[:, 3:3 + T], scalar1=ptab[:, cw + 3:cw + 4], scalar2=0.0,
                                                      op0=OP.mult, op1=OP.add), r=[Bu, Bpt], w=[Bacc])
                for k in range(3):
                    t.op("dve", lambda e, k=k: e.scalar_tensor_tensor(out=acc, in0=ubuf[:, k:k + T], scalar=ptab[:, cw + k:cw + k + 1],
                                                                     in1=acc, op0=OP.mult, op1=OP.add), r=[Bu, Bpt, Bacc], w=[Bacc])
                t.op("act", lambda e: e.activation(out=out_ap, in_=acc, func=AF.Silu, bias=pcol("sconv_b", l * 12 + ctile)),
                     r=[Bacc, Bpt], w=[Bout])

            for ct in range(4):
                conv_tile(4 * g + ct, xTg[:, ct, :], BxT)
            conv_tile(8 + g, BTg, BBT)
            conv_tile(10 + g, CTg, BCT)
            for ct in range(4):
                wt, Bw = load_w(w_in[l, 24 + 4 * g + ct])

                def evz(tt, ts, ps, pb, ct=ct):
                    t.op("act", lambda e: e.activation(out=zsg[:, ct, ts], in_=ps[:, :], func=AF.Silu), r=[pb], w=[Bzs])
                proj_tile(wt, Bw, evz)
            t.barrier()
            t.op("dve", lambda e: e.memset(HT, 0.0), w=[BH])
            t.op("dve", lambda e: e.memset(HTb, 0.0), w=[BHb])

            def front(c):
                cs = slice(c * P, (c + 1) * P)
                cb = c % 2
                hs = slice(c * 16 + 8 * g, c * 16 + 8 * g + 8)
                for ct in range(4):
                    t.op("pe", lambda e, ct=ct: e.transpose(out=psbf[:, ct * P:(ct + 1) * P], in_=xTg[:, ct, cs], identity=ident),
                         r=[BxT, Bc], w=[PSB[7]])
                t.op("pe", lambda e: e.transpose(out=psbf[:, 512:640], in_=BTg[:, cs], identity=ident), r=[BBT, Bc], w=[PSB[7]])
                t.op("dve", lambda e: e.tensor_tensor(
                    out=xc[cb].rearrange("p (h d) -> p h d", h=8), in0=psbf[:, 0:512].rearrange("p (h d) -> p h d", h=8),
                    in1=dtT[:, hs].unsqueeze(2).to_broadcast([P, 8, 64]), op=OP.mult), r=[PSB[7], Bdt], w=[Bxc[cb]])
                t.op("act", lambda e: e.activation(out=Btok[cb], in_=psbf[:, 512:640], func=AF.Copy), r=[PSB[7]], w=[BBtok[cb]])
                t.op(PL, lambda e: e.tensor_tensor(
                    out=xcd[cb].rearrange("p (h d) -> p h d", h=8), in0=xc[cb].rearrange("p (h d) -> p h d", h=8),
                    in1=ds[:, hs].unsqueeze(2).to_broadcast([P, 8, 64]), op=OP.mult), r=[Bxc[cb], Bds], w=[Bxcd[cb]])
                t.op("pe", lambda e: e.matmul(psum[6][:, 0:P], lhsT=BTg[:, cs], rhs=CTg[:, cs], start=True, stop=True),
                     r=[BBT, BCT], w=[PSB[6]])
                t.op("dve", lambda e: e.tensor_tensor(out=CBm[cb], in0=psum[6][:, 0:P], in1=tri, op=OP.mult),
                     r=[PSB[6], Bc], w=[BCBm[cb]])
                t.op("dve", lambda e: e.tensor_tensor(out=Zb, in0=tri.unsqueeze(1).to_broadcast([P, 8, P]),
                                                      in1=dAT[:, hs].unsqueeze(2).to_broadcast([P, 8, P]), op=OP.mult),
                     r=[Bc, BdA], w=[BZ])
                for hb in range(2):
                    t.op("pe", lambda e, hb=hb: e.matmul(psum[hb][:, :], lhsT=U, rhs=Zb[:, 4 * hb:4 * hb + 4, :], start=True, stop=True),
                         r=[BZ, Bc], w=[PSB[hb]])
                    t.op("act", lambda e, hb=hb: e.activation(out=Ebuf[cb][:, 4 * hb:4 * hb + 4, :], in_=psum[hb][:, :].rearrange("p (a b) -> p a b", a=4), func=AF.Exp),
                         r=[PSB[hb]], w=[BE[cb]])
                for ct in range(4):
                    t.op("pe", lambda e, ct=ct, g=g: e.matmul(psum[5][:, ct * P:(ct + 1) * P], lhsT=ohx[0:16, 4 * g + ct, :], rhs=acsF[0:16, cs],
                                                       start=True, stop=True), r=[BacsF, Boh], w=[PSB[5]])
                t.op("act", lambda e: e.activation(out=EAc[cb], in_=psum[5][:, :].rearrange("p (a b) -> p a b", a=4), func=AF.Exp), r=[PSB[5]], w=[BEA[cb]])

            def back(c):
                cs = slice(c * P, (c + 1) * P)
                cb = c % 2
                hs = slice(c * 16 + 8 * g, c * 16 + 8 * g + 8)
                t.op(PL, lambda e: e.tensor_tensor(out=Wb, in0=Ebuf[cb], in1=CBm[cb].unsqueeze(1).to_broadcast([P, 8, P]), op=OP.mult),
                     r=[BE[cb], BCBm[cb]], w=[BW])
                for ct in range(4):
                    t.op("pe", lambda e, ct=ct: e.matmul(psum[4][:, ct * P:(ct + 1) * P], lhsT=HTb[:, ct * P:(ct + 1) * P], rhs=CTg[:, cs],
                                                       start=True, stop=True), r=[BHb, BCT], w=[PSB[4]])
                for k in range(8):
                    ct = k // 2
                    t.op("pe", lambda e, k=k, ct=ct: e.matmul(psum[2 + k % 2][:, ct * P:(ct + 1) * P], lhsT=xc[cb][:, ct * P:(ct + 1) * P],
                                                             rhs=Wb[:, k, :], start=True, stop=True), r=[Bxc[cb], BW], w=[PSB[2 + k % 2]])
                dbg = taps and c == 1 and g == GORD[0]
                if dbg:
                    t.dma("sp", DBG[:, 0:512], EAc[cb].rearrange("p a b -> p (a b)"), r=[BEA[cb]], owner=BEA[cb])
                    t.dma("sp", DBG[:, 512:1024], HT, r=[BH], owner=BH)
                t.op("dve", lambda e: e.tensor_tensor(out=t0, in0=psum[4][:, :].rearrange("p (a b) -> p a b", a=4), in1=EAc[cb], op=OP.mult),
                     r=[PSB[4], BEA[cb]], w=[Bt0])
                if dbg:
                    t.dma("sp", DBG[:, 1024:1536], t0.rearrange("p a b -> p (a b)"), r=[Bt0], owner=Bt0)
                for half in range(2):
                    pr = slice(half * 64, half * 64 + 64)
                    t.op("dve", lambda e, half=half, pr=pr: e.tensor_tensor(
                        out=t0[pr], in0=t0[pr], in1=psum[2 + half][pr, :].rearrange("p (a b) -> p a b", a=4), op=OP.add),
                        r=[Bt0, PSB[2 + half]], w=[Bt0])
                dc0 = PT_OFF["dexp"] + l * 8 + 4 * g
                t.op(PL, lambda e: e.tensor_tensor(out=xD, in0=xTg[:, :, cs], in1=ptab[:, dc0:dc0 + 4].unsqueeze(2).to_broadcast([P, 4, P]),
                                                       op=OP.mult), r=[BxT, Bpt], w=[BxD])
                t.op("dve", lambda e: e.tensor_tensor(out=t0, in0=t0, in1=xD, op=OP.add), r=[Bt0, BxD], w=[Bt0])
                t.op("dve", lambda e: e.tensor_tensor(out=t0, in0=t0, in1=zsg[:, :, cs], op=OP.mult), r=[Bt0, Bzs], w=[Bt0])
                t.op("act", lambda e: e.activation(out=sq, in_=t0, func=AF.Square), r=[Bt0], w=[Bsq])
                for ct in range(4):
                    t.op("pe", lambda e, ct=ct: e.matmul(psum[6][:, P:2 * P], lhsT=ones_bf, rhs=sq[:, ct, :], start=(ct == 0), stop=(ct == 3)),
                         r=[Bsq, Bc], w=[PSB[6]])
                t.op("act", lambda e: e.activation(out=rstd, in_=psum[6][:, P:2 * P], func=AF.Sqrt, bias=EPS, scale=1.0 / 512),
                     r=[PSB[6]], w=[Brs])
                t.op("dve", lambda e: e.reciprocal(out=rstd, in_=rstd), r=[Brs], w=[Brs])
                o4 = (c // 4) % 2
                for ct in range(4):
                    ncol = PT_OFF["snw"] + l * 8 + 4 * g + ct
                    t.op("dve", lambda e, ct=ct, ncol=ncol: e.scalar_tensor_tensor(
                        out=ob4[o4][:, ct, (c % 4) * P:(c % 4 + 1) * P], in0=t0[:, ct, :], scalar=ptab[:, ncol:ncol + 1], in1=rstd,
                        op0=OP.mult, op1=OP.mult), r=[Bt0, Bpt, Brs], w=[Bob4[o4]])
                if c % 4 == 3:
                    f0 = 1024 + 512 * g
                    t.dma("sp", MIX[f0:f0 + 512, (c // 4) * TT:(c // 4 + 1) * TT].rearrange("(a p) t -> p a t", p=P), ob4[o4],
                          r=[Bob4[o4]], owner=Bob4[o4])
                t.op("pe", lambda e: e.matmul(psum[4][:, :], lhsT=Btok[cb], rhs=xcd[cb], start=True, stop=True),
                     r=[BBtok[cb], Bxcd[cb]], w=[PSB[4]])
                t.op("dve", lambda e: e.tensor_tensor(out=HT.rearrange("p (h d) -> p h d", h=8), in0=HT.rearrange("p (h d) -> p h d", h=8),
                                                      in1=cdB[:, hs].unsqueeze(2).to_broadcast([P, 8, 64]), op=OP.mult),
                     r=[BH, Bcd], w=[BH])
                t.op("dve", lambda e: e.tensor_tensor(out=HT, in0=HT, in1=psum[4][:, :], op=OP.add), r=[BH, PSB[4]], w=[BH])
                t.op("act", lambda e: e.activation(out=HTb, in_=HT, func=AF.Copy), r=[BH], w=[BHb])

            if not os.environ.get("SSM_NOPIPE"):
                front(0)
                for c in range(16):
                    if c + 1 < 16:
                        front(c + 1)
                    back(c)
            else:
                for c in range(16):
                    front(c)
                    back(c)
            t.barrier()
        t.barrier()

    def epilogue_gen(m, Bm, psstat, Bpsstat, tq, Xin, Xout, gpost, l, gnext, lnext, bufs):
        xt, Bxt, sq, Bsq, rstd, Brs = bufs
        ts = slice(tq * TT, (tq + 1) * TT)
        t.op("act", lambda e: e.activation(out=rstd, in_=psstat, func=AF.Sqrt, bias=EPS, scale=1.0 / D), r=[Bpsstat], w=[Brs])
        t.op("dve", lambda e: e.reciprocal(out=rstd, in_=rstd), r=[Brs], w=[Brs])
        yield
        xts, Bxts = (xt, Bxt) if isinstance(xt, list) else ([xt], [Bxt])
        gsz = xts[0].shape[1]
        for q in range(16 // gsz):
            xt, Bxt = xts[q % len(xts)], Bxts[q % len(xts)]
            t.dma("sp", xt, xview(Xin)[:, gsz * q:gsz * q + gsz, ts], w=[Bxt])
            for f in range(gsz):
                ft = gsz * q + f
                t.op("dve", lambda e, ft=ft: e.scalar_tensor_tensor(out=m[:, ft, :], in0=m[:, ft, :], scalar=pcol(gpost, l * 16 + ft),
                                                                   in1=rstd, op0=OP.mult, op1=OP.mult), r=[Bm, Bpt, Brs], w=[Bm])
                t.op("dve", lambda e, ft=ft, f=f, xt=xt: e.tensor_tensor(out=m[:, ft, :], in0=m[:, ft, :], in1=xt[:, f, :], op=OP.add),
                     r=[Bm, Bxt], w=[Bm])
                yield
        for q in range(4):
            t.dma("sp", xview(Xout)[:, 4 * q:4 * q + 4, ts], m[:, 4 * q:4 * q + 4, :], r=[Bm], owner=Bm)
        yield
        if gnext is not None:
            for ft in range(16):
                b = ft % 2
                t.op("act", lambda e, ft=ft, b=b: e.activation(out=sq[b], in_=m[:, ft, :], func=AF.Square), r=[Bm], w=[Bsq[b]])
                t.op("pe", lambda e, ft=ft, b=b: e.matmul(psum[6][:, :], lhsT=ones_bf, rhs=sq[b], start=(ft == 0), stop=(ft == 15)),
                     r=[Bsq[b], Bc], w=[PSB[6]])
                yield
            t.op("act", lambda e: e.activation(out=rstd, in_=psum[6][:, :], func=AF.Sqrt, bias=EPS, scale=1.0 / D), r=[PSB[6]], w=[Brs])
            t.op("dve", lambda e: e.reciprocal(out=rstd, in_=rstd), r=[Brs], w=[Brs])
            yield
            for ft in range(16):
                t.op("dve", lambda e, ft=ft: e.scalar_tensor_tensor(out=hT[:, ft, ts], in0=m[:, ft, :], scalar=pcol(gnext, lnext * 16 + ft),
                                                                   in1=rstd, op0=OP.mult, op1=OP.mult), r=[Bm, Bpt, Brs], w=[hTb[tq]])
                yield

    def epilogue(*a):
        for _ in epilogue_gen(*a):
            pass

    def stage_o(l, Xin, Xout):
        A.off = PHASE0
        mixt = A.bf(16, TT); Bmx = [Buf() for _ in range(4)]
        m = [A.f32(16, TT) for _ in range(2)]; Bm = [Buf(), Buf()]
        xt = [A.f32(4, TT) for _ in range(2)]; Bxt = [Buf(), Buf()]
        sq = [A.bf(TT) for _ in range(2)]; Bsq = [Buf(), Buf()]
        rstd = A.f32(TT); Brs = Buf()
        sq2 = [A.bf(TT) for _ in range(2)]; Bsq2 = [Buf(), Buf()]
        pend = [None]
        BWO = [Buf() for _ in range(16)]

        def load_mix(tq):
            ts = slice(tq * TT, (tq + 1) * TT)
            for q in range(4):
                t.dma("sp", mixt[:, 4 * q:4 * q + 4, :], MIX.rearrange("(f p) t -> p f t", p=P)[:, 4 * q:4 * q + 4, ts], w=[Bmx[q]])
        load_mix(0)
        for tq in range(NTT):
            pb_ = tq % 2
            for ft in range(16):
                if tq == 0:
                    wt, Bw = load_w(w_out[l, ft])
                    t.dma("sp", WOB[ft].rearrange("p (k n) -> p k n", k=16), wt, r=[Bw], w=[BWO[ft]])
                else:
                    wt, Bw = load_w_bf(WOB[ft], BWO[ft])
                b = ft % 2
                for kt in range(16):
                    t.op("pe", lambda e, kt=kt, b=b, wt=wt: e.matmul(psum[b][:, :], lhsT=wt[:, kt, :], rhs=mixt[:, kt, :],
                                                                    start=(kt == 0), stop=(kt == 15)), r=[Bw, Bmx[kt // 4]], w=[PSB[b]])
                t.op("act", lambda e, ft=ft, b=b, pb_=pb_: e.activation(out=m[pb_][:, ft, :], in_=psum[b][:, :], func=AF.Copy), r=[PSB[b]], w=[Bm[pb_]])
                t.op("act", lambda e, ft=ft, b=b: e.activation(out=sq[b], in_=psum[b][:, :], func=AF.Square), r=[PSB[b]], w=[Bsq[b]])
                t.op("pe", lambda e, ft=ft, b=b, pb_=pb_: e.matmul(psum[4 + pb_][:, :], lhsT=ones_bf, rhs=sq[b], start=(ft == 0), stop=(ft == 15)),
                     r=[Bsq[b], Bc], w=[PSB[4 + pb_]])
                if pend[0] is not None:
                    for _ in range(4):
                        next(pend[0], None)
            if pend[0] is not None:
                for _ in pend[0]:
                    pass
            if tq + 1 < NTT:
                load_mix(tq + 1)
            pend[0] = epilogue_gen(m[pb_], Bm[pb_], psum[4 + pb_][:, :], PSB[4 + pb_], tq, Xin, Xout, "g_mix_post", l, "g_ffn_pre", l,
                                   (xt, Bxt, sq2, Bsq2, rstd, Brs))
        for _ in pend[0]:
            pass
        t.barrier()

    def stage_f1(l):
        A.off = PHASE0
        ug = A.f32(T + 2); uu = A.f32(T + 2); Bug, Buu = Buf(), Buf()
        ag = A.f32(T); au = A.f32(T); Bag, Bau = Buf(), Buf()
        gb = [A.bf(T) for _ in range(2)]; Bgb = [Buf(), Buf()]
        t.op("dve", lambda e: e.memset(ug[:, 0:2], 0.0), w=[Bug])
        t.op("dve", lambda e: e.memset(uu[:, 0:2], 0.0), w=[Buu])
        for j in range(NHT):
            for (tile, u, Bu_, a, Ba) in ((j, ug, Bug, ag, Bag), (NHT + j, uu, Buu, au, Bau)):
                wt, Bw = load_w(w_up[l, tile])

                def ev(tt, ts, ps, pb, u=u, Bu_=Bu_):
                    t.op("act", lambda e: e.activation(out=u[:, 2 + tt * TT:2 + (tt + 1) * TT], in_=ps[:, :], func=AF.Copy), r=[pb], w=[Bu_])
                proj_tile(wt, Bw, ev)
                cw = PT_OFF["fconv_w"] + (l * 88 + tile) * 3
                t.op("dve", lambda e, u=u, a=a, cw=cw, tile=tile: e.tensor_scalar(
                    out=a, in0=u[:, 2:2 + T], scalar1=ptab[:, cw + 2:cw + 3], scalar2=pcol("fconv_b", l * 88 + tile),
                    op0=OP.mult, op1=OP.add), r=[Bu_, Bpt], w=[Ba])
                for k in range(2):
                    t.op("dve", lambda e, u=u, a=a, cw=cw, k=k: e.scalar_tensor_tensor(
                        out=a, in0=u[:, k:k + T], scalar=ptab[:, cw + k:cw + k + 1], in1=a, op0=OP.mult, op1=OP.add),
                        r=[Bu_, Bpt, Ba], w=[Ba])
            t.op("act", lambda e: e.activation(out=ag, in_=ag, func=AF.Gelu_apprx_tanh), r=[Bag], w=[Bag])
            t.op("dve", lambda e, j=j: e.tensor_tensor(out=gb[j % 2], in0=ag, in1=au, op=OP.mult), r=[Bag, Bau], w=[Bgb[j % 2]])
            t.dma("sp", G[j * P:(j + 1) * P, :], gb[j % 2], r=[Bgb[j % 2]], owner=Bgb[j % 2])
        t.barrier()

    def stage_f2(l, Xin, Xout, gnext, lnext):
        A.off = PHASE0 - NWS * 1024
        gt = A.bf(NHT, TT); Bgt = [Buf() for _ in range(4)]
        m = A.f32(16, TT); Bm = Buf()
        xt = [A.f32(4, TT) for _ in range(2)]; Bxt = [Buf(), Buf()]
        sq = [A.bf(TT) for _ in range(2)]; Bsq = [Buf(), Buf()]
        rstd = A.f32(TT); Brs = Buf()
        sq2 = [A.bf(TT) for _ in range(2)]; Bsq2 = [Buf(), Buf()]
        wd = [A.bf(NHT, P) for _ in range(2)]; Bwd = [Buf(), Buf()]
        BWDB = [Buf() for _ in range(16)]
        def load_g(tq):
            ts = slice(tq * TT, (tq + 1) * TT)
            for q in range(4):
                t.dma("sp", gt[:, 11 * q:11 * q + 11, :], G.rearrange("(f p) t -> p f t", p=P)[:, 11 * q:11 * q + 11, ts], w=[Bgt[q]])
        load_g(0)
        for tq in range(NTT):
            for ft in range(16):
                b = ft % 2
                if tq == 0:
                    t.dma("pool", wd[b], w_dn[l, ft].rearrange("p (k n) -> p k n", k=NHT), w=[Bwd[b]])
                    t.dma("sp", WDB[ft].rearrange("p (k n) -> p k n", k=NHT), wd[b], r=[Bwd[b]], w=[BWDB[ft]])
                else:
                    t.dma("pool", wd[b], WDB[ft].rearrange("p (k n) -> p k n", k=NHT), r=[BWDB[ft]], w=[Bwd[b]], owner=Bwd[b])
                for kt in range(NHT):
                    t.op("pe", lambda e, kt=kt, b=b: e.matmul(psum[b][:, :], lhsT=wd[b][:, kt, :], rhs=gt[:, kt, :],
                                                             start=(kt == 0), stop=(kt == NHT - 1)), r=[Bwd[b], Bgt[kt // 11]], w=[PSB[b]])
                t.op("act", lambda e, ft=ft, b=b: e.activation(out=m[:, ft, :], in_=psum[b][:, :], func=AF.Copy), r=[PSB[b]], w=[Bm])
                t.op("act", lambda e, ft=ft, b=b: e.activation(out=sq2[b], in_=psum[b][:, :], func=AF.Square), r=[PSB[b]], w=[Bsq2[b]])
                t.op("pe", lambda e, ft=ft, b=b: e.matmul(psum[5][:, :], lhsT=ones_bf, rhs=sq2[b], start=(ft == 0), stop=(ft == 15)),
                     r=[Bsq2[b], Bc], w=[PSB[5]])
            if tq + 1 < NTT:
                load_g(tq + 1)
            epilogue(m, Bm, psum[5][:, :], PSB[5], tq, Xin, Xout, "g_ffn_post", l, gnext, lnext, (xt, Bxt, sq, Bsq, rstd, Brs))
        t.barrier()

    Xcur = xT
    stage_n1(xT, "g_mix_pre", 0)
    done = False
    for l in range(n_layers):
        if stop_after == "n1":
            break
        for h in range(int(os.environ.get('NHEADS', '8'))):
            stage_attn(l, h)
        if stop_after == "attn":
            break
        stage_ssm(l)
        if stop_after == "ssm":
            break
        stage_o(l, Xcur, XA)
        if stop_after == "o":
            break
        stage_f1(l)
        if stop_after == "f1":
            break
        last = (l == n_layers - 1)
        Xn = yT if last else XB
        stage_f2(l, XA, Xn, None if last else "g_mix_pre", l + 1)
        Xcur = Xn
    t.barrier()
    t.emit()
    es.close()
    return nc, {"MIX": MIX, "XA": XA, "G": G}


def _t5_bucket_np(rel):
    n = np.maximum(rel, 0)
    max_exact = 16
    nf = np.maximum(n, 1).astype(np.float32)
    large = max_exact + (np.log(nf / max_exact) / np.log(128 / max_exact) * (32 - max_exact)).astype(np.int32)
    large = np.minimum(large, 31)
    return np.where(n < max_exact, n, large)


def _tile_w(w, kt, ntiles):
    K, N = w.shape
    if N < ntiles * 128:
        wp = np.zeros((K, ntiles * 128), np.float32)
        wp[:, :N] = w
        w = wp
    return np.ascontiguousarray(w.reshape(kt, 128, ntiles, 128).transpose(2, 1, 0, 3)).reshape(ntiles, 128, kt * 128)


def prep_shared(inp):
    f = np.float32
    sh = {}
    sh["w_in"] = np.stack([_tile_w(np.asarray(inp["w_in"][l], f), 16, IN_TILES) for l in range(L)])
    sh["w_out"] = np.stack([_tile_w(np.asarray(inp["w_out"][l], f), 16, 16) for l in range(L)])
    sh["w_up"] = np.stack([_tile_w(np.asarray(inp["w_ffn_up"][l], f), 16, 88) for l in range(L)])
    sh["w_dn"] = np.stack([_tile_w(np.asarray(inp["w_ffn_down"][l], f), NHT, 16) for l in range(L)])
    pt = np.zeros((P, PT_COLS), f)

    def put(name, arr):
        n = arr.shape[0]
        pt[:, PT_OFF[name]:PT_OFF[name] + n] = arr.T

    for name, key in (("g_mix_pre", "ln_mix_pre"), ("g_mix_post", "ln_mix_post"), ("g_ffn_pre", "ln_ffn_pre"), ("g_ffn_post", "ln_ffn_post")):
        put(name, np.asarray(inp[key], f).reshape(L * 16, P))
    scw = np.asarray(inp["ssm_conv_w"], f)
    put("sconv_w", scw.reshape(L, 4, 12, P).transpose(0, 2, 1, 3).reshape(L * 12 * 4, P))
    put("sconv_b", np.asarray(inp["ssm_conv_b"], f).reshape(L * 12, P))
    dsk = np.asarray(inp["d_skip"], f)
    put("dexp", np.repeat(dsk, 64, axis=1).reshape(L * 8, P))
    put("snw", np.asarray(inp["ssm_norm_w"], f).reshape(L * 8, P))
    fcw = np.asarray(inp["ffn_conv_w"], f)
    put("fconv_w", fcw.reshape(L, 3, 88, P).transpose(0, 2, 1, 3).reshape(L * 88 * 3, P))
    put("fconv_b", np.asarray(inp["ffn_conv_b"], f).reshape(L * 88, P))
    pt[:, PT_OFF["dtb"]:PT_OFF["dtb"] + L * 16] = np.asarray(inp["dt_bias"], f).reshape(1, L * 16)
    pt[:, PT_OFF["alog"]:PT_OFF["alog"] + L * 16] = np.asarray(inp["a_log"], f).reshape(1, L * 16)
    sh["ptab"] = pt
    rb = np.asarray(inp["rel_bias"], f)
    kp = np.arange(P)[:, None]
    cc = np.arange(1024)[None, :]
    rel = cc - 384 - kp
    idx = _t5_bucket_np(rel)
    tab = rb[idx, :]
    tab = np.where((rel >= 0)[:, :, None], tab, f(NEG))
    sh["relT"] = np.ascontiguousarray(tab.transpose(2, 0, 1)).astype(f)
    sh["cfar"] = np.ascontiguousarray(np.repeat(rb[31, :][:, None], T, axis=1)).astype(f)
    sh["c_ident"] = np.eye(P, dtype=f)
    ti = np.arange(P)
    sh["c_tri"] = (ti[:, None] <= ti[None, :]).astype(f)
    sh["c_caus"] = (ti[:, None] > ti[None, :]).astype(f)
    ohm = np.zeros((16, 16, P), f)
    for j in range(8):
        for m_ in range(P):
            ohm[2 * j + m_ // 64, j, m_] = 1.0
    sh["c_oh"] = ohm.reshape(16, 16 * P)
    lm = np.zeros((9, 16, P), f)
    for n in range(8):
        for far in range(2):
            lm[n, 2 * n + far, :] = 1.0
            lm[8, 2 * n + far, :] = float(far)
    sh["c_lmat"] = lm.reshape(9, 16 * P)
    return sh


_CACHE = {}


def kernel(**inputs):
    x = np.asarray(inputs["x"], np.float32)
    sh = prep_shared(inputs)
    if "nc" not in _CACHE:
        _CACHE["nc"] = build_program()[0]
    nc = _CACHE["nc"]
    in_maps = []
    for c in range(8):
        d = dict(sh)
        d["xT"] = np.ascontiguousarray(x[c % 4].T)
        in_maps.append(d)
    res = run_bass_kernel_spmd(nc, in_maps, core_ids=list(range(8)))
    out = np.stack([np.ascontiguousarray(res.results[b]["yT"].T) for b in range(4)], axis=0)
    return out.astype(np.float32)
```
